# Optimizing a Trainium2 kernel written in Bass

```python
import math
import jax, jax.numpy as jnp
from jax import lax
import numpy as np

D_MODEL = 1024
BATCH = 4
SEQ = 8192
DEPTH = 1

GMLP_WIDTH = D_MODEL
CHUNK = 128
GMLP_GROUPS = 8
GMLP_GROUP_CH = GMLP_WIDTH // GMLP_GROUPS
DIFF_HEADS = 8
DIFF_VDIM = D_MODEL // DIFF_HEADS
DIFF_QK_DIM = DIFF_VDIM // 2
ATTN_WIDTH = DIFF_HEADS * DIFF_VDIM
Q_BLOCK = 128
ROPE_THETA = 10000.0
D_FF = 4 * D_MODEL
IN_COLS = 2 * GMLP_WIDTH + 3 * ATTN_WIDTH + 2 * D_MODEL
N_MOD = 6
EPS = 1e-6

kernel_name = "hybrid_gmlp_diffattn_block"


def rmsnorm(x, g):
    xf = x.astype(jnp.float32)
    y = xf * lax.rsqrt(jnp.mean(xf * xf, axis=-1, keepdims=True) + EPS)
    return (y * g.astype(jnp.float32)).astype(x.dtype)


def layernorm(x, g, b):
    xf = x.astype(jnp.float32)
    mu = jnp.mean(xf, axis=-1, keepdims=True)
    xc = xf - mu
    y = xc * lax.rsqrt(jnp.mean(xc * xc, axis=-1, keepdims=True) + EPS)
    return (y * g.astype(jnp.float32) + b.astype(jnp.float32)).astype(x.dtype)


def rope_tables(positions):
    inv_freq = ROPE_THETA ** (-jnp.arange(0, DIFF_QK_DIM, 2, dtype=jnp.float32) / DIFF_QK_DIM)
    ang = positions.astype(jnp.float32)[..., None] * inv_freq
    return jnp.cos(ang)[:, :, None, None, :], jnp.sin(ang)[:, :, None, None, :]


def apply_rope(t, cos, sin):
    half = DIFF_QK_DIM // 2
    tf = t.astype(jnp.float32)
    t1, t2 = tf[..., :half], tf[..., half:]
    out = jnp.concatenate([t1 * cos - t2 * sin, t2 * cos + t1 * sin], axis=-1)
    return out.astype(t.dtype)


def gmlp_mixer(u, v, ln_g, ln_b, w_s, b_s):
    B, S, _ = u.shape
    u = jax.nn.gelu(u)
    v = layernorm(jax.nn.gelu(v), ln_g, ln_b)
    vc = v.reshape(B, S // CHUNK, CHUNK, GMLP_GROUPS, GMLP_GROUP_CH)
    sv = jnp.einsum('gpq,bnqgc->bnpgc', w_s, vc) + b_s.T[None, None, :, :, None]
    return u * sv.reshape(B, S, GMLP_WIDTH)


def diff_attention(q, k, v, cos, sin, lam, subln_g, lambda_init):
    B, S = q.shape[0], q.shape[1]
    scale = DIFF_QK_DIM ** -0.5
    q = apply_rope(q, cos, sin) * scale
    k = apply_rope(k, cos, sin)
    nb = S // Q_BLOCK
    qb = q.reshape(B, nb, Q_BLOCK, DIFF_HEADS, 2, DIFF_QK_DIM).transpose(1, 0, 2, 3, 4, 5)

    def attend(q_blk):
        s = jnp.einsum('bqhmd,bkhmd->bhmqk', q_blk, k).astype(jnp.float32)
        p = jax.nn.softmax(s, axis=-1)
        a = p[:, :, 0] - lam * p[:, :, 1]
        return jnp.einsum('bhqk,bkhe->bqhe', a.astype(v.dtype), v)

    o = lax.map(attend, qb)
    o = o.transpose(1, 0, 2, 3, 4).reshape(B, S, DIFF_HEADS, DIFF_VDIM)
    o = rmsnorm(o, subln_g) * (1.0 - lambda_init)
    return o.reshape(B, S, ATTN_WIDTH)


def setup_inputs(seed: int = 0) -> dict:
    key = jax.random.key(seed)
    ks = jax.random.split(key, 24)
    f32 = jnp.float32
    L, D = DEPTH, D_MODEL
    nrm = lambda k, shape, s: jax.random.normal(k, shape, f32) * s
    x = nrm(ks[0], (BATCH, SEQ, D), 1.0)
    c = nrm(ks[1], (BATCH, D), 1.0)
    offs = jax.random.randint(ks[2], (BATCH, 1), 0, SEQ, dtype=jnp.int32)
    positions = (jnp.arange(SEQ, dtype=jnp.int32)[None, :] + offs).astype(jnp.int32)
    return {
        "x": x,
        "c": c,
        "positions": positions,
        "w_ada": nrm(ks[3], (L, D, N_MOD * D), D ** -0.5),
        "b_ada": nrm(ks[4], (L, N_MOD * D), 0.01),
        "g_norm1": 1.0 + nrm(ks[5], (L, D), 0.02),
        "w_in": nrm(ks[6], (L, D, IN_COLS), D ** -0.5),
        "gmlp_ln_g": 1.0 + nrm(ks[7], (L, GMLP_WIDTH), 0.02),
        "gmlp_ln_b": nrm(ks[8], (L, GMLP_WIDTH), 0.01),
        "w_spatial": nrm(ks[9], (L, GMLP_GROUPS, CHUNK, CHUNK), CHUNK ** -0.5),
        "b_spatial": 1.0 + nrm(ks[10], (L, GMLP_GROUPS, CHUNK), 0.02),
        "lambda_q1": nrm(ks[11], (L, DIFF_QK_DIM), 0.1),
        "lambda_k1": nrm(ks[12], (L, DIFF_QK_DIM), 0.1),
        "lambda_q2": nrm(ks[13], (L, DIFF_QK_DIM), 0.1),
        "lambda_k2": nrm(ks[14], (L, DIFF_QK_DIM), 0.1),
        "subln_g": 1.0 + nrm(ks[15], (L, DIFF_VDIM), 0.02),
        "w_out": nrm(ks[16], (L, D, D), D ** -0.5),
        "g_norm2": 1.0 + nrm(ks[17], (L, D), 0.02),
        "w_ff1": nrm(ks[18], (L, D, D_FF), D ** -0.5),
        "w_ff2": nrm(ks[19], (L, D_FF, D), D_FF ** -0.5),
        "g_final": 1.0 + nrm(ks[20], (D,), 0.02),
    }


def reference(x, c, positions, w_ada, b_ada, g_norm1, w_in, gmlp_ln_g, gmlp_ln_b,
              w_spatial, b_spatial, lambda_q1, lambda_k1, lambda_q2, lambda_k2,
              subln_g, w_out, g_norm2, w_ff1, w_ff2, g_final):
    B, S, D = x.shape
    cos, sin = rope_tables(positions)
    c_act = jax.nn.silu(c)
    col = np.cumsum([0, GMLP_WIDTH, GMLP_WIDTH, ATTN_WIDTH, ATTN_WIDTH, ATTN_WIDTH, D_MODEL, D_MODEL])
    for l in range(DEPTH):
        lambda_init = 0.8 - 0.6 * math.exp(-0.3 * l)
        mod = c_act @ w_ada[l] + b_ada[l]
        sh1, sc1, gt1, sh2, sc2, gt2 = [m[:, None, :] for m in jnp.split(mod, N_MOD, axis=-1)]

        h = rmsnorm(x, g_norm1[l]) * (1.0 + sc1) + sh1
        z = h @ w_in[l]
        u_a = z[..., col[0]:col[1]]
        v_a = z[..., col[1]:col[2]]
        q = z[..., col[2]:col[3]].reshape(B, S, DIFF_HEADS, 2, DIFF_QK_DIM)
        k = z[..., col[3]:col[4]].reshape(B, S, DIFF_HEADS, 2, DIFF_QK_DIM)
        v = z[..., col[4]:col[5]].reshape(B, S, DIFF_HEADS, DIFF_VDIM)
        gate_a = jax.nn.sigmoid(z[..., col[5]:col[6]])
        gate_b = jax.nn.sigmoid(z[..., col[6]:col[7]])

        branch_a = gmlp_mixer(u_a, v_a, gmlp_ln_g[l], gmlp_ln_b[l], w_spatial[l], b_spatial[l])
        lam = (jnp.exp(jnp.sum(lambda_q1[l].astype(jnp.float32) * lambda_k1[l].astype(jnp.float32)))
               - jnp.exp(jnp.sum(lambda_q2[l].astype(jnp.float32) * lambda_k2[l].astype(jnp.float32)))
               + lambda_init)
        branch_b = diff_attention(q, k, v, cos, sin, lam, subln_g[l], lambda_init)

        merged = gate_a * branch_a + gate_b * branch_b
        x = x + gt1 * (merged @ w_out[l])

        h2 = rmsnorm(x, g_norm2[l]) * (1.0 + sc2) + sh2
        ff = jnp.square(jax.nn.relu(h2 @ w_ff1[l])) @ w_ff2[l]
        x = x + gt2 * ff
    return rmsnorm(x, g_final)
```

```python
import numpy as np
from contextlib import ExitStack

import concourse.bass as bass
import concourse.mybir as mybir
from concourse.bass_utils import run_bass_kernel_spmd

F32 = mybir.dt.float32
BF16 = mybir.dt.bfloat16
I32 = mybir.dt.int32
AF = mybir.ActivationFunctionType
ALU = mybir.AluOpType
AX = mybir.AxisListType
PI = float(np.pi)

D = 1024
KC = 8
H = 8
DFF = 4096
NCOL = 7168
EPS = 1e-6
VW = 130
LAMBDA_INIT = 0.2


class Res:
    __slots__ = ("name", "last_write", "reads", "excl")

    def __init__(self, name, excl=False):
        self.name = name
        self.last_write = None
        self.reads = []
        self.excl = excl


class Op:
    __slots__ = ("eng", "fn", "deps", "dma", "semkey", "marked", "ev")

    def __init__(self, eng, fn, dma, semkey):
        self.eng = eng
        self.fn = fn
        self.deps = []
        self.dma = dma
        self.semkey = semkey
        self.marked = False
        self.ev = None


class Prog:
    ENGS = ("pe", "act", "dve", "pool", "sp")

    def __init__(self, nc, semstack):
        self.nc = nc
        self._semstack = semstack
        self.all_res = []
        self.sems = {}
        self.semcount = {}
        self.known = {e: {} for e in self.ENGS}
        self.ops = []
        self.nres = 0
        self.disabled = False

    def res(self, name=None, excl=False):
        self.nres += 1
        r = Res(name or f"r{self.nres}", excl)
        self.all_res.append(r)
        return r

    def _sem(self, key):
        if key not in self.sems:
            self.sems[key] = self._semstack.enter_context(self.nc.semaphore(f"s_{key}"))
            self.semcount[key] = 0
        return self.sems[key]

    def add(self, eng, fn, reads=(), writes=(), dma=False, semkey=None):
        if self.disabled:
            return None
        op = Op(eng, fn, dma, semkey)
        deps = []
        for r in reads:
            if r.last_write is not None:
                deps.append(r.last_write)
            if r.excl:
                deps.extend(o for o in r.reads if o.eng != eng)
        for w in writes:
            if w.last_write is not None:
                deps.append(w.last_write)
            deps.extend(w.reads)
        for r in reads:
            r.reads.append(op)
        for w in writes:
            w.last_write = op
            w.reads = []
        seen = set()
        for d in deps:
            if id(d) in seen or d is op:
                continue
            seen.add(id(d))
            if d.eng == "pe" and eng == "pe" and not d.dma and not dma:
                continue
            op.deps.append(d)
            d.marked = True
        if dma:
            op.marked = True
        self.ops.append(op)
        return op

    def emit(self, block_name):
        nc = self.nc
        last = {}
        for op in self.ops:
            if not op.dma:
                last[op.eng] = op
        for op in last.values():
            op.marked = True
        for op in self.ops:
            if op.marked and op.ev is None:
                if op.dma:
                    key = "d_" + op.semkey
                    self._sem(key)
                    self.semcount[key] += 16
                else:
                    key = "e_" + op.eng
                    self._sem(key)
                    self.semcount[key] += 1
                op.ev = (key, self.semcount[key])
        per = {e: [] for e in self.ENGS}
        for op in self.ops:
            per[op.eng].append(op)
        prog = self

        def run(engname, e):
            known = prog.known[engname]
            for op in per[engname]:
                for d in op.deps:
                    key, val = d.ev
                    if known.get(key, 0) >= val:
                        continue
                    e.wait_ge(prog.sems[key], val)
                    known[key] = val
                ins = op.fn(e)
                if op.marked:
                    key, val = op.ev
                    ins.then_inc(prog.sems[key], 16 if op.dma else 1)
            for key, val in prog.semcount.items():
                if val > 0 and known.get(key, 0) < val:
                    e.wait_ge(prog.sems[key], val)
                    known[key] = val

        with nc.Block(block_name) as block:
            @block.tensor
            def _(e):
                run("pe", e)

            @block.scalar
            def _(e):
                run("act", e)

            @block.vector
            def _(e):
                run("dve", e)

            @block.gpsimd
            def _(e):
                run("pool", e)

            @block.sync
            def _(e):
                run("sp", e)
        self.ops = []
        for r in self.all_res:
            r.last_write = None
            r.reads = []


class Ring:
    def __init__(self, items):
        self.items = items
        self.i = -1

    def next(self):
        self.i = (self.i + 1) % len(self.items)
        return self.items[self.i]

    def cur(self):
        return self.items[self.i]


import os
_MAXPH = int(os.environ.get("KPH", "9"))
_KSUB = int(os.environ.get("KSUB", "0"))


def build(S_OWN, S_SEQ):
    NT_OWN = S_OWN // 128
    NT_SEQ = S_SEQ // 128
    NB_OWN = S_OWN // 512
    NB_SEQ = S_SEQ // 512
    NCH = NT_SEQ
    NPAIR = NCH // 2
    NQB = S_OWN // 256

    nc = bass.Bass("TRN2", target_bir_lowering=False)

    def din(name, shape, dt=F32):
        return nc.dram_tensor(name, list(shape), dt, kind="ExternalInput").ap()

    def dscr(name, shape, dt):
        return nc.dram_tensor(name, list(shape), dt, kind="Internal").ap()

    x_d = din("x", [S_SEQ, D])
    pos_d = din("pos", [1, S_SEQ], I32)
    cT_d = din("cT", [128, KC])
    wada_d = din("w_ada", [D, 6 * D])
    bada_d = din("b_ada", [1, 6 * D])
    badac_d = din("b_ada_col", [128, 48])
    g1c_d = din("g1_col", [128, KC])
    g2c_d = din("g2_col", [128, KC])
    win_d = din("w_in", [D, NCOL])
    lng_d = din("ln_g", [1, D])
    lnb_d = din("ln_b", [1, D])
    wsT_d = din("wsT", [128, 8, 128])
    bsc_d = din("bs_col", [128, 8])
    lamv_d = din("lamv", [1, 256])
    subg_d = din("subln_g", [1, 128])
    wout_d = din("w_out", [D, D])
    wff1_d = din("w_ff1", [D, DFF])
    wff2_d = din("w_ff2", [DFF, D])
    gfin_d = din("g_final", [1, D])
    ident_d = din("ident", [128, 128])
    rmat_d = din("rmat", [128, 128])
    cst_d = din("cst", [128, 4])
    out_d = nc.dram_tensor("out", [S_OWN, D], F32, kind="ExternalOutput").ap()

    TAB_d = dscr("tab", [2, 128, S_SEQ], F32)
    KT_d = dscr("ktd", [H, 128, S_SEQ], BF16)
    QT_d = dscr("qtd", [H, 128, S_OWN], BF16)
    V_d = dscr("vd", [H, 128, NCH, VW], BF16)
    GA_d = dscr("gad", [S_OWN, D], BF16)
    GB_d = dscr("gbd", [S_OWN, D], BF16)
    BB_d = dscr("bbd", [S_OWN, D], BF16)
    X1_d = dscr("x1d", [S_OWN, D], F32)
    H2T_d = dscr("h2td", [128, KC, S_OWN], BF16)
    GT_d = dscr("gtd", [2, D], F32)

    with ExitStack() as semstack, ExitStack() as glob:
        P = Prog(nc, semstack)

        def T(es, name, shape, dt):
            return es.enter_context(nc.sbuf_tensor("sb_" + name, list(shape), dt)), P.res(name)

        def PS(es, name, shape, dt=F32):
            return es.enter_context(nc.psum_tensor("pp_" + name, list(shape), dt)), P.res(name, excl=True)

        def dma(eng, out, in_, reads=(), writes=(), key=None):
            P.add(eng, lambda e: e.dma_start(out=out, in_=in_), reads=reads, writes=writes,
                  dma=True, semkey=key)

        ident, r_ident = T(glob, "ident", [128, 128], BF16)
        modcol, r_modcol = T(glob, "modcol", [128, 48], F32)
        A1, r_A1 = T(glob, "A1", [128, KC], F32)
        A2, r_A2 = T(glob, "A2", [128, KC], F32)
        lamc, r_lamc = T(glob, "lamc", [128, 1], F32)
        cst, r_cst = T(glob, "cst", [128, 4], F32)

        with ExitStack() as es:
            cTt, r_cTt = T(es, "cTt", [128, KC], F32)
            gt1row, r_gt1 = T(es, "gt1row", [128, D], F32)
            gt2row, r_gt2 = T(es, "gt2row", [128, D], F32)
            cact2, r_cact2 = T(es, "cact2", [128, KC, 2], F32)
            CB, r_CB = T(es, "CB", [128, KC, 128], F32)
            wad = [T(es, f"wad{i}", [128, KC, 1024], F32) for i in range(2)]
            badac, r_badac = T(es, "badac", [128, 48], F32)
            g1c, r_g1c = T(es, "g1c", [128, KC], F32)
            g2c, r_g2c = T(es, "g2c", [128, KC], F32)
            lamv, r_lamv = T(es, "lamv", [128, 256], F32)
            lprod, r_lprod = T(es, "lprod", [128, 128], F32)
            ls12, r_ls12 = T(es, "ls12", [128, 2], F32)
            posi, r_posi = T(es, "posi", [128, S_SEQ], I32)
            CW = min(2048, S_SEQ)
            posf, r_posf = T(es, "posf", [128, CW], F32)
            ang, r_ang = T(es, "ang", [128, CW], F32)
            kf, r_kf = T(es, "kf", [128, CW], F32)
            ki, r_ki = T(es, "ki", [128, CW], I32)
            tsin = [T(es, f"tsin{i}", [128, CW], F32) for i in range(2)]
            tcos = [T(es, f"tcos{i}", [128, CW], F32) for i in range(2)]
            sgnhp, r_sgnhp = T(es, "sgnhp", [128, 2], F32)
            ps_col, r_pscol = PS(es, "ps_col", [128, 512])
            ps_row = [PS(es, f"ps_row{i}", [128, 512]) for i in range(2)]

            dma("sp", cTt[:], cT_d, writes=[r_cTt], key="cTt")
            dma("sp", cst[:], cst_d, writes=[r_cst], key="cst")
            dma("sp", badac[:], badac_d, writes=[r_badac], key="badac")
            dma("sp", g1c[:], g1c_d, writes=[r_g1c], key="g1c")
            dma("sp", g2c[:], g2c_d, writes=[r_g2c], key="g2c")
            dma("sp", lamv[:], lamv_d.partition_broadcast(128), writes=[r_lamv], key="lamv")
            dma("sp", posi[:], pos_d.partition_broadcast(128), writes=[r_posi], key="posi")
            dma("pool", ident[:], ident_d, writes=[r_ident], key="ident")
            dma("sp", gt1row[:], bada_d[:, 2 * D:3 * D].partition_broadcast(128), writes=[r_gt1], key="gt1")
            dma("sp", gt2row[:], bada_d[:, 5 * D:6 * D].partition_broadcast(128), writes=[r_gt2], key="gt2")

            for c2 in range(2):
                P.add("act", lambda e, c2=c2: e.activation(out=cact2[:, :, c2], in_=cTt[:], func=AF.Silu),
                      reads=[r_cTt], writes=[r_cact2])
            P.add("dve", lambda e: e.tensor_copy(out=CB[:], in_=cact2[:, :, 0:1].to_broadcast([128, KC, 128])),
                  reads=[r_cact2], writes=[r_CB])

            wada_v = wada_d.rearrange("(kc p) n -> p kc n", p=128)
            for g in range(6):
                wt, r_wt = wad[g % 2]
                dma("sp", wt[:], wada_v[:, :, g * D:(g + 1) * D], writes=[r_wt], key=f"wad{g % 2}")
                if g in (0, 1, 3, 4):
                    for jj in range(8):
                        j = g * 8 + jj
                        for kc in range(KC):
                            P.add("pe", lambda e, wt=wt, jj=jj, j=j, kc=kc: e.matmul(
                                ps_col[:, 2 * j:2 * j + 2], lhsT=wt[:, kc, jj * 128:(jj + 1) * 128],
                                rhs=cact2[:, kc, :], start=(kc == 0), stop=(kc == KC - 1)),
                                reads=[r_wt, r_cact2], writes=[r_pscol])
                else:
                    grow, r_grow = (gt1row, r_gt1) if g == 2 else (gt2row, r_gt2)
                    for half in range(2):
                        pr, r_pr = ps_row[half]
                        for kc in range(KC):
                            P.add("pe", lambda e, wt=wt, pr=pr, half=half, kc=kc: e.matmul(
                                pr[:, :], lhsT=CB[:, kc, :], rhs=wt[:, kc, half * 512:(half + 1) * 512],
                                start=(kc == 0), stop=(kc == KC - 1)),
                                reads=[r_wt, r_CB], writes=[r_pr])
                        P.add("dve", lambda e, grow=grow, pr=pr, half=half: e.tensor_tensor(
                            out=grow[:, half * 512:(half + 1) * 512], in0=pr[:, :],
                            in1=grow[:, half * 512:(half + 1) * 512], op=ALU.add),
                            reads=[r_pr, r_grow], writes=[r_grow])
            dma("sp", GT_d[0:1, :], gt1row[0:1, :], reads=[r_gt1], key="gt1")
            dma("sp", GT_d[1:2, :], gt2row[0:1, :], reads=[r_gt2], key="gt2")
            for (j0, j1) in ((0, 16), (24, 40)):
                P.add("dve", lambda e, j0=j0, j1=j1: e.tensor_tensor(
                    out=modcol[:, j0:j1], in0=ps_col[:, 2 * j0:2 * j1].rearrange("p (j t) -> p j t", t=2)[:, :, 0],
                    in1=badac[:, j0:j1], op=ALU.add), reads=[r_pscol, r_badac], writes=[r_modcol])
            P.add("dve", lambda e: e.scalar_tensor_tensor(out=A1[:], in0=modcol[:, 8:16], scalar=1.0, in1=g1c[:],
                                                           op0=ALU.add, op1=ALU.mult),
                  reads=[r_modcol, r_g1c], writes=[r_A1])
            P.add("dve", lambda e: e.scalar_tensor_tensor(out=A2[:], in0=modcol[:, 32:40], scalar=1.0, in1=g2c[:],
                                                           op0=ALU.add, op1=ALU.mult),
                  reads=[r_modcol, r_g2c], writes=[r_A2])
            P.add("dve", lambda e: e.tensor_tensor(out=lprod[:], in0=lamv[:, 0:128], in1=lamv[:, 128:256], op=ALU.mult),
                  reads=[r_lamv], writes=[r_lprod])
            P.add("dve", lambda e: e.tensor_reduce(out=ls12[:], in_=lprod[:].rearrange("p (a b) -> p a b", a=2),
                                                   axis=AX.X, op=ALU.add), reads=[r_lprod], writes=[r_ls12])
            P.add("act", lambda e: e.activation(out=ls12[:], in_=ls12[:], func=AF.Exp), reads=[r_ls12], writes=[r_ls12])
            P.add("dve", lambda e: e.tensor_tensor(out=lamc[:], in0=ls12[:, 0:1], in1=ls12[:, 1:2], op=ALU.subtract),
                  reads=[r_ls12], writes=[r_lamc])
            P.add("dve", lambda e: e.tensor_scalar(out=lamc[:], in0=lamc[:], scalar1=LAMBDA_INIT, scalar2=None, op0=ALU.add),
                  reads=[r_lamc], writes=[r_lamc])
            for ci in range(S_SEQ // CW):
                cs = slice(ci * CW, (ci + 1) * CW)
                ts_, r_ts = tsin[ci % 2]
                tc_, r_tc = tcos[ci % 2]
                P.add("dve", lambda e, cs=cs: e.tensor_copy(out=posf[:], in_=posi[:, cs]), reads=[r_posi], writes=[r_posf])
                P.add("dve", lambda e: e.tensor_scalar(out=ang[:], in0=posf[:], scalar1=cst[:, 0:1], scalar2=None, op0=ALU.mult),
                      reads=[r_posf, r_cst], writes=[r_ang])
                P.add("dve", lambda e: e.tensor_scalar(out=kf[:], in0=ang[:], scalar1=1.0 / (2 * PI), scalar2=None, op0=ALU.mult),
                      reads=[r_ang], writes=[r_kf])
                P.add("dve", lambda e: e.tensor_copy(out=ki[:], in_=kf[:]), reads=[r_kf], writes=[r_ki])
                P.add("dve", lambda e: e.tensor_copy(out=kf[:], in_=ki[:]), reads=[r_ki], writes=[r_kf])
                P.add("dve", lambda e: e.scalar_tensor_tensor(out=kf[:], in0=kf[:], scalar=-2 * PI, in1=ang[:], op0=ALU.mult, op1=ALU.add),
                      reads=[r_kf, r_ang], writes=[r_kf])
                P.add("act", lambda e, ts_=ts_: e.activation(out=ts_[:], in_=kf[:], func=AF.Sin, scale=cst[:, 1:2]),
                      reads=[r_kf, r_cst], writes=[r_ts])
                P.add("dve", lambda e: e.tensor_scalar(out=kf[:], in0=ang[:], scalar1=1.0 / (2 * PI), scalar2=0.25, op0=ALU.mult, op1=ALU.add),
                      reads=[r_ang], writes=[r_kf])
                P.add("dve", lambda e: e.tensor_copy(out=ki[:], in_=kf[:]), reads=[r_kf], writes=[r_ki])
                P.add("dve", lambda e: e.tensor_copy(out=kf[:], in_=ki[:]), reads=[r_ki], writes=[r_kf])
                P.add("dve", lambda e: e.scalar_tensor_tensor(out=kf[:], in0=kf[:], scalar=-2 * PI, in1=ang[:], op0=ALU.mult, op1=ALU.add),
                      reads=[r_kf, r_ang], writes=[r_kf])
                P.add("act", lambda e, tc_=tc_: e.activation(out=tc_[:], in_=kf[:], func=AF.Sin, bias=cst[:, 2:3]),
                      reads=[r_kf, r_cst], writes=[r_tc])
                dma("sp", TAB_d[0, :, cs], tc_[:], reads=[r_tc], key=f"tcos{ci % 2}")
                dma("sp", TAB_d[1, :, cs], ts_[:], reads=[r_ts], key=f"tsin{ci % 2}")
            P.emit("phase0")

        with ExitStack() as es:
            P.disabled = (1 > _MAXPH)
            win, r_win = T(es, "win", [128, KC, NCOL], BF16)
            wsT, r_wsT = T(es, "wsT", [128, 8, 128], BF16)
            rmat, r_rmat = T(es, "rmat", [128, 128], BF16)
            lngr, r_lngr = T(es, "lngr", [128, D], F32)
            lnbr, r_lnbr = T(es, "lnbr", [128, D], F32)
            bsc, r_bsc = T(es, "bsc", [128, 8], F32)
            xt = Ring([T(es, f"xt{i}", [128, D], F32) for i in range(2)])
            junk, r_junk = T(es, "junk", [128, D], BF16)
            st = Ring([T(es, f"st{i}", [128, 8], F32) for i in range(2)])
            xn = Ring([T(es, f"xn{i}", [128, D], BF16) for i in range(2)])
            hT = Ring([T(es, f"hT{i}", [128, KC, 512], BF16) for i in range(2)])
            cosb = Ring([T(es, f"cosb{i}", [128, 512], F32) for i in range(2)])
            sinb = Ring([T(es, f"sinb{i}", [128, 512], F32) for i in range(2)])
            kraw = Ring([T(es, f"kraw{i}", [128, 512], BF16) for i in range(2)])
            t1r = Ring([T(es, f"t1r{i}", [128, 512], F32) for i in range(2)])
            t2r = Ring([T(es, f"t2r{i}", [128, 512], F32) for i in range(2)])
            kfin = Ring([T(es, f"kfin{i}", [128, 512], BF16) for i in range(4)])
            vblk = Ring([T(es, f"vblk{i}", [128, H, VW], BF16) for i in range(2)])
            gu, r_gu = T(es, "gu", [128, D], BF16)
            gv, r_gv = T(es, "gv", [128, D], F32)
            sga, r_sga = T(es, "sga", [128, D], BF16)
            vln, r_vln = T(es, "vln", [128, D], BF16)
            tmpf, r_tmpf = T(es, "tmpf", [128, D], F32)
            gbt = Ring([T(es, f"gbt{i}", [128, D], BF16) for i in range(2)])
            gat = Ring([T(es, f"gat{i}", [128, D], BF16) for i in range(2)])
            lst, r_lst = T(es, "lst", [128, 8], F32)
            ps_tr, r_pstr = PS(es, "ps_tr", [128, KC, 128], BF16)
            ps_tm = Ring([PS(es, f"ps_tm{i}", [128, 512]) for i in range(2)])
            ps_sv, r_pssv = PS(es, "ps_sv", [128, 8, 128])
            ps_k = Ring([PS(es, f"ps_k{i}", [128, 512]) for i in range(2)])
            ps_rot, r_psrot = PS(es, "ps_rot", [128, 512])

            for kc in range(KC):
                for hf in range(2):
                    c0, c1 = hf * (NCOL // 2), (hf + 1) * (NCOL // 2)
                    dma("pool", win[:, kc, c0:c1], win_d[kc * 128:(kc + 1) * 128, c0:c1], writes=[r_win], key="win")
            dma("pool", wsT[:], wsT_d, writes=[r_wsT], key="wsT")
            dma("pool", rmat[:], rmat_d, writes=[r_rmat], key="rmat")
            dma("sp", lngr[:], lng_d.partition_broadcast(128), writes=[r_lngr], key="lngr")
            dma("sp", lnbr[:], lnb_d.partition_broadcast(128), writes=[r_lnbr], key="lnbr")
            dma("sp", bsc[:], bsc_d, writes=[r_bsc], key="bsc")
            for (vb, r_vb) in vblk.items:
                P.add("pool", lambda e, vb=vb: e.memset(vb[:, :, 128:VW], 1.0), writes=[r_vb])

            def tm_group(hTt, r_hT, i, cg, consumer):
                pt, r_pt = ps_tm.next()
                for kc in range(KC):
                    P.add("pe", lambda e, pt=pt, kc=kc: e.matmul(
                        pt[:, :], lhsT=hTt[:, kc, i * 128:(i + 1) * 128], rhs=win[:, kc, cg * 512:(cg + 1) * 512],
                        start=(kc == 0), stop=(kc == KC - 1)), reads=[r_hT, r_win], writes=[r_pt])
                consumer(pt, r_pt)

            for blk in range(NB_SEQ):
                own = blk < NB_OWN and not (_KSUB & 1)
                hTt, r_hT = hT.next()
                cb, r_cb = cosb.next()
                sb_, r_sb = sinb.next()
                bs = slice(blk * 512, (blk + 1) * 512)
                dma("sp", cb[:], TAB_d[0, :, bs], writes=[r_cb], key=f"cosb{cosb.i}")
                dma("sp", sb_[:], TAB_d[1, :, bs], writes=[r_sb], key=f"sinb{sinb.i}")
                for i in range(4):
                    t = blk * 4 + i
                    xtt, r_xt = xt.next()
                    stt, r_st = st.next()
                    xnt, r_xn = xn.next()
                    vb, r_vb = vblk.next()
                    vbi = vblk.i
                    dma("sp", xtt[:], x_d[t * 128:(t + 1) * 128, :], writes=[r_xt], key=f"xt{xt.i}")
                    P.add("act", lambda e, xtt=xtt, stt=stt: e.activation(out=junk[:], in_=xtt[:], func=AF.Square, accum_out=stt[:, 0:1]),
                          reads=[r_xt], writes=[r_junk, r_st])
                    P.add("dve", lambda e, stt=stt: e.tensor_scalar(out=stt[:, 1:2], in0=stt[:, 0:1], scalar1=1.0 / D, scalar2=EPS,
                                                                    op0=ALU.mult, op1=ALU.add), reads=[r_st], writes=[r_st])
                    P.add("act", lambda e, stt=stt: e.activation(out=stt[:, 2:3], in_=stt[:, 1:2], func=AF.Sqrt), reads=[r_st], writes=[r_st])
                    P.add("dve", lambda e, stt=stt: e.reciprocal(out=stt[:, 3:4], in_=stt[:, 2:3]), reads=[r_st], writes=[r_st])
                    P.add("pool", lambda e, xnt=xnt, xtt=xtt, stt=stt: e.tensor_scalar(out=xnt[:], in0=xtt[:], scalar1=stt[:, 3:4], scalar2=None,
                                                                                      op0=ALU.mult), reads=[r_xt, r_st], writes=[r_xn])
                    for kc in range(KC):
                        P.add("pe", lambda e, xnt=xnt, kc=kc: e.transpose(out=ps_tr[:, kc, :], in_=xnt[:, kc * 128:(kc + 1) * 128], identity=ident[:]),
                              reads=[r_xn, r_ident], writes=[r_pstr])
                    for kc in range(KC):
                        P.add("dve", lambda e, kc=kc, i=i, hTt=hTt: e.tensor_scalar(
                            out=hTt[:, kc, i * 128:(i + 1) * 128], in0=ps_tr[:, kc, :], scalar1=A1[:, kc:kc + 1], scalar2=modcol[:, kc:kc + 1],
                            op0=ALU.mult, op1=ALU.add), reads=[r_pstr, r_A1, r_modcol], writes=[r_hT])
                    for hf in (() if (_KSUB & 4) else range(2)):
                        def cons_v(pt, r_pt, hf=hf, i=i, vb=vb, r_vb=r_vb):
                            P.add("dve", lambda e: e.tensor_copy(out=vb[:, hf * 4:(hf + 1) * 4, 0:128],
                                                                 in_=pt[:, :].rearrange("p (h e) -> p h e", h=4)),
                                  reads=[r_pt], writes=[r_vb])
                        tm_group(hTt, r_hT, i, 8 + hf, cons_v)
                    dma("sp", V_d[:, :, t, :].rearrange("h p e -> p h e"), vb[:], reads=[r_vb], key=f"vblk{vbi}")
                    if not own:
                        continue
                    gbt_t, r_gbt = gbt.next()
                    gat_t, r_gat = gat.next()
                    for hf in range(2):
                        def cons_u(pt, r_pt, hf=hf):
                            P.add("act", lambda e: e.activation(out=gu[:, hf * 512:(hf + 1) * 512], in_=pt[:, :], func=AF.Gelu_apprx_tanh),
                                  reads=[r_pt], writes=[r_gu])
                        tm_group(hTt, r_hT, i, 0 + hf, cons_u)
                    for hf in range(2):
                        def cons_va(pt, r_pt, hf=hf):
                            P.add("act", lambda e: e.activation(out=gv[:, hf * 512:(hf + 1) * 512], in_=pt[:, :], func=AF.Gelu_apprx_tanh,
                                                                accum_out=lst[:, hf:hf + 1]), reads=[r_pt], writes=[r_gv, r_lst])
                        tm_group(hTt, r_hT, i, 2 + hf, cons_va)
                    for hf in range(2):
                        def cons_ga(pt, r_pt, hf=hf):
                            P.add("act", lambda e: e.activation(out=sga[:, hf * 512:(hf + 1) * 512], in_=pt[:, :], func=AF.Sigmoid),
                                  reads=[r_pt], writes=[r_sga])
                        tm_group(hTt, r_hT, i, 10 + hf, cons_ga)
                    for hf in range(2):
                        def cons_gb(pt, r_pt, hf=hf, gbt_t=gbt_t, r_gbt=r_gbt):
                            P.add("act", lambda e: e.activation(out=gbt_t[:, hf * 512:(hf + 1) * 512], in_=pt[:, :], func=AF.Sigmoid),
                                  reads=[r_pt], writes=[r_gbt])
                        tm_group(hTt, r_hT, i, 12 + hf, cons_gb)
                    dma("sp", GB_d[t * 128:(t + 1) * 128, :], gbt_t[:], reads=[r_gbt], key=f"gbt{gbt.i}")
                    if _KSUB & 8:
                        continue
                    P.add("act", lambda e: e.activation(out=junk[:], in_=gv[:], func=AF.Square, accum_out=lst[:, 2:3]),
                          reads=[r_gv], writes=[r_junk, r_lst])
                    P.add("dve", lambda e: e.tensor_tensor(out=lst[:, 3:4], in0=lst[:, 0:1], in1=lst[:, 1:2], op=ALU.add), reads=[r_lst], writes=[r_lst])
                    P.add("dve", lambda e: e.tensor_scalar(out=lst[:, 3:4], in0=lst[:, 3:4], scalar1=-1.0 / D, scalar2=None, op0=ALU.mult),
                          reads=[r_lst], writes=[r_lst])
                    P.add("dve", lambda e: e.tensor_tensor(out=lst[:, 4:5], in0=lst[:, 3:4], in1=lst[:, 3:4], op=ALU.mult), reads=[r_lst], writes=[r_lst])
                    P.add("dve", lambda e: e.scalar_tensor_tensor(out=lst[:, 5:6], in0=lst[:, 2:3], scalar=1.0 / D, in1=lst[:, 4:5],
                                                                   op0=ALU.mult, op1=ALU.subtract), reads=[r_lst], writes=[r_lst])
                    P.add("dve", lambda e: e.tensor_scalar(out=lst[:, 5:6], in0=lst[:, 5:6], scalar1=EPS, scalar2=None, op0=ALU.add),
                          reads=[r_lst], writes=[r_lst])
                    P.add("act", lambda e: e.activation(out=lst[:, 6:7], in_=lst[:, 5:6], func=AF.Sqrt), reads=[r_lst], writes=[r_lst])
                    P.add("dve", lambda e: e.reciprocal(out=lst[:, 7:8], in_=lst[:, 6:7]), reads=[r_lst], writes=[r_lst])
                    P.add("dve", lambda e: e.tensor_scalar(out=gv[:], in0=gv[:], scalar1=lst[:, 3:4], scalar2=lst[:, 7:8], op0=ALU.add, op1=ALU.mult),
                          reads=[r_gv, r_lst], writes=[r_gv])
                    P.add("pool", lambda e: e.tensor_tensor(out=gv[:], in0=gv[:], in1=lngr[:], op=ALU.mult), reads=[r_gv, r_lngr], writes=[r_gv])
                    P.add("pool", lambda e: e.tensor_tensor(out=vln[:], in0=gv[:], in1=lnbr[:], op=ALU.add), reads=[r_gv, r_lnbr], writes=[r_vln])
                    if _KSUB & 16:
                        continue
                    for g in range(8):
                        P.add("pe", lambda e, g=g: e.matmul(ps_sv[:, g, :], lhsT=wsT[:, g, :], rhs=vln[:, g * 128:(g + 1) * 128], start=True, stop=True),
                              reads=[r_wsT, r_vln], writes=[r_pssv])
                    for g in range(8):
                        P.add("dve", lambda e, g=g: e.scalar_tensor_tensor(out=tmpf[:, g * 128:(g + 1) * 128], in0=ps_sv[:, g, :], scalar=bsc[:, g:g + 1],
                                                                            in1=gu[:, g * 128:(g + 1) * 128], op0=ALU.add, op1=ALU.mult),
                              reads=[r_pssv, r_bsc, r_gu], writes=[r_tmpf])
                    P.add("pool", lambda e, gat_t=gat_t: e.tensor_tensor(out=gat_t[:], in0=tmpf[:], in1=sga[:], op=ALU.mult),
                          reads=[r_tmpf, r_sga], writes=[r_gat])
                    dma("sp", GA_d[t * 128:(t + 1) * 128, :], gat_t[:], reads=[r_gat], key=f"gat{gat.i}")
                for which in (() if (_KSUB & 2) else ((("k", 3072),) + ((("q", 2048),) if blk < NB_OWN else ()))):
                    nm, cbase = which
                    dst = KT_d if nm == "k" else QT_d
                    for h in range(H):
                        kf_t, r_kf_t = kfin.next()
                        pk, r_pk = ps_k.next()
                        kr, r_kr = kraw.next()
                        t1, r_t1 = t1r.next()
                        t2, r_t2 = t2r.next()
                        for kc in range(KC):
                            P.add("pe", lambda e, pk=pk, kc=kc, h=h, cbase=cbase, hTt=hTt: e.matmul(
                                pk[:, :], lhsT=win[:, kc, cbase + h * 128:cbase + (h + 1) * 128], rhs=hTt[:, kc, :],
                                start=(kc == 0), stop=(kc == KC - 1)), reads=[r_win, r_hT], writes=[r_pk])
                        P.add("act", lambda e, kr=kr, pk=pk: e.activation(out=kr[:], in_=pk[:, :], func=AF.Copy), reads=[r_pk], writes=[r_kr])
                        P.add("pe", lambda e, kr=kr: e.matmul(ps_rot[:, :], lhsT=rmat[:], rhs=kr[:], start=True, stop=True),
                              reads=[r_rmat, r_kr], writes=[r_psrot])
                        P.add("dve", lambda e, t1=t1, pk=pk, cb=cb: e.tensor_tensor(out=t1[:], in0=pk[:, :], in1=cb[:], op=ALU.mult),
                              reads=[r_pk, r_cb], writes=[r_t1])
                        P.add("dve", lambda e, t2=t2, sb_=sb_: e.tensor_tensor(out=t2[:], in0=ps_rot[:, :], in1=sb_[:], op=ALU.mult),
                              reads=[r_psrot, r_sb], writes=[r_t2])
                        P.add("pool", lambda e, kf_t=kf_t, h=h, t1=t1, t2=t2: e.tensor_tensor(out=kf_t[:], in0=t1[:], in1=t2[:], op=ALU.add),
                              reads=[r_t1, r_t2], writes=[r_kf_t])
                        dma("sp", dst[h, :, bs], kf_t[:], reads=[r_kf_t], key=f"kfin{kfin.i}")
            P.emit("phase1")

        with ExitStack() as es:
            P.disabled = (2 > _MAXPH)
            KT = [T(es, f"KT{i}", [128, S_SEQ], BF16) for i in range(2)]
            VA = [T(es, f"VA{i}", [128, NCH, VW], BF16) for i in range(2)]
            Q1 = [T(es, f"Q1p{i}", [128, S_OWN], BF16) for i in range(2)]
            Q2 = [T(es, f"Q2p{i}", [128, S_OWN], BF16) for i in range(2)]
            ET = Ring([T(es, f"ET{i}", [128, 2, 2, 256], BF16) for i in range(3)])
            obuf = [T(es, f"obuf{i}", [128, NT_OWN, 128], BF16) for i in range(2)]
            nst = Ring([T(es, f"nst{i}", [128, 4], F32) for i in range(2)])
            ntmp = Ring([T(es, f"ntmp{i}", [128, 128], F32) for i in range(2)])
            SP_ = Ring([PS(es, f"S{i}", [128, 2, 2, 256]) for i in range(2)])
            acc = [[PS(es, f"acc{m}{qt}", [128, 512]) for qt in range(2)] for m in range(2)]

            for s in range(2):
                P.add("pool", lambda e, s=s: e.memset(Q1[s][0][64:128, :], 0.0), writes=[Q1[s][1]])
                P.add("pool", lambda e, s=s: e.memset(Q2[s][0][0:64, :], 0.0), writes=[Q2[s][1]])

            def load_head(h):
                s = h % 2
                dma("sp", KT[s][0][:], KT_d[h], writes=[KT[s][1]], key=f"KT{s}")
                dma("sp", VA[s][0][:], V_d[h], writes=[VA[s][1]], key=f"VA{s}")
                dma("sp", Q1[s][0][0:64, :], QT_d[h, 0:64, :], writes=[Q1[s][1]], key=f"Q1{s}")
                dma("sp", Q2[s][0][64:128, :], QT_d[h, 64:128, :], writes=[Q2[s][1]], key=f"Q2{s}")

            steps = [(h, qb, j) for h in range(H) for qb in range(NQB) for j in range(NPAIR)]

            def issue_qk(step):
                h, qb, j = step
                s = h % 2
                S_, r_S = SP_.next()
                qs = slice(qb * 256, (qb + 1) * 256)
                for kk in range(2):
                    c = 2 * j + kk
                    P.add("pe", lambda e, S_=S_, kk=kk, c=c, s=s, qs=qs: e.matmul(
                        S_[:, kk, 0, :], lhsT=KT[s][0][:, c * 128:(c + 1) * 128], rhs=Q1[s][0][:, qs], start=True, stop=True),
                        reads=[KT[s][1], Q1[s][1]], writes=[r_S])
                    P.add("pe", lambda e, S_=S_, kk=kk, c=c, s=s, qs=qs: e.matmul(
                        S_[:, kk, 1, :], lhsT=KT[s][0][:, c * 128:(c + 1) * 128], rhs=Q2[s][0][:, qs], start=True, stop=True),
                        reads=[KT[s][1], Q2[s][1]], writes=[r_S])
                return S_, r_S

            load_head(0)
            pending = issue_qk(steps[0])
            for si, (h, qb, j) in enumerate(steps):
                s = h % 2
                if qb == 0 and j == 0 and h + 1 < H:
                    load_head(h + 1)
                S_, r_S = pending
                if si + 1 < len(steps):
                    pending = issue_qk(steps[si + 1])
                E_, r_E = ET.next()
                P.add("act", lambda e, E_=E_, S_=S_: e.activation(out=E_[:].rearrange("p a b c -> p (a b c)"),
                                                                  in_=S_[:].rearrange("p a b c -> p (a b c)"), func=AF.Exp, scale=0.125),
                      reads=[r_S], writes=[r_E])
                for kk in range(2):
                    c = 2 * j + kk
                    for m in range(2):
                        for qt in range(2):
                            a_, r_a = acc[m][qt]
                            P.add("pe", lambda e, a_=a_, E_=E_, kk=kk, m=m, qt=qt, c=c, s=s: e.matmul(
                                a_[:, 0:129], lhsT=E_[:, kk, m, qt * 128:(qt + 1) * 128], rhs=VA[s][0][:, c, 0:129],
                                start=(c == 0), stop=(c == NCH - 1)), reads=[r_E, VA[s][1]], writes=[r_a])
                if j == NPAIR - 1:
                    ob, r_ob = obuf[h % 2]
                    for qt in range(2):
                        a0, r_a0 = acc[0][qt]
                        a1, r_a1 = acc[1][qt]
                        ns, r_ns = nst.next()
                        nt, r_nt = ntmp.next()
                        P.add("dve", lambda e, ns=ns, a0=a0: e.reciprocal(out=ns[:, 0:1], in_=a0[:, 128:129]), reads=[r_a0], writes=[r_ns])
                        P.add("dve", lambda e, ns=ns, a1=a1: e.reciprocal(out=ns[:, 1:2], in_=a1[:, 128:129]), reads=[r_a1], writes=[r_ns])
                        P.add("dve", lambda e, ns=ns: e.tensor_tensor(out=ns[:, 2:3], in0=ns[:, 1:2], in1=lamc[:], op=ALU.mult),
                              reads=[r_ns, r_lamc], writes=[r_ns])
                        P.add("dve", lambda e, ns=ns, nt=nt, a1=a1: e.tensor_scalar(out=nt[:], in0=a1[:, 0:128], scalar1=ns[:, 2:3], scalar2=None, op0=ALU.mult),
                              reads=[r_a1, r_ns], writes=[r_nt])
                        P.add("dve", lambda e, ns=ns, nt=nt, a0=a0, ob=ob, qb=qb, qt=qt: e.scalar_tensor_tensor(
                            out=ob[:, qb * 2 + qt, :], in0=a0[:, 0:128], scalar=ns[:, 0:1], in1=nt[:], op0=ALU.mult, op1=ALU.subtract),
                            reads=[r_a0, r_ns, r_nt], writes=[r_ob])
                    if qb == NQB - 1:
                        TG = min(8, NT_OWN)
                        for t0 in range(0, NT_OWN, TG):
                            dma("sp", BB_d[t0 * 128:(t0 + TG) * 128, h * 128:(h + 1) * 128].rearrange("(t p) e -> p t e", p=128),
                                ob[:, t0:t0 + TG, :], reads=[r_ob], key=f"obuf{h % 2}")
            P.emit("phase2")

        with ExitStack() as es:
            P.disabled = (3 > _MAXPH)
            wout, r_wout = T(es, "wout", [128, KC, D], BF16)
            g08, r_g08 = T(es, "g08", [128, 128], F32)
            gt1row, r_gt1 = T(es, "gt1row3", [128, D], F32)
            dma("sp", gt1row[:], GT_d[0:1, :].partition_broadcast(128), writes=[r_gt1], key="gt1row3")
            gaT = Ring([T(es, f"gaT{i}", [128, D], BF16) for i in range(2)])
            gbT = Ring([T(es, f"gbT{i}", [128, D], BF16) for i in range(2)])
            bbT = Ring([T(es, f"bbT{i}", [128, D], BF16) for i in range(2)])
            xt = Ring([T(es, f"xa{i}", [128, D], F32) for i in range(2)])
            sq, r_sq = T(es, "sq", [128, D], F32)
            s8 = Ring([T(es, f"s8{i}", [128, 32], F32) for i in range(2)])
            mg, r_mg = T(es, "mg", [128, D], BF16)
            mT, r_mT = T(es, "mT", [128, KC, 128], BF16)
            x1 = Ring([T(es, f"x1{i}", [128, D], F32) for i in range(2)])
            junk, r_junk = T(es, "junk3", [128, D], BF16)
            xn2, r_xn2 = T(es, "xn2", [128, D], BF16)
            h2 = Ring([T(es, f"h2{i}", [128, KC, 128], BF16) for i in range(2)])
            ps_tr, r_pstr = PS(es, "ps_tr3", [128, KC, 128], BF16)
            ps_tr2, r_pstr2 = PS(es, "ps_tr3b", [128, KC, 128], BF16)
            ps_o = [PS(es, f"ps_o{i}", [128, 512]) for i in range(2)]

            for kc in range(KC):
                dma("pool", wout[:, kc, :], wout_d[kc * 128:(kc + 1) * 128, :], writes=[r_wout], key="wout")
            dma("sp", g08[:], subg_d.partition_broadcast(128), writes=[r_g08], key="g08")
            P.add("dve", lambda e: e.tensor_scalar(out=g08[:], in0=g08[:], scalar1=1.0 - LAMBDA_INIT, scalar2=None, op0=ALU.mult),
                  reads=[r_g08], writes=[r_g08])
            for t in range(NT_OWN):
                rows = slice(t * 128, (t + 1) * 128)
                ga_, r_ga = gaT.next()
                gb_, r_gb = gbT.next()
                bb_, r_bb = bbT.next()
                xa, r_xa = xt.next()
                s8t, r_s8 = s8.next()
                x1t, r_x1 = x1.next()
                h2t, r_h2 = h2.next()
                dma("sp", ga_[:], GA_d[rows, :], writes=[r_ga], key=f"gaT{gaT.i}")
                dma("sp", gb_[:], GB_d[rows, :], writes=[r_gb], key=f"gbT{gbT.i}")
                dma("sp", bb_[:], BB_d[rows, :], writes=[r_bb], key=f"bbT{bbT.i}")
                dma("sp", xa[:], x_d[rows, :], writes=[r_xa], key=f"xa{xt.i}")
                P.add("dve", lambda e, bb_=bb_: e.tensor_tensor(out=sq[:], in0=bb_[:], in1=bb_[:], op=ALU.mult), reads=[r_bb], writes=[r_sq])
                P.add("dve", lambda e, s8t=s8t: e.tensor_reduce(out=s8t[:, 0:8], in_=sq[:].rearrange("p (h e) -> p h e", h=8), axis=AX.X, op=ALU.add),
                      reads=[r_sq], writes=[r_s8])
                P.add("dve", lambda e, s8t=s8t: e.tensor_scalar(out=s8t[:, 8:16], in0=s8t[:, 0:8], scalar1=1.0 / 128, scalar2=EPS, op0=ALU.mult, op1=ALU.add),
                      reads=[r_s8], writes=[r_s8])
                P.add("act", lambda e, s8t=s8t: e.activation(out=s8t[:, 16:24], in_=s8t[:, 8:16], func=AF.Sqrt), reads=[r_s8], writes=[r_s8])
                P.add("dve", lambda e, s8t=s8t: e.reciprocal(out=s8t[:, 24:32], in_=s8t[:, 16:24]), reads=[r_s8], writes=[r_s8])
                for hh in range(8):
                    P.add("dve", lambda e, s8t=s8t, bb_=bb_, hh=hh: e.scalar_tensor_tensor(
                        out=sq[:, hh * 128:(hh + 1) * 128], in0=bb_[:, hh * 128:(hh + 1) * 128], scalar=s8t[:, 24 + hh:25 + hh], in1=g08[:],
                        op0=ALU.mult, op1=ALU.mult), reads=[r_bb, r_s8, r_g08], writes=[r_sq])
                P.add("pool", lambda e, gb_=gb_: e.tensor_tensor(out=sq[:], in0=sq[:], in1=gb_[:], op=ALU.mult), reads=[r_sq, r_gb], writes=[r_sq])
                P.add("dve", lambda e, ga_=ga_: e.tensor_tensor(out=mg[:], in0=sq[:], in1=ga_[:], op=ALU.add), reads=[r_sq, r_ga], writes=[r_mg])
                for kc in range(KC):
                    P.add("pe", lambda e, kc=kc: e.transpose(out=ps_tr[:, kc, :], in_=mg[:, kc * 128:(kc + 1) * 128], identity=ident[:]),
                          reads=[r_mg, r_ident], writes=[r_pstr])
                P.add("act", lambda e: e.activation(out=mT[:].rearrange("p a b -> p (a b)"), in_=ps_tr[:].rearrange("p a b -> p (a b)"), func=AF.Copy),
                      reads=[r_pstr], writes=[r_mT])
                for hf in range(2):
                    po, r_po = ps_o[hf]
                    for kc in range(KC):
                        P.add("pe", lambda e, po=po, kc=kc, hf=hf: e.matmul(po[:, :], lhsT=mT[:, kc, :], rhs=wout[:, kc, hf * 512:(hf + 1) * 512],
                                                                           start=(kc == 0), stop=(kc == KC - 1)), reads=[r_mT, r_wout], writes=[r_po])
                    P.add("dve", lambda e, po=po, hf=hf, x1t=x1t: e.tensor_tensor(out=x1t[:, hf * 512:(hf + 1) * 512], in0=po[:, :],
                                                                               in1=gt1row[:, hf * 512:(hf + 1) * 512], op=ALU.mult),
                          reads=[r_po, r_gt1], writes=[r_x1])
                P.add("pool", lambda e, x1t=x1t, xa=xa: e.tensor_tensor(out=x1t[:], in0=x1t[:], in1=xa[:], op=ALU.add), reads=[r_x1, r_xa], writes=[r_x1])
                dma("sp", X1_d[rows, :], x1t[:], reads=[r_x1], key=f"x1{x1.i}")
                P.add("act", lambda e, x1t=x1t, s8t=s8t: e.activation(out=junk[:], in_=x1t[:], func=AF.Square, accum_out=s8t[:, 0:1]),
                      reads=[r_x1, r_s8], writes=[r_junk, r_s8])
                P.add("dve", lambda e, s8t=s8t: e.tensor_scalar(out=s8t[:, 1:2], in0=s8t[:, 0:1], scalar1=1.0 / D, scalar2=EPS, op0=ALU.mult, op1=ALU.add),
                      reads=[r_s8], writes=[r_s8])
                P.add("act", lambda e, s8t=s8t: e.activation(out=s8t[:, 2:3], in_=s8t[:, 1:2], func=AF.Sqrt), reads=[r_s8], writes=[r_s8])
                P.add("dve", lambda e, s8t=s8t: e.reciprocal(out=s8t[:, 3:4], in_=s8t[:, 2:3]), reads=[r_s8], writes=[r_s8])
                P.add("pool", lambda e, x1t=x1t, s8t=s8t: e.tensor_scalar(out=xn2[:], in0=x1t[:], scalar1=s8t[:, 3:4], scalar2=None, op0=ALU.mult),
                      reads=[r_x1, r_s8], writes=[r_xn2])
                for kc in range(KC):
                    P.add("pe", lambda e, kc=kc: e.transpose(out=ps_tr2[:, kc, :], in_=xn2[:, kc * 128:(kc + 1) * 128], identity=ident[:]),
                          reads=[r_xn2, r_ident], writes=[r_pstr2])
                for kc in range(KC):
                    P.add("dve", lambda e, kc=kc, h2t=h2t: e.tensor_scalar(out=h2t[:, kc, :], in0=ps_tr2[:, kc, :], scalar1=A2[:, kc:kc + 1],
                                                                         scalar2=modcol[:, 24 + kc:25 + kc], op0=ALU.mult, op1=ALU.add),
                          reads=[r_pstr2, r_A2, r_modcol], writes=[r_h2])
                dma("sp", H2T_d[:, :, rows], h2t[:], reads=[r_h2], key=f"h2{h2.i}")
            P.emit("phase3a")

        with ExitStack() as es:
            P.disabled = (4 > _MAXPH)
            wf1, r_wf1 = T(es, "wf1", [128, KC, DFF], BF16)
            wf2, r_wf2 = T(es, "wf2", [128, 32, D], BF16)
            gfr, r_gfr = T(es, "gfr", [128, D], F32)
            gt2row, r_gt2 = T(es, "gt2row3", [128, D], F32)
            dma("sp", gt2row[:], GT_d[1:2, :].partition_broadcast(128), writes=[r_gt2], key="gt2row3")
            h2b = Ring([T(es, f"h2b{i}", [128, KC, 256], BF16) for i in range(2)])
            sqr = Ring([T(es, f"sqr{i}", [128, 256], F32) for i in range(2)])
            aT = Ring([T(es, f"aT{i}", [128, 256], BF16) for i in range(3)])
            x1 = Ring([T(es, f"x1b{i}", [128, D], F32) for i in range(2)])
            x2 = Ring([T(es, f"x2{i}", [128, D], F32) for i in range(2)])
            junk, r_junk = T(es, "junk4", [128, D], BF16)
            s4 = Ring([T(es, f"s4{i}", [128, 4], F32) for i in range(2)])
            ps_f = Ring([PS(es, f"ps_f{i}", [128, 512]) for i in range(2)])
            acc = [[PS(es, f"fa{tl}{cg}", [128, 512]) for cg in range(2)] for tl in range(2)]

            for kc in range(KC):
                dma("pool", wf1[:, kc, :], wff1_d[kc * 128:(kc + 1) * 128, :], writes=[r_wf1], key="wf1")
            for j0 in range(0, 32, 4):
                dma("pool", wf2[:, j0:j0 + 4, :], wff2_d[j0 * 128:(j0 + 4) * 128, :].rearrange("(j p) n -> p j n", p=128), writes=[r_wf2], key="wf2")
            dma("sp", gfr[:], gfin_d.partition_broadcast(128), writes=[r_gfr], key="gfr")

            NB3 = S_OWN // 256

            def issue_f1(hb, j):
                pf, r_pf = ps_f.next()
                for kc in range(KC):
                    P.add("pe", lambda e, pf=pf, kc=kc, j=j, hb=hb: e.matmul(pf[:, 0:256], lhsT=wf1[:, kc, j * 128:(j + 1) * 128], rhs=hb[0][:, kc, :],
                                                                          start=(kc == 0), stop=(kc == KC - 1)), reads=[r_wf1, hb[1]], writes=[r_pf])
                return pf, r_pf

            cur = h2b.next()
            dma("sp", cur[0][:], H2T_d[:, :, 0:256], writes=[cur[1]], key=f"h2b{h2b.i}")
            for b3 in range(NB3):
                hb = cur
                if b3 + 1 < NB3:
                    cur = h2b.next()
                    dma("sp", cur[0][:], H2T_d[:, :, (b3 + 1) * 256:(b3 + 2) * 256], writes=[cur[1]], key=f"h2b{h2b.i}")
                pend = issue_f1(hb, 0)
                for j in range(32):
                    pf, r_pf = pend
                    if j + 1 < 32:
                        pend = issue_f1(hb, j + 1)
                    sq_, r_sq = sqr.next()
                    a_, r_a = aT.next()
                    P.add("act", lambda e, sq_=sq_, pf=pf: e.activation(out=sq_[:], in_=pf[:, 0:256], func=AF.Square), reads=[r_pf], writes=[r_sq])
                    P.add("dve", lambda e, sq_=sq_, pf=pf, a_=a_: e.scalar_tensor_tensor(out=a_[:], in0=pf[:, 0:256], scalar=0.0, in1=sq_[:],
                                                                                     op0=ALU.is_gt, op1=ALU.mult), reads=[r_pf, r_sq], writes=[r_a])
                    for tl in range(2):
                        for cg in range(2):
                            fa, r_fa = acc[tl][cg]
                            P.add("pe", lambda e, fa=fa, a_=a_, tl=tl, cg=cg, j=j: e.matmul(fa[:, :], lhsT=a_[:, tl * 128:(tl + 1) * 128],
                                                                                          rhs=wf2[:, j, cg * 512:(cg + 1) * 512], start=(j == 0), stop=(j == 31)),
                                  reads=[r_a, r_wf2], writes=[r_fa])
                for tl in range(2):
                    t = b3 * 2 + tl
                    rows = slice(t * 128, (t + 1) * 128)
                    x1t, r_x1 = x1.next()
                    x2t, r_x2 = x2.next()
                    s4t, r_s4 = s4.next()
                    dma("sp", x1t[:], X1_d[rows, :], writes=[r_x1], key=f"x1b{x1.i}")
                    for cg in range(2):
                        fa, r_fa = acc[tl][cg]
                        P.add("dve", lambda e, fa=fa, cg=cg, x2t=x2t: e.tensor_tensor(out=x2t[:, cg * 512:(cg + 1) * 512], in0=fa[:, :],
                                                                                   in1=gt2row[:, cg * 512:(cg + 1) * 512], op=ALU.mult),
                              reads=[r_fa, r_gt2], writes=[r_x2])
                    P.add("pool", lambda e, x2t=x2t, x1t=x1t: e.tensor_tensor(out=x2t[:], in0=x2t[:], in1=x1t[:], op=ALU.add), reads=[r_x2, r_x1], writes=[r_x2])
                    P.add("act", lambda e, x2t=x2t, s4t=s4t: e.activation(out=junk[:], in_=x2t[:], func=AF.Square, accum_out=s4t[:, 0:1]),
                          reads=[r_x2], writes=[r_junk, r_s4])
                    P.add("dve", lambda e, s4t=s4t: e.tensor_scalar(out=s4t[:, 1:2], in0=s4t[:, 0:1], scalar1=1.0 / D, scalar2=EPS, op0=ALU.mult, op1=ALU.add),
                          reads=[r_s4], writes=[r_s4])
                    P.add("act", lambda e, s4t=s4t: e.activation(out=s4t[:, 2:3], in_=s4t[:, 1:2], func=AF.Sqrt), reads=[r_s4], writes=[r_s4])
                    P.add("dve", lambda e, s4t=s4t: e.reciprocal(out=s4t[:, 3:4], in_=s4t[:, 2:3]), reads=[r_s4], writes=[r_s4])
                    P.add("dve", lambda e, x2t=x2t, s4t=s4t: e.scalar_tensor_tensor(out=x2t[:], in0=x2t[:], scalar=s4t[:, 3:4], in1=gfr[:], op0=ALU.mult, op1=ALU.mult),
                          reads=[r_x2, r_s4, r_gfr], writes=[r_x2])
                    dma("sp", out_d[rows, :], x2t[:], reads=[r_x2], key=f"x2{x2.i}")
            P.emit("phase3b")
    return nc


_NC_CACHE = {}


def _consts():
    ident = np.eye(128, dtype=np.float32)
    rmat = np.zeros((128, 128), dtype=np.float32)
    for p in range(128):
        partner = p + 32 if (p % 64) < 32 else p - 32
        rmat[partner, p] = 1.0
    inv_freq = (np.float32(10000.0) ** (-np.arange(0, 64, 2, dtype=np.float32) / np.float32(64))).astype(np.float32)
    cst = np.zeros((128, 4), dtype=np.float32)
    for p in range(128):
        cst[p, 0] = inv_freq[p % 32]
        cst[p, 1] = -1.0 if (p % 64) < 32 else 1.0
        cst[p, 2] = np.float32(np.pi / 2)
    return ident, rmat, cst


def _run(inputs, n_cores_per_seq=2):
    x = np.asarray(inputs["x"], dtype=np.float32)
    B, S, _ = x.shape
    S_OWN = S // n_cores_per_seq
    n_cores = B * n_cores_per_seq
    key = (S_OWN, S)
    if key not in _NC_CACHE:
        _NC_CACHE[key] = build(S_OWN, S)
    nc = _NC_CACHE[key]
    ident, rmat, cst = _consts()
    f = lambda k: np.ascontiguousarray(np.asarray(inputs[k], dtype=np.float32))
    pos = np.asarray(inputs["positions"]).astype(np.int32, copy=False)
    c = f("c")
    col = lambda v: np.ascontiguousarray(v.reshape(-1, 128).T)
    shared = {
        "w_ada": f("w_ada")[0], "b_ada": f("b_ada")[0][None, :], "b_ada_col": col(f("b_ada")[0]),
        "g1_col": col(f("g_norm1")[0]), "g2_col": col(f("g_norm2")[0]),
        "w_in": f("w_in")[0], "ln_g": f("gmlp_ln_g")[0][None, :], "ln_b": f("gmlp_ln_b")[0][None, :],
        "wsT": np.ascontiguousarray(f("w_spatial")[0].transpose(2, 0, 1)),
        "bs_col": np.ascontiguousarray(f("b_spatial")[0].T),
        "lamv": np.concatenate([f("lambda_q1")[0], f("lambda_q2")[0], f("lambda_k1")[0], f("lambda_k2")[0]])[None, :],
        "subln_g": f("subln_g")[0][None, :], "w_out": f("w_out")[0], "w_ff1": f("w_ff1")[0], "w_ff2": f("w_ff2")[0],
        "g_final": f("g_final")[None, :], "ident": ident, "rmat": rmat, "cst": cst,
    }
    in_maps = []
    for core in range(n_cores):
        b, half = divmod(core, n_cores_per_seq)
        own = slice(half * S_OWN, (half + 1) * S_OWN)
        order = np.concatenate([np.arange(own.start, own.stop),
                                np.arange(0, own.start), np.arange(own.stop, S)])
        m = dict(shared)
        m["x"] = np.ascontiguousarray(x[b][order])
        m["pos"] = np.ascontiguousarray(pos[b][order][None, :])
        m["cT"] = col(c[b])
        in_maps.append(m)
    res = run_bass_kernel_spmd(nc, in_maps, core_ids=list(range(n_cores)))
    out = np.empty((B, S, D), dtype=np.float32)
    for core in range(n_cores):
        b, half = divmod(core, n_cores_per_seq)
        out[b, half * S_OWN:(half + 1) * S_OWN] = res.results[core]["out"]
    return out


def kernel(**inputs):
    return _run(inputs, n_cores_per_seq=2)
```

```python
import numpy as np
from contextlib import ExitStack

import concourse.bass as bass
import concourse.mybir as mybir
from concourse.bass_utils import run_bass_kernel_spmd

F32 = mybir.dt.float32
BF16 = mybir.dt.bfloat16
I32 = mybir.dt.int32
AF = mybir.ActivationFunctionType
ALU = mybir.AluOpType
AX = mybir.AxisListType
PI = float(np.pi)

D = 1024
KC = 8
H = 8
DFF = 4096
NCOL = 7168
EPS = 1e-6
VW = 130
LAMBDA_INIT = 0.2


class Res:
    __slots__ = ("name", "last_write", "reads", "excl")

    def __init__(self, name, excl=False):
        self.name = name
        self.last_write = None
        self.reads = []
        self.excl = excl


class Op:
    __slots__ = ("eng", "fn", "deps", "dma", "semkey", "marked", "ev")

    def __init__(self, eng, fn, dma, semkey):
        self.eng = eng
        self.fn = fn
        self.deps = []
        self.dma = dma
        self.semkey = semkey
        self.marked = False
        self.ev = None


class Prog:
    ENGS = ("pe", "act", "dve", "pool", "sp")

    def __init__(self, nc, semstack):
        self.nc = nc
        self._semstack = semstack
        self.all_res = []
        self.sems = {}
        self.semcount = {}
        self.known = {e: {} for e in self.ENGS}
        self.ops = []
        self.nres = 0
        self.disabled = False

    def res(self, name=None, excl=False):
        self.nres += 1
        r = Res(name or f"r{self.nres}", excl)
        self.all_res.append(r)
        return r

    def _sem(self, key):
        if key not in self.sems:
            self.sems[key] = self._semstack.enter_context(self.nc.semaphore(f"s_{key}"))
            self.semcount[key] = 0
        return self.sems[key]

    def add(self, eng, fn, reads=(), writes=(), dma=False, semkey=None):
        if self.disabled:
            return None
        op = Op(eng, fn, dma, semkey)
        deps = []
        for r in reads:
            if r.last_write is not None:
                deps.append(r.last_write)
            if r.excl:
                deps.extend(o for o in r.reads if o.eng != eng)
        for w in writes:
            if w.last_write is not None:
                deps.append(w.last_write)
            deps.extend(w.reads)
        for r in reads:
            r.reads.append(op)
        for w in writes:
            w.last_write = op
            w.reads = []
        seen = set()
        for d in deps:
            if id(d) in seen or d is op:
                continue
            seen.add(id(d))
            if d.eng == "pe" and eng == "pe" and not d.dma and not dma:
                continue
            op.deps.append(d)
            d.marked = True
        if dma:
            op.marked = True
        self.ops.append(op)
        return op

    def emit(self, block_name):
        nc = self.nc
        last = {}
        for op in self.ops:
            if not op.dma:
                last[op.eng] = op
        for op in last.values():
            op.marked = True
        for op in self.ops:
            if op.marked and op.ev is None:
                if op.dma:
                    key = "d_" + op.semkey
                    self._sem(key)
                    self.semcount[key] += 16
                else:
                    key = "e_" + op.eng
                    self._sem(key)
                    self.semcount[key] += 1
                op.ev = (key, self.semcount[key])
        per = {e: [] for e in self.ENGS}
        for op in self.ops:
            per[op.eng].append(op)
        prog = self

        def run(engname, e):
            known = prog.known[engname]
            for op in per[engname]:
                for d in op.deps:
                    key, val = d.ev
                    if known.get(key, 0) >= val:
                        continue
                    e.wait_ge(prog.sems[key], val)
                    known[key] = val
                ins = op.fn(e)
                if op.marked:
                    key, val = op.ev
                    ins.then_inc(prog.sems[key], 16 if op.dma else 1)
            for key, val in prog.semcount.items():
                if val > 0 and known.get(key, 0) < val:
                    e.wait_ge(prog.sems[key], val)
                    known[key] = val

        with nc.Block(block_name) as block:
            @block.tensor
            def _(e):
                run("pe", e)

            @block.scalar
            def _(e):
                run("act", e)

            @block.vector
            def _(e):
                run("dve", e)

            @block.gpsimd
            def _(e):
                run("pool", e)

            @block.sync
            def _(e):
                run("sp", e)
        self.ops = []
        for r in self.all_res:
            r.last_write = None
            r.reads = []


class Ring:
    def __init__(self, items):
        self.items = items
        self.i = -1

    def next(self):
        self.i = (self.i + 1) % len(self.items)
        return self.items[self.i]

    def cur(self):
        return self.items[self.i]


import os
_MAXPH = int(os.environ.get("KPH", "9"))
_KSUB = int(os.environ.get("KSUB", "0"))


def build(S_OWN, S_SEQ):
    NT_OWN = S_OWN // 128
    NT_SEQ = S_SEQ // 128
    NB_OWN = S_OWN // 512
    NB_SEQ = S_SEQ // 512
    NCH = NT_SEQ
    NPAIR = NCH // 2
    NQB = S_OWN // 256

    nc = bass.Bass("TRN2", target_bir_lowering=False)

    def din(name, shape, dt=F32):
        return nc.dram_tensor(name, list(shape), dt, kind="ExternalInput").ap()

    def dscr(name, shape, dt):
        return nc.dram_tensor(name, list(shape), dt, kind="Internal").ap()

    x_d = din("x", [S_SEQ, D])
    pos_d = din("pos", [1, S_SEQ], I32)
    cT_d = din("cT", [128, KC])
    wada_d = din("w_ada", [D, 6 * D])
    bada_d = din("b_ada", [1, 6 * D])
    badac_d = din("b_ada_col", [128, 48])
    g1c_d = din("g1_col", [128, KC])
    g2c_d = din("g2_col", [128, KC])
    win_d = din("w_in", [D, NCOL])
    lng_d = din("ln_g", [1, D])
    lnb_d = din("ln_b", [1, D])
    wsT_d = din("wsT", [128, 8, 128])
    bsc_d = din("bs_col", [128, 8])
    lamv_d = din("lamv", [1, 256])
    subg_d = din("subln_g", [1, 128])
    wout_d = din("w_out", [D, D])
    wff1_d = din("w_ff1", [D, DFF])
    wff2_d = din("w_ff2", [DFF, D])
    gfin_d = din("g_final", [1, D])
    ident_d = din("ident", [128, 128])
    rmat_d = din("rmat", [128, 128])
    cst_d = din("cst", [128, 4])
    out_d = nc.dram_tensor("out", [S_OWN, D], F32, kind="ExternalOutput").ap()

    TAB_d = dscr("tab", [2, 128, S_SEQ], F32)
    KT_d = dscr("ktd", [H, 128, S_SEQ], BF16)
    QT_d = dscr("qtd", [H, 128, S_OWN], BF16)
    V_d = dscr("vd", [H, 128, NCH, VW], BF16)
    GA_d = dscr("gad", [S_OWN, D], BF16)
    GB_d = dscr("gbd", [S_OWN, D], BF16)
    BB_d = dscr("bbd", [S_OWN, D], BF16)
    X1_d = dscr("x1d", [S_OWN, D], F32)
    H2T_d = dscr("h2td", [128, KC, S_OWN], BF16)
    GT_d = dscr("gtd", [2, D], F32)

    with ExitStack() as semstack, ExitStack() as glob:
        P = Prog(nc, semstack)

        def T(es, name, shape, dt):
            return es.enter_context(nc.sbuf_tensor("sb_" + name, list(shape), dt)), P.res(name)

        def PS(es, name, shape, dt=F32):
            return es.enter_context(nc.psum_tensor("pp_" + name, list(shape), dt)), P.res(name, excl=True)

        def dma(eng, out, in_, reads=(), writes=(), key=None):
            P.add(eng, lambda e: e.dma_start(out=out, in_=in_), reads=reads, writes=writes,
                  dma=True, semkey=key)

        ident, r_ident = T(glob, "ident", [128, 128], BF16)
        modcol, r_modcol = T(glob, "modcol", [128, 48], F32)
        A1, r_A1 = T(glob, "A1", [128, KC], F32)
        A2, r_A2 = T(glob, "A2", [128, KC], F32)
        lamc, r_lamc = T(glob, "lamc", [128, 1], F32)
        cst, r_cst = T(glob, "cst", [128, 4], F32)

        with ExitStack() as es:
            cTt, r_cTt = T(es, "cTt", [128, KC], F32)
            gt1row, r_gt1 = T(es, "gt1row", [128, D], F32)
            gt2row, r_gt2 = T(es, "gt2row", [128, D], F32)
            cact2, r_cact2 = T(es, "cact2", [128, KC, 2], F32)
            CB, r_CB = T(es, "CB", [128, KC, 128], F32)
            wad = [T(es, f"wad{i}", [128, KC, 1024], F32) for i in range(2)]
            badac, r_badac = T(es, "badac", [128, 48], F32)
            g1c, r_g1c = T(es, "g1c", [128, KC], F32)
            g2c, r_g2c = T(es, "g2c", [128, KC], F32)
            lamv, r_lamv = T(es, "lamv", [128, 256], F32)
            lprod, r_lprod = T(es, "lprod", [128, 128], F32)
            ls12, r_ls12 = T(es, "ls12", [128, 2], F32)
            posi, r_posi = T(es, "posi", [128, S_SEQ], I32)
            CW = min(2048, S_SEQ)
            posf, r_posf = T(es, "posf", [128, CW], F32)
            ang, r_ang = T(es, "ang", [128, CW], F32)
            kf, r_kf = T(es, "kf", [128, CW], F32)
            ki, r_ki = T(es, "ki", [128, CW], I32)
            tsin = [T(es, f"tsin{i}", [128, CW], F32) for i in range(2)]
            tcos = [T(es, f"tcos{i}", [128, CW], F32) for i in range(2)]
            sgnhp, r_sgnhp = T(es, "sgnhp", [128, 2], F32)
            ps_col, r_pscol = PS(es, "ps_col", [128, 512])
            ps_row = [PS(es, f"ps_row{i}", [128, 512]) for i in range(2)]

            dma("sp", cTt[:], cT_d, writes=[r_cTt], key="cTt")
            dma("sp", cst[:], cst_d, writes=[r_cst], key="cst")
            dma("sp", badac[:], badac_d, writes=[r_badac], key="badac")
            dma("sp", g1c[:], g1c_d, writes=[r_g1c], key="g1c")
            dma("sp", g2c[:], g2c_d, writes=[r_g2c], key="g2c")
            dma("sp", lamv[:], lamv_d.partition_broadcast(128), writes=[r_lamv], key="lamv")
            dma("sp", posi[:], pos_d.partition_broadcast(128), writes=[r_posi], key="posi")
            dma("pool", ident[:], ident_d, writes=[r_ident], key="ident")
            dma("sp", gt1row[:], bada_d[:, 2 * D:3 * D].partition_broadcast(128), writes=[r_gt1], key="gt1")
            dma("sp", gt2row[:], bada_d[:, 5 * D:6 * D].partition_broadcast(128), writes=[r_gt2], key="gt2")

            for c2 in range(2):
                P.add("act", lambda e, c2=c2: e.activation(out=cact2[:, :, c2], in_=cTt[:], func=AF.Silu),
                      reads=[r_cTt], writes=[r_cact2])
            P.add("dve", lambda e: e.tensor_copy(out=CB[:], in_=cact2[:, :, 0:1].to_broadcast([128, KC, 128])),
                  reads=[r_cact2], writes=[r_CB])

            wada_v = wada_d.rearrange("(kc p) n -> p kc n", p=128)
            for g in range(6):
                wt, r_wt = wad[g % 2]
                dma("sp", wt[:], wada_v[:, :, g * D:(g + 1) * D], writes=[r_wt], key=f"wad{g % 2}")
                if g in (0, 1, 3, 4):
                    for jj in range(8):
                        j = g * 8 + jj
                        for kc in range(KC):
                            P.add("pe", lambda e, wt=wt, jj=jj, j=j, kc=kc: e.matmul(
                                ps_col[:, 2 * j:2 * j + 2], lhsT=wt[:, kc, jj * 128:(jj + 1) * 128],
                                rhs=cact2[:, kc, :], start=(kc == 0), stop=(kc == KC - 1)),
                                reads=[r_wt, r_cact2], writes=[r_pscol])
                else:
                    grow, r_grow = (gt1row, r_gt1) if g == 2 else (gt2row, r_gt2)
                    for half in range(2):
                        pr, r_pr = ps_row[half]
                        for kc in range(KC):
                            P.add("pe", lambda e, wt=wt, pr=pr, half=half, kc=kc: e.matmul(
                                pr[:, :], lhsT=CB[:, kc, :], rhs=wt[:, kc, half * 512:(half + 1) * 512],
                                start=(kc == 0), stop=(kc == KC - 1)),
                                reads=[r_wt, r_CB], writes=[r_pr])
                        P.add("dve", lambda e, grow=grow, pr=pr, half=half: e.tensor_tensor(
                            out=grow[:, half * 512:(half + 1) * 512], in0=pr[:, :],
                            in1=grow[:, half * 512:(half + 1) * 512], op=ALU.add),
                            reads=[r_pr, r_grow], writes=[r_grow])
            dma("sp", GT_d[0:1, :], gt1row[0:1, :], reads=[r_gt1], key="gt1")
            dma("sp", GT_d[1:2, :], gt2row[0:1, :], reads=[r_gt2], key="gt2")
            for (j0, j1) in ((0, 16), (24, 40)):
                P.add("dve", lambda e, j0=j0, j1=j1: e.tensor_tensor(
                    out=modcol[:, j0:j1], in0=ps_col[:, 2 * j0:2 * j1].rearrange("p (j t) -> p j t", t=2)[:, :, 0],
                    in1=badac[:, j0:j1], op=ALU.add), reads=[r_pscol, r_badac], writes=[r_modcol])
            P.add("dve", lambda e: e.scalar_tensor_tensor(out=A1[:], in0=modcol[:, 8:16], scalar=1.0, in1=g1c[:],
                                                           op0=ALU.add, op1=ALU.mult),
                  reads=[r_modcol, r_g1c], writes=[r_A1])
            P.add("dve", lambda e: e.scalar_tensor_tensor(out=A2[:], in0=modcol[:, 32:40], scalar=1.0, in1=g2c[:],
                                                           op0=ALU.add, op1=ALU.mult),
                  reads=[r_modcol, r_g2c], writes=[r_A2])
            P.add("dve", lambda e: e.tensor_tensor(out=lprod[:], in0=lamv[:, 0:128], in1=lamv[:, 128:256], op=ALU.mult),
                  reads=[r_lamv], writes=[r_lprod])
            P.add("dve", lambda e: e.tensor_reduce(out=ls12[:], in_=lprod[:].rearrange("p (a b) -> p a b", a=2),
                                                   axis=AX.X, op=ALU.add), reads=[r_lprod], writes=[r_ls12])
            P.add("act", lambda e: e.activation(out=ls12[:], in_=ls12[:], func=AF.Exp), reads=[r_ls12], writes=[r_ls12])
            P.add("dve", lambda e: e.tensor_tensor(out=lamc[:], in0=ls12[:, 0:1], in1=ls12[:, 1:2], op=ALU.subtract),
                  reads=[r_ls12], writes=[r_lamc])
            P.add("dve", lambda e: e.tensor_scalar(out=lamc[:], in0=lamc[:], scalar1=LAMBDA_INIT, scalar2=None, op0=ALU.add),
                  reads=[r_lamc], writes=[r_lamc])
            for ci in range(S_SEQ // CW):
                cs = slice(ci * CW, (ci + 1) * CW)
                ts_, r_ts = tsin[ci % 2]
                tc_, r_tc = tcos[ci % 2]
                P.add("dve", lambda e, cs=cs: e.tensor_copy(out=posf[:], in_=posi[:, cs]), reads=[r_posi], writes=[r_posf])
                P.add("dve", lambda e: e.tensor_scalar(out=ang[:], in0=posf[:], scalar1=cst[:, 0:1], scalar2=None, op0=ALU.mult),
                      reads=[r_posf, r_cst], writes=[r_ang])
                P.add("dve", lambda e: e.tensor_scalar(out=kf[:], in0=ang[:], scalar1=1.0 / (2 * PI), scalar2=None, op0=ALU.mult),
                      reads=[r_ang], writes=[r_kf])
                P.add("dve", lambda e: e.tensor_copy(out=ki[:], in_=kf[:]), reads=[r_kf], writes=[r_ki])
                P.add("dve", lambda e: e.tensor_copy(out=kf[:], in_=ki[:]), reads=[r_ki], writes=[r_kf])
                P.add("dve", lambda e: e.scalar_tensor_tensor(out=kf[:], in0=kf[:], scalar=-2 * PI, in1=ang[:], op0=ALU.mult, op1=ALU.add),
                      reads=[r_kf, r_ang], writes=[r_kf])
                P.add("act", lambda e, ts_=ts_: e.activation(out=ts_[:], in_=kf[:], func=AF.Sin, scale=cst[:, 1:2]),
                      reads=[r_kf, r_cst], writes=[r_ts])
                P.add("dve", lambda e: e.tensor_scalar(out=kf[:], in0=ang[:], scalar1=1.0 / (2 * PI), scalar2=0.25, op0=ALU.mult, op1=ALU.add),
                      reads=[r_ang], writes=[r_kf])
                P.add("dve", lambda e: e.tensor_copy(out=ki[:], in_=kf[:]), reads=[r_kf], writes=[r_ki])
                P.add("dve", lambda e: e.tensor_copy(out=kf[:], in_=ki[:]), reads=[r_ki], writes=[r_kf])
                P.add("dve", lambda e: e.scalar_tensor_tensor(out=kf[:], in0=kf[:], scalar=-2 * PI, in1=ang[:], op0=ALU.mult, op1=ALU.add),
                      reads=[r_kf, r_ang], writes=[r_kf])
                P.add("act", lambda e, tc_=tc_: e.activation(out=tc_[:], in_=kf[:], func=AF.Sin, bias=cst[:, 2:3]),
                      reads=[r_kf, r_cst], writes=[r_tc])
                dma("sp", TAB_d[0, :, cs], tc_[:], reads=[r_tc], key=f"tcos{ci % 2}")
                dma("sp", TAB_d[1, :, cs], ts_[:], reads=[r_ts], key=f"tsin{ci % 2}")
            P.emit("phase0")

        with ExitStack() as es:
            P.disabled = (1 > _MAXPH)
            win, r_win = T(es, "win", [128, KC, NCOL], BF16)
            wsT, r_wsT = T(es, "wsT", [128, 8, 128], BF16)
            rmat, r_rmat = T(es, "rmat", [128, 128], BF16)
            lngr, r_lngr = T(es, "lngr", [128, D], F32)
            lnbr, r_lnbr = T(es, "lnbr", [128, D], F32)
            bsc, r_bsc = T(es, "bsc", [128, 8], F32)
            xt = [T(es, f"xt{i}", [128, D], F32) for i in range(2)]
            junk, r_junk = T(es, "junk", [128, D], BF16)
            st = [T(es, f"st{i}", [128, 8], F32) for i in range(2)]
            xn = [T(es, f"xn{i}", [128, D], BF16) for i in range(2)]
            hT = [T(es, f"hT{i}", [128, KC, 512], BF16) for i in range(2)]
            cosb = [T(es, f"cosb{i}", [128, 512], F32) for i in range(2)]
            sinb = [T(es, f"sinb{i}", [128, 512], F32) for i in range(2)]
            kraw = Ring([T(es, f"kraw{i}", [128, 512], BF16) for i in range(2)])
            t1r = Ring([T(es, f"t1r{i}", [128, 512], F32) for i in range(2)])
            t2r = Ring([T(es, f"t2r{i}", [128, 512], F32) for i in range(2)])
            kfin = Ring([T(es, f"kfin{i}", [128, 512], BF16) for i in range(2)])
            vblk = [T(es, f"vblk{i}", [128, H, VW], BF16) for i in range(2)]
            gu = [T(es, f"gu{i}", [128, D], BF16) for i in range(2)]
            gv = [T(es, f"gv{i}", [128, D], F32) for i in range(2)]
            sga = [T(es, f"sga{i}", [128, D], BF16) for i in range(2)]
            lst = [T(es, f"lst{i}", [128, 8], F32) for i in range(2)]
            vln, r_vln = T(es, "vln", [128, D], BF16)
            tmpf, r_tmpf = T(es, "tmpf", [128, D], F32)
            gbt = [T(es, f"gbt{i}", [128, D], BF16) for i in range(2)]
            gat = [T(es, f"gat{i}", [128, D], BF16) for i in range(2)]
            ps_tr, r_pstr = PS(es, "ps_tr", [128, KC, 128], BF16)
            ps_tm = Ring([PS(es, f"ps_tm{i}", [128, 512]) for i in range(2)])
            ps_sv, r_pssv = PS(es, "ps_sv", [128, 8, 128])
            ps_k = Ring([PS(es, f"ps_k{i}", [128, 512]) for i in range(2)])
            ps_rot, r_psrot = PS(es, "ps_rot", [128, 512])

            for (c0, c1) in ((4096, 5120), (3072, 4096), (2048, 3072), (0, 2048), (5120, 7168)):
                for kc in range(KC):
                    dma("pool", win[:, kc, c0:c1], win_d[kc * 128:(kc + 1) * 128, c0:c1], writes=[r_win], key="win")
            dma("pool", wsT[:], wsT_d, writes=[r_wsT], key="wsT")
            dma("pool", rmat[:], rmat_d, writes=[r_rmat], key="rmat")
            dma("sp", lngr[:], lng_d.partition_broadcast(128), writes=[r_lngr], key="lngr")
            dma("sp", lnbr[:], lnb_d.partition_broadcast(128), writes=[r_lnbr], key="lnbr")
            dma("sp", bsc[:], bsc_d, writes=[r_bsc], key="bsc")
            for (vb, r_vb) in vblk:
                P.add("pool", lambda e, vb=vb: e.memset(vb[:, :, 128:VW], 1.0), writes=[r_vb])

            def is_own(t):
                return (t // 4) < NB_OWN and not (_KSUB & 1)

            def stageA(t):
                blk, i = divmod(t, 4)
                hTt, r_hT = hT[blk % 2]
                if i == 0:
                    cb, r_cb = cosb[blk % 2]
                    sb_, r_sb = sinb[blk % 2]
                    bs = slice(blk * 512, (blk + 1) * 512)
                    dma("sp", cb[:], TAB_d[0, :, bs], writes=[r_cb], key=f"cosb{blk % 2}")
                    dma("sp", sb_[:], TAB_d[1, :, bs], writes=[r_sb], key=f"sinb{blk % 2}")
                xtt, r_xt = xt[t % 2]
                stt, r_st = st[t % 2]
                xnt, r_xn = xn[t % 2]
                dma("sp", xtt[:], x_d[t * 128:(t + 1) * 128, :], writes=[r_xt], key=f"xt{t % 2}")
                P.add("act", lambda e: e.activation(out=junk[:], in_=xtt[:], func=AF.Square, accum_out=stt[:, 0:1]),
                      reads=[r_xt], writes=[r_junk, r_st])
                P.add("dve", lambda e: e.tensor_scalar(out=stt[:, 1:2], in0=stt[:, 0:1], scalar1=1.0 / D, scalar2=EPS,
                                                       op0=ALU.mult, op1=ALU.add), reads=[r_st], writes=[r_st])
                P.add("act", lambda e: e.activation(out=stt[:, 2:3], in_=stt[:, 1:2], func=AF.Sqrt), reads=[r_st], writes=[r_st])
                P.add("dve", lambda e: e.reciprocal(out=stt[:, 3:4], in_=stt[:, 2:3]), reads=[r_st], writes=[r_st])
                P.add("pool", lambda e: e.tensor_scalar(out=xnt[:], in0=xtt[:], scalar1=stt[:, 3:4], scalar2=None, op0=ALU.mult),
                      reads=[r_xt, r_st], writes=[r_xn])
                for kc in range(KC):
                    P.add("pe", lambda e, kc=kc: e.transpose(out=ps_tr[:, kc, :], in_=xnt[:, kc * 128:(kc + 1) * 128], identity=ident[:]),
                          reads=[r_xn, r_ident], writes=[r_pstr])
                for kc in range(KC):
                    P.add("dve", lambda e, kc=kc: e.tensor_scalar(
                        out=hTt[:, kc, i * 128:(i + 1) * 128], in0=ps_tr[:, kc, :], scalar1=A1[:, kc:kc + 1], scalar2=modcol[:, kc:kc + 1],
                        op0=ALU.mult, op1=ALU.add), reads=[r_pstr, r_A1, r_modcol], writes=[r_hT])

            def tm_group(t, cg, consumer):
                blk, i = divmod(t, 4)
                hTt, r_hT = hT[blk % 2]
                pt, r_pt = ps_tm.next()
                for kc in range(KC):
                    P.add("pe", lambda e, kc=kc: e.matmul(
                        pt[:, :], lhsT=hTt[:, kc, i * 128:(i + 1) * 128], rhs=win[:, kc, cg * 512:(cg + 1) * 512],
                        start=(kc == 0), stop=(kc == KC - 1)), reads=[r_hT, r_win], writes=[r_pt])
                consumer(pt, r_pt)

            def stageB(t):
                vb, r_vb = vblk[t % 2]
                if not (_KSUB & 4):
                    for hf in range(2):
                        def cons_v(pt, r_pt, hf=hf):
                            P.add("dve", lambda e: e.tensor_copy(out=vb[:, hf * 4:(hf + 1) * 4, 0:128],
                                                                 in_=pt[:, :].rearrange("p (h e) -> p h e", h=4)),
                                  reads=[r_pt], writes=[r_vb])
                        tm_group(t, 8 + hf, cons_v)
                    dma("sp", V_d[:, :, t, :].rearrange("h p e -> p h e"), vb[:], reads=[r_vb], key=f"vblk{t % 2}")
                if not is_own(t):
                    return
                gu_, r_gu = gu[t % 2]
                gv_, r_gv = gv[t % 2]
                sga_, r_sga = sga[t % 2]
                lst_, r_lst = lst[t % 2]
                gbt_t, r_gbt = gbt[t % 2]
                for hf in range(2):
                    def cons_u(pt, r_pt, hf=hf):
                        P.add("act", lambda e: e.activation(out=gu_[:, hf * 512:(hf + 1) * 512], in_=pt[:, :], func=AF.Gelu_apprx_tanh),
                              reads=[r_pt], writes=[r_gu])
                    tm_group(t, 0 + hf, cons_u)
                for hf in range(2):
                    def cons_va(pt, r_pt, hf=hf):
                        P.add("act", lambda e: e.activation(out=gv_[:, hf * 512:(hf + 1) * 512], in_=pt[:, :], func=AF.Gelu_apprx_tanh,
                                                            accum_out=lst_[:, hf:hf + 1]), reads=[r_pt], writes=[r_gv, r_lst])
                    tm_group(t, 2 + hf, cons_va)
                for hf in range(2):
                    def cons_ga(pt, r_pt, hf=hf):
                        P.add("act", lambda e: e.activation(out=sga_[:, hf * 512:(hf + 1) * 512], in_=pt[:, :], func=AF.Sigmoid),
                              reads=[r_pt], writes=[r_sga])
                    tm_group(t, 10 + hf, cons_ga)
                for hf in range(2):
                    def cons_gb(pt, r_pt, hf=hf):
                        P.add("act", lambda e: e.activation(out=gbt_t[:, hf * 512:(hf + 1) * 512], in_=pt[:, :], func=AF.Sigmoid),
                              reads=[r_pt], writes=[r_gbt])
                    tm_group(t, 12 + hf, cons_gb)
                dma("sp", GB_d[t * 128:(t + 1) * 128, :], gbt_t[:], reads=[r_gbt], key=f"gbt{t % 2}")

            def stageC(t):
                if not is_own(t):
                    return
                gu_, r_gu = gu[t % 2]
                gv_, r_gv = gv[t % 2]
                sga_, r_sga = sga[t % 2]
                lst_, r_lst = lst[t % 2]
                gat_t, r_gat = gat[t % 2]
                P.add("act", lambda e: e.activation(out=junk[:], in_=gv_[:], func=AF.Square, accum_out=lst_[:, 2:3]),
                      reads=[r_gv], writes=[r_junk, r_lst])
                P.add("dve", lambda e: e.tensor_tensor(out=lst_[:, 3:4], in0=lst_[:, 0:1], in1=lst_[:, 1:2], op=ALU.add), reads=[r_lst], writes=[r_lst])
                P.add("dve", lambda e: e.tensor_scalar(out=lst_[:, 3:4], in0=lst_[:, 3:4], scalar1=-1.0 / D, scalar2=None, op0=ALU.mult),
                      reads=[r_lst], writes=[r_lst])
                P.add("dve", lambda e: e.tensor_tensor(out=lst_[:, 4:5], in0=lst_[:, 3:4], in1=lst_[:, 3:4], op=ALU.mult), reads=[r_lst], writes=[r_lst])
                P.add("dve", lambda e: e.scalar_tensor_tensor(out=lst_[:, 5:6], in0=lst_[:, 2:3], scalar=1.0 / D, in1=lst_[:, 4:5],
                                                               op0=ALU.mult, op1=ALU.subtract), reads=[r_lst], writes=[r_lst])
                P.add("dve", lambda e: e.tensor_scalar(out=lst_[:, 5:6], in0=lst_[:, 5:6], scalar1=EPS, scalar2=None, op0=ALU.add),
                      reads=[r_lst], writes=[r_lst])
                P.add("act", lambda e: e.activation(out=lst_[:, 6:7], in_=lst_[:, 5:6], func=AF.Sqrt), reads=[r_lst], writes=[r_lst])
                P.add("dve", lambda e: e.reciprocal(out=lst_[:, 7:8], in_=lst_[:, 6:7]), reads=[r_lst], writes=[r_lst])
                P.add("dve", lambda e: e.tensor_scalar(out=gv_[:], in0=gv_[:], scalar1=lst_[:, 3:4], scalar2=lst_[:, 7:8], op0=ALU.add, op1=ALU.mult),
                      reads=[r_gv, r_lst], writes=[r_gv])
                P.add("pool", lambda e: e.tensor_tensor(out=gv_[:], in0=gv_[:], in1=lngr[:], op=ALU.mult), reads=[r_gv, r_lngr], writes=[r_gv])
                P.add("pool", lambda e: e.tensor_tensor(out=vln[:], in0=gv_[:], in1=lnbr[:], op=ALU.add), reads=[r_gv, r_lnbr], writes=[r_vln])
                for g in range(8):
                    P.add("pe", lambda e, g=g: e.matmul(ps_sv[:, g, :], lhsT=wsT[:, g, :], rhs=vln[:, g * 128:(g + 1) * 128], start=True, stop=True),
                          reads=[r_wsT, r_vln], writes=[r_pssv])
                for g in range(8):
                    P.add("dve", lambda e, g=g: e.scalar_tensor_tensor(out=tmpf[:, g * 128:(g + 1) * 128], in0=ps_sv[:, g, :], scalar=bsc[:, g:g + 1],
                                                                        in1=gu_[:, g * 128:(g + 1) * 128], op0=ALU.add, op1=ALU.mult),
                          reads=[r_pssv, r_bsc, r_gu], writes=[r_tmpf])
                P.add("pool", lambda e: e.tensor_tensor(out=gat_t[:], in0=tmpf[:], in1=sga_[:], op=ALU.mult),
                      reads=[r_tmpf, r_sga], writes=[r_gat])
                dma("sp", GA_d[t * 128:(t + 1) * 128, :], gat_t[:], reads=[r_gat], key=f"gat{t % 2}")

            def stageR(blk):
                if _KSUB & 2:
                    return
                hTt, r_hT = hT[blk % 2]
                cb, r_cb = cosb[blk % 2]
                sb_, r_sb = sinb[blk % 2]
                bs = slice(blk * 512, (blk + 1) * 512)
                jobs = [("k", 3072, h) for h in range(H)]
                if blk < NB_OWN:
                    jobs += [("q", 2048, h) for h in range(H)]

                def mm(job):
                    nm, cbase, h = job
                    pk, r_pk = ps_k.next()
                    for kc in range(KC):
                        P.add("pe", lambda e, kc=kc: e.matmul(
                            pk[:, :], lhsT=win[:, kc, cbase + h * 128:cbase + (h + 1) * 128], rhs=hTt[:, kc, :],
                            start=(kc == 0), stop=(kc == KC - 1)), reads=[r_win, r_hT], writes=[r_pk])
                    return pk, r_pk

                pend = mm(jobs[0])
                for ji, job in enumerate(jobs):
                    nm, cbase, h = job
                    pk, r_pk = pend
                    if ji + 1 < len(jobs):
                        pend = mm(jobs[ji + 1])
                    kr, r_kr = kraw.next()
                    t1, r_t1 = t1r.next()
                    t2, r_t2 = t2r.next()
                    kf_t, r_kf_t = kfin.next()
                    kfi = kfin.i
                    P.add("act", lambda e, kr=kr, pk=pk: e.activation(out=kr[:], in_=pk[:, :], func=AF.Copy), reads=[r_pk], writes=[r_kr])
                    P.add("pe", lambda e, kr=kr: e.matmul(ps_rot[:, :], lhsT=rmat[:], rhs=kr[:], start=True, stop=True),
                          reads=[r_rmat, r_kr], writes=[r_psrot])
                    P.add("dve", lambda e, t1=t1, pk=pk: e.tensor_tensor(out=t1[:], in0=pk[:, :], in1=cb[:], op=ALU.mult),
                          reads=[r_pk, r_cb], writes=[r_t1])
                    P.add("dve", lambda e, t2=t2: e.tensor_tensor(out=t2[:], in0=ps_rot[:, :], in1=sb_[:], op=ALU.mult),
                          reads=[r_psrot, r_sb], writes=[r_t2])
                    P.add("pool", lambda e, kf_t=kf_t, t1=t1, t2=t2: e.tensor_tensor(out=kf_t[:], in0=t1[:], in1=t2[:], op=ALU.add),
                          reads=[r_t1, r_t2], writes=[r_kf_t])
                    dst = KT_d if nm == "k" else QT_d
                    dma("sp", dst[h, :, bs], kf_t[:], reads=[r_kf_t], key=f"kfin{kfi}")

            for s in range(NT_SEQ + 2):
                if s < NT_SEQ:
                    stageA(s)
                if 0 <= s - 1 < NT_SEQ:
                    stageB(s - 1)
                    if (s - 1) % 4 == 3:
                        stageR((s - 1) // 4)
                if 0 <= s - 2 < NT_SEQ:
                    if not (_KSUB & 8):
                        stageC(s - 2)
            P.emit("phase1")

        with ExitStack() as es:
            P.disabled = (2 > _MAXPH)
            KT = [T(es, f"KT{i}", [128, S_SEQ], BF16) for i in range(2)]
            VA = [T(es, f"VA{i}", [128, NCH, VW], BF16) for i in range(2)]
            Q1 = [T(es, f"Q1p{i}", [128, S_OWN], BF16) for i in range(2)]
            Q2 = [T(es, f"Q2p{i}", [128, S_OWN], BF16) for i in range(2)]
            ET = Ring([T(es, f"ET{i}", [128, 2, 2, 256], BF16) for i in range(3)])
            obuf = [T(es, f"obuf{i}", [128, NT_OWN, 128], BF16) for i in range(2)]
            nst = Ring([T(es, f"nst{i}", [128, 4], F32) for i in range(2)])
            ntmp = Ring([T(es, f"ntmp{i}", [128, 128], F32) for i in range(2)])
            SP_ = Ring([PS(es, f"S{i}", [128, 2, 2, 256]) for i in range(2)])
            acc = [[PS(es, f"acc{m}{qt}", [128, 512]) for qt in range(2)] for m in range(2)]

            for s in range(2):
                P.add("pool", lambda e, s=s: e.memset(Q1[s][0][64:128, :], 0.0), writes=[Q1[s][1]])
                P.add("pool", lambda e, s=s: e.memset(Q2[s][0][0:64, :], 0.0), writes=[Q2[s][1]])

            def load_head(h):
                s = h % 2
                dma("sp", KT[s][0][:], KT_d[h], writes=[KT[s][1]], key=f"KT{s}")
                dma("sp", VA[s][0][:], V_d[h], writes=[VA[s][1]], key=f"VA{s}")
                dma("sp", Q1[s][0][0:64, :], QT_d[h, 0:64, :], writes=[Q1[s][1]], key=f"Q1{s}")
                dma("sp", Q2[s][0][64:128, :], QT_d[h, 64:128, :], writes=[Q2[s][1]], key=f"Q2{s}")

            steps = [(h, qb, j) for h in range(H) for qb in range(NQB) for j in range(NPAIR)]

            def issue_qk(step):
                h, qb, j = step
                s = h % 2
                S_, r_S = SP_.next()
                qs = slice(qb * 256, (qb + 1) * 256)
                for kk in range(2):
                    c = 2 * j + kk
                    P.add("pe", lambda e, S_=S_, kk=kk, c=c, s=s, qs=qs: e.matmul(
                        S_[:, kk, 0, :], lhsT=KT[s][0][:, c * 128:(c + 1) * 128], rhs=Q1[s][0][:, qs], start=True, stop=True),
                        reads=[KT[s][1], Q1[s][1]], writes=[r_S])
                    P.add("pe", lambda e, S_=S_, kk=kk, c=c, s=s, qs=qs: e.matmul(
                        S_[:, kk, 1, :], lhsT=KT[s][0][:, c * 128:(c + 1) * 128], rhs=Q2[s][0][:, qs], start=True, stop=True),
                        reads=[KT[s][1], Q2[s][1]], writes=[r_S])
                return S_, r_S

            load_head(0)
            pending = issue_qk(steps[0])
            for si, (h, qb, j) in enumerate(steps):
                s = h % 2
                if qb == 0 and j == 0 and h + 1 < H:
                    load_head(h + 1)
                S_, r_S = pending
                if si + 1 < len(steps):
                    pending = issue_qk(steps[si + 1])
                E_, r_E = ET.next()
                P.add("act", lambda e, E_=E_, S_=S_: e.activation(out=E_[:].rearrange("p a b c -> p (a b c)"),
                                                                  in_=S_[:].rearrange("p a b c -> p (a b c)"), func=AF.Exp, scale=0.125),
                      reads=[r_S], writes=[r_E])
                for kk in range(2):
                    c = 2 * j + kk
                    for m in range(2):
                        for qt in range(2):
                            a_, r_a = acc[m][qt]
                            P.add("pe", lambda e, a_=a_, E_=E_, kk=kk, m=m, qt=qt, c=c, s=s: e.matmul(
                                a_[:, 0:129], lhsT=E_[:, kk, m, qt * 128:(qt + 1) * 128], rhs=VA[s][0][:, c, 0:129],
                                start=(c == 0), stop=(c == NCH - 1)), reads=[r_E, VA[s][1]], writes=[r_a])
                if j == NPAIR - 1:
                    ob, r_ob = obuf[h % 2]
                    for qt in range(2):
                        a0, r_a0 = acc[0][qt]
                        a1, r_a1 = acc[1][qt]
                        ns, r_ns = nst.next()
                        nt, r_nt = ntmp.next()
                        P.add("dve", lambda e, ns=ns, a0=a0: e.reciprocal(out=ns[:, 0:1], in_=a0[:, 128:129]), reads=[r_a0], writes=[r_ns])
                        P.add("dve", lambda e, ns=ns, a1=a1: e.reciprocal(out=ns[:, 1:2], in_=a1[:, 128:129]), reads=[r_a1], writes=[r_ns])
                        P.add("dve", lambda e, ns=ns: e.tensor_tensor(out=ns[:, 2:3], in0=ns[:, 1:2], in1=lamc[:], op=ALU.mult),
                              reads=[r_ns, r_lamc], writes=[r_ns])
                        P.add("dve", lambda e, ns=ns, nt=nt, a1=a1: e.tensor_scalar(out=nt[:], in0=a1[:, 0:128], scalar1=ns[:, 2:3], scalar2=None, op0=ALU.mult),
                              reads=[r_a1, r_ns], writes=[r_nt])
                        P.add("dve", lambda e, ns=ns, nt=nt, a0=a0, ob=ob, qb=qb, qt=qt: e.scalar_tensor_tensor(
                            out=ob[:, qb * 2 + qt, :], in0=a0[:, 0:128], scalar=ns[:, 0:1], in1=nt[:], op0=ALU.mult, op1=ALU.subtract),
                            reads=[r_a0, r_ns, r_nt], writes=[r_ob])
                    if qb == NQB - 1:
                        TG = min(8, NT_OWN)
                        for t0 in range(0, NT_OWN, TG):
                            dma("sp", BB_d[t0 * 128:(t0 + TG) * 128, h * 128:(h + 1) * 128].rearrange("(t p) e -> p t e", p=128),
                                ob[:, t0:t0 + TG, :], reads=[r_ob], key=f"obuf{h % 2}")
            P.emit("phase2")

        with ExitStack() as es:
            P.disabled = (3 > _MAXPH)
            wout, r_wout = T(es, "wout", [128, KC, D], BF16)
            g08, r_g08 = T(es, "g08", [128, 128], F32)
            gt1row, r_gt1 = T(es, "gt1row3", [128, D], F32)
            dma("sp", gt1row[:], GT_d[0:1, :].partition_broadcast(128), writes=[r_gt1], key="gt1row3")
            gaT = [T(es, f"gaT{i}", [128, D], BF16) for i in range(2)]
            gbT = [T(es, f"gbT{i}", [128, D], BF16) for i in range(2)]
            bbT = [T(es, f"bbT{i}", [128, D], BF16) for i in range(2)]
            xa = [T(es, f"xa{i}", [128, D], F32) for i in range(3)]
            sq = [T(es, f"sq{i}", [128, D], F32) for i in range(2)]
            s8 = [T(es, f"s8{i}", [128, 32], F32) for i in range(3)]
            mg = [T(es, f"mg{i}", [128, D], BF16) for i in range(2)]
            mT = [T(es, f"mT{i}", [128, KC, 128], BF16) for i in range(2)]
            x1 = [T(es, f"x1{i}", [128, D], F32) for i in range(3)]
            junk, r_junk = T(es, "junk3", [128, D], BF16)
            xn2 = [T(es, f"xn2{i}", [128, D], BF16) for i in range(2)]
            h2 = [T(es, f"h2{i}", [128, KC, 128], BF16) for i in range(2)]
            ps_tr = [PS(es, f"ps_tr3{i}", [128, KC, 128], BF16) for i in range(2)]
            ps_tr2 = [PS(es, f"ps_tr3b{i}", [128, KC, 128], BF16) for i in range(2)]
            ps_o = [[PS(es, f"ps_o{i}{hf}", [128, 512]) for hf in range(2)] for i in range(2)]

            for kc in range(KC):
                dma("pool", wout[:, kc, :], wout_d[kc * 128:(kc + 1) * 128, :], writes=[r_wout], key="wout")
            dma("sp", g08[:], subg_d.partition_broadcast(128), writes=[r_g08], key="g08")
            P.add("dve", lambda e: e.tensor_scalar(out=g08[:], in0=g08[:], scalar1=1.0 - LAMBDA_INIT, scalar2=None, op0=ALU.mult),
                  reads=[r_g08], writes=[r_g08])

            def s3A(t):
                rows = slice(t * 128, (t + 1) * 128)
                ga_, r_ga = gaT[t % 2]
                gb_, r_gb = gbT[t % 2]
                bb_, r_bb = bbT[t % 2]
                xa_, r_xa = xa[t % 3]
                s8t, r_s8 = s8[t % 3]
                sq_, r_sq = sq[t % 2]
                mg_, r_mg = mg[t % 2]
                dma("sp", bb_[:], BB_d[rows, :], writes=[r_bb], key=f"bbT{t % 2}")
                dma("sp", gb_[:], GB_d[rows, :], writes=[r_gb], key=f"gbT{t % 2}")
                dma("sp", ga_[:], GA_d[rows, :], writes=[r_ga], key=f"gaT{t % 2}")
                dma("sp", xa_[:], x_d[rows, :], writes=[r_xa], key=f"xa{t % 3}")
                P.add("dve", lambda e: e.tensor_tensor(out=sq_[:], in0=bb_[:], in1=bb_[:], op=ALU.mult), reads=[r_bb], writes=[r_sq])
                P.add("dve", lambda e: e.tensor_reduce(out=s8t[:, 0:8], in_=sq_[:].rearrange("p (h e) -> p h e", h=8), axis=AX.X, op=ALU.add),
                      reads=[r_sq], writes=[r_s8])
                P.add("dve", lambda e: e.tensor_scalar(out=s8t[:, 8:16], in0=s8t[:, 0:8], scalar1=1.0 / 128, scalar2=EPS, op0=ALU.mult, op1=ALU.add),
                      reads=[r_s8], writes=[r_s8])
                P.add("act", lambda e: e.activation(out=s8t[:, 16:24], in_=s8t[:, 8:16], func=AF.Sqrt), reads=[r_s8], writes=[r_s8])
                P.add("dve", lambda e: e.reciprocal(out=s8t[:, 24:32], in_=s8t[:, 16:24]), reads=[r_s8], writes=[r_s8])
                for hh in range(8):
                    P.add("dve", lambda e, hh=hh: e.scalar_tensor_tensor(
                        out=sq_[:, hh * 128:(hh + 1) * 128], in0=bb_[:, hh * 128:(hh + 1) * 128], scalar=s8t[:, 24 + hh:25 + hh], in1=g08[:],
                        op0=ALU.mult, op1=ALU.mult), reads=[r_bb, r_s8, r_g08], writes=[r_sq])
                P.add("pool", lambda e: e.tensor_tensor(out=sq_[:], in0=sq_[:], in1=gb_[:], op=ALU.mult), reads=[r_sq, r_gb], writes=[r_sq])
                P.add("pool", lambda e: e.tensor_tensor(out=mg_[:], in0=sq_[:], in1=ga_[:], op=ALU.add), reads=[r_sq, r_ga], writes=[r_mg])

            def s3B(t):
                rows = slice(t * 128, (t + 1) * 128)
                mg_, r_mg = mg[t % 2]
                mT_, r_mT = mT[t % 2]
                ptr, r_ptr = ps_tr[t % 2]
                xa_, r_xa = xa[t % 3]
                s8t, r_s8 = s8[t % 3]
                x1t, r_x1 = x1[t % 3]
                xn2_, r_xn2 = xn2[t % 2]
                for kc in range(KC):
                    P.add("pe", lambda e, kc=kc: e.transpose(out=ptr[:, kc, :], in_=mg_[:, kc * 128:(kc + 1) * 128], identity=ident[:]),
                          reads=[r_mg, r_ident], writes=[r_ptr])
                P.add("act", lambda e: e.activation(out=mT_[:].rearrange("p a b -> p (a b)"), in_=ptr[:].rearrange("p a b -> p (a b)"), func=AF.Copy),
                      reads=[r_ptr], writes=[r_mT])
                for hf in range(2):
                    po, r_po = ps_o[t % 2][hf]
                    for kc in range(KC):
                        P.add("pe", lambda e, po=po, kc=kc, hf=hf: e.matmul(po[:, :], lhsT=mT_[:, kc, :], rhs=wout[:, kc, hf * 512:(hf + 1) * 512],
                                                                           start=(kc == 0), stop=(kc == KC - 1)), reads=[r_mT, r_wout], writes=[r_po])
                    P.add("dve", lambda e, po=po, hf=hf: e.tensor_tensor(out=x1t[:, hf * 512:(hf + 1) * 512], in0=po[:, :],
                                                                      in1=gt1row[:, hf * 512:(hf + 1) * 512], op=ALU.mult),
                          reads=[r_po, r_gt1], writes=[r_x1])
                P.add("pool", lambda e: e.tensor_tensor(out=x1t[:], in0=x1t[:], in1=xa_[:], op=ALU.add), reads=[r_x1, r_xa], writes=[r_x1])
                dma("sp", X1_d[rows, :], x1t[:], reads=[r_x1], key=f"x1{t % 3}")
                P.add("act", lambda e: e.activation(out=junk[:], in_=x1t[:], func=AF.Square, accum_out=s8t[:, 0:1]),
                      reads=[r_x1, r_s8], writes=[r_junk, r_s8])
                P.add("dve", lambda e: e.tensor_scalar(out=s8t[:, 1:2], in0=s8t[:, 0:1], scalar1=1.0 / D, scalar2=EPS, op0=ALU.mult, op1=ALU.add),
                      reads=[r_s8], writes=[r_s8])
                P.add("act", lambda e: e.activation(out=s8t[:, 2:3], in_=s8t[:, 1:2], func=AF.Sqrt), reads=[r_s8], writes=[r_s8])
                P.add("dve", lambda e: e.reciprocal(out=s8t[:, 3:4], in_=s8t[:, 2:3]), reads=[r_s8], writes=[r_s8])
                P.add("pool", lambda e: e.tensor_scalar(out=xn2_[:], in0=x1t[:], scalar1=s8t[:, 3:4], scalar2=None, op0=ALU.mult),
                      reads=[r_x1, r_s8], writes=[r_xn2])

            def s3C(t):
                rows = slice(t * 128, (t + 1) * 128)
                xn2_, r_xn2 = xn2[t % 2]
                ptr2, r_ptr2 = ps_tr2[t % 2]
                h2t, r_h2 = h2[t % 2]
                for kc in range(KC):
                    P.add("pe", lambda e, kc=kc: e.transpose(out=ptr2[:, kc, :], in_=xn2_[:, kc * 128:(kc + 1) * 128], identity=ident[:]),
                          reads=[r_xn2, r_ident], writes=[r_ptr2])
                for kc in range(KC):
                    P.add("dve", lambda e, kc=kc: e.tensor_scalar(out=h2t[:, kc, :], in0=ptr2[:, kc, :], scalar1=A2[:, kc:kc + 1],
                                                                 scalar2=modcol[:, 24 + kc:25 + kc], op0=ALU.mult, op1=ALU.add),
                          reads=[r_ptr2, r_A2, r_modcol], writes=[r_h2])
                dma("sp", H2T_d[:, :, rows], h2t[:], reads=[r_h2], key=f"h2{t % 2}")

            for s in range(NT_OWN + 2):
                if s < NT_OWN:
                    s3A(s)
                if 0 <= s - 1 < NT_OWN:
                    s3B(s - 1)
                if 0 <= s - 2 < NT_OWN:
                    s3C(s - 2)
            P.emit("phase3a")

        with ExitStack() as es:
            P.disabled = (4 > _MAXPH)
            wf1, r_wf1 = T(es, "wf1", [128, KC, DFF], BF16)
            wf2, r_wf2 = T(es, "wf2", [128, 32, D], BF16)
            gfr, r_gfr = T(es, "gfr", [128, D], F32)
            gt2row, r_gt2 = T(es, "gt2row3", [128, D], F32)
            dma("sp", gt2row[:], GT_d[1:2, :].partition_broadcast(128), writes=[r_gt2], key="gt2row3")
            LA = 3
            h2b = [T(es, f"h2b{i}", [128, KC, 256], BF16) for i in range(2)]
            sqr = Ring([T(es, f"sqr{i}", [128, 256], F32) for i in range(LA + 1)])
            aT = Ring([T(es, f"aT{i}", [128, 256], BF16) for i in range(LA + 2)])
            x1 = Ring([T(es, f"x1b{i}", [128, D], F32) for i in range(2)])
            x2 = Ring([T(es, f"x2{i}", [128, D], F32) for i in range(2)])
            junk, r_junk = T(es, "junk4", [128, D], BF16)
            s4 = Ring([T(es, f"s4{i}", [128, 4], F32) for i in range(2)])
            ps_f = Ring([PS(es, f"ps_f{i}", [128, 512]) for i in range(LA + 1)])
            acc = [[PS(es, f"fa{tl}{cg}", [128, 512]) for cg in range(2)] for tl in range(2)]

            for kc in range(KC):
                dma("pool", wf1[:, kc, :], wff1_d[kc * 128:(kc + 1) * 128, :], writes=[r_wf1], key="wf1")
            for j0 in range(0, 32, 4):
                dma("pool", wf2[:, j0:j0 + 4, :], wff2_d[j0 * 128:(j0 + 4) * 128, :].rearrange("(j p) n -> p j n", p=128), writes=[r_wf2], key="wf2")
            dma("sp", gfr[:], gfin_d.partition_broadcast(128), writes=[r_gfr], key="gfr")

            NB3 = S_OWN // 256
            fsteps = [(b3, j) for b3 in range(NB3) for j in range(32)]

            def load_h2(b3):
                hb, r_hb = h2b[b3 % 2]
                dma("sp", hb[:], H2T_d[:, :, b3 * 256:(b3 + 1) * 256], writes=[r_hb], key=f"h2b{b3 % 2}")

            def issue_f1(step):
                b3, j = step
                hb, r_hb = h2b[b3 % 2]
                pf, r_pf = ps_f.next()
                for kc in range(KC):
                    P.add("pe", lambda e, kc=kc: e.matmul(pf[:, 0:256], lhsT=wf1[:, kc, j * 128:(j + 1) * 128], rhs=hb[:, kc, :],
                                                         start=(kc == 0), stop=(kc == KC - 1)), reads=[r_wf1, r_hb], writes=[r_pf])
                return pf, r_pf

            load_h2(0)
            if NB3 > 1:
                load_h2(1)
            pend = [issue_f1(fsteps[k]) for k in range(min(LA, len(fsteps)))]
            for si, (b3, j) in enumerate(fsteps):
                pf, r_pf = pend.pop(0)
                if si + LA < len(fsteps):
                    pend.append(issue_f1(fsteps[si + LA]))
                sq_, r_sq = sqr.next()
                a_, r_a = aT.next()
                P.add("act", lambda e, sq_=sq_, pf=pf: e.activation(out=sq_[:], in_=pf[:, 0:256], func=AF.Square), reads=[r_pf], writes=[r_sq])
                P.add("dve", lambda e, sq_=sq_, pf=pf, a_=a_: e.scalar_tensor_tensor(out=a_[:], in0=pf[:, 0:256], scalar=0.0, in1=sq_[:],
                                                                                 op0=ALU.is_gt, op1=ALU.mult), reads=[r_pf, r_sq], writes=[r_a])
                for tl in range(2):
                    for cg in range(2):
                        fa, r_fa = acc[tl][cg]
                        P.add("pe", lambda e, fa=fa, a_=a_, tl=tl, cg=cg, j=j: e.matmul(fa[:, :], lhsT=a_[:, tl * 128:(tl + 1) * 128],
                                                                                      rhs=wf2[:, j, cg * 512:(cg + 1) * 512], start=(j == 0), stop=(j == 31)),
                              reads=[r_a, r_wf2], writes=[r_fa])
                if j != 31:
                    continue
                if b3 + 2 < NB3:
                    load_h2(b3 + 2)
                for tl in range(2):
                    t = b3 * 2 + tl
                    rows = slice(t * 128, (t + 1) * 128)
                    x1t, r_x1 = x1.next()
                    x2t, r_x2 = x2.next()
                    s4t, r_s4 = s4.next()
                    dma("sp", x1t[:], X1_d[rows, :], writes=[r_x1], key=f"x1b{x1.i}")
                    for cg in range(2):
                        fa, r_fa = acc[tl][cg]
                        P.add("dve", lambda e, fa=fa, cg=cg, x2t=x2t: e.tensor_tensor(out=x2t[:, cg * 512:(cg + 1) * 512], in0=fa[:, :],
                                                                                   in1=gt2row[:, cg * 512:(cg + 1) * 512], op=ALU.mult),
                              reads=[r_fa, r_gt2], writes=[r_x2])
                    P.add("pool", lambda e, x2t=x2t, x1t=x1t: e.tensor_tensor(out=x2t[:], in0=x2t[:], in1=x1t[:], op=ALU.add), reads=[r_x2, r_x1], writes=[r_x2])
                    P.add("act", lambda e, x2t=x2t, s4t=s4t: e.activation(out=junk[:], in_=x2t[:], func=AF.Square, accum_out=s4t[:, 0:1]),
                          reads=[r_x2], writes=[r_junk, r_s4])
                    P.add("dve", lambda e, s4t=s4t: e.tensor_scalar(out=s4t[:, 1:2], in0=s4t[:, 0:1], scalar1=1.0 / D, scalar2=EPS, op0=ALU.mult, op1=ALU.add),
                          reads=[r_s4], writes=[r_s4])
                    P.add("act", lambda e, s4t=s4t: e.activation(out=s4t[:, 2:3], in_=s4t[:, 1:2], func=AF.Sqrt), reads=[r_s4], writes=[r_s4])
                    P.add("dve", lambda e, s4t=s4t: e.reciprocal(out=s4t[:, 3:4], in_=s4t[:, 2:3]), reads=[r_s4], writes=[r_s4])
                    P.add("dve", lambda e, x2t=x2t, s4t=s4t: e.scalar_tensor_tensor(out=x2t[:], in0=x2t[:], scalar=s4t[:, 3:4], in1=gfr[:], op0=ALU.mult, op1=ALU.mult),
                          reads=[r_x2, r_s4, r_gfr], writes=[r_x2])
                    dma("sp", out_d[rows, :], x2t[:], reads=[r_x2], key=f"x2{x2.i}")
            P.emit("phase3b")
    return nc


_NC_CACHE = {}


def _consts():
    ident = np.eye(128, dtype=np.float32)
    rmat = np.zeros((128, 128), dtype=np.float32)
    for p in range(128):
        partner = p + 32 if (p % 64) < 32 else p - 32
        rmat[partner, p] = 1.0
    inv_freq = (np.float32(10000.0) ** (-np.arange(0, 64, 2, dtype=np.float32) / np.float32(64))).astype(np.float32)
    cst = np.zeros((128, 4), dtype=np.float32)
    for p in range(128):
        cst[p, 0] = inv_freq[p % 32]
        cst[p, 1] = -1.0 if (p % 64) < 32 else 1.0
        cst[p, 2] = np.float32(np.pi / 2)
    return ident, rmat, cst


def _run(inputs, n_cores_per_seq=2):
    x = np.asarray(inputs["x"], dtype=np.float32)
    B, S, _ = x.shape
    S_OWN = S // n_cores_per_seq
    n_cores = B * n_cores_per_seq
    key = (S_OWN, S)
    if key not in _NC_CACHE:
        _NC_CACHE[key] = build(S_OWN, S)
    nc = _NC_CACHE[key]
    ident, rmat, cst = _consts()
    f = lambda k: np.ascontiguousarray(np.asarray(inputs[k], dtype=np.float32))
    pos = np.asarray(inputs["positions"]).astype(np.int32, copy=False)
    c = f("c")
    col = lambda v: np.ascontiguousarray(v.reshape(-1, 128).T)
    shared = {
        "w_ada": f("w_ada")[0], "b_ada": f("b_ada")[0][None, :], "b_ada_col": col(f("b_ada")[0]),
        "g1_col": col(f("g_norm1")[0]), "g2_col": col(f("g_norm2")[0]),
        "w_in": f("w_in")[0], "ln_g": f("gmlp_ln_g")[0][None, :], "ln_b": f("gmlp_ln_b")[0][None, :],
        "wsT": np.ascontiguousarray(f("w_spatial")[0].transpose(2, 0, 1)),
        "bs_col": np.ascontiguousarray(f("b_spatial")[0].T),
        "lamv": np.concatenate([f("lambda_q1")[0], f("lambda_q2")[0], f("lambda_k1")[0], f("lambda_k2")[0]])[None, :],
        "subln_g": f("subln_g")[0][None, :], "w_out": f("w_out")[0], "w_ff1": f("w_ff1")[0], "w_ff2": f("w_ff2")[0],
        "g_final": f("g_final")[None, :], "ident": ident, "rmat": rmat, "cst": cst,
    }
    in_maps = []
    for core in range(n_cores):
        b, half = divmod(core, n_cores_per_seq)
        own = slice(half * S_OWN, (half + 1) * S_OWN)
        order = np.concatenate([np.arange(own.start, own.stop),
                                np.arange(0, own.start), np.arange(own.stop, S)])
        m = dict(shared)
        m["x"] = np.ascontiguousarray(x[b][order])
        m["pos"] = np.ascontiguousarray(pos[b][order][None, :])
        m["cT"] = col(c[b])
        in_maps.append(m)
    res = run_bass_kernel_spmd(nc, in_maps, core_ids=list(range(n_cores)))
    out = np.empty((B, S, D), dtype=np.float32)
    for core in range(n_cores):
        b, half = divmod(core, n_cores_per_seq)
        out[b, half * S_OWN:(half + 1) * S_OWN] = res.results[core]["out"]
    return out


def kernel(**inputs):
    return _run(inputs, n_cores_per_seq=2)
```

```python
import numpy as np
from contextlib import ExitStack

import concourse.bass as bass
import concourse.mybir as mybir
from concourse.bass_utils import run_bass_kernel_spmd

F32 = mybir.dt.float32
BF16 = mybir.dt.bfloat16
I32 = mybir.dt.int32
AF = mybir.ActivationFunctionType
ALU = mybir.AluOpType
AX = mybir.AxisListType
PI = float(np.pi)

D = 1024
KC = 8
H = 8
DFF = 4096
NCOL = 7168
EPS = 1e-6
VW = 130
LAMBDA_INIT = 0.2


class Res:
    __slots__ = ("name", "last_write", "reads", "excl")

    def __init__(self, name, excl=False):
        self.name = name
        self.last_write = None
        self.reads = []
        self.excl = excl


class Op:
    __slots__ = ("eng", "fn", "deps", "dma", "semkey", "marked", "ev", "idx")

    def __init__(self, eng, fn, dma, semkey):
        self.eng = eng
        self.fn = fn
        self.deps = []
        self.dma = dma
        self.semkey = semkey
        self.marked = False
        self.ev = None


class Prog:
    ENGS = ("pe", "act", "dve", "pool", "sp")

    def __init__(self, nc, semstack):
        self.nc = nc
        self._semstack = semstack
        self.all_res = []
        self.sems = {}
        self.semcount = {}
        self.known = {e: {} for e in self.ENGS}
        self.ops = []
        self.nres = 0
        self.nops = 0
        self.disabled = False

    def res(self, name=None, excl=False):
        self.nres += 1
        r = Res(name or f"r{self.nres}", excl)
        self.all_res.append(r)
        return r

    def _sem(self, key):
        if key not in self.sems:
            self.sems[key] = self._semstack.enter_context(self.nc.semaphore(f"s_{key}"))
            self.semcount[key] = 0
        return self.sems[key]

    def add(self, eng, fn, reads=(), writes=(), dma=False, semkey=None):
        if self.disabled:
            return None
        op = Op(eng, fn, dma, semkey)
        deps = []
        for r in reads:
            if r.last_write is not None:
                deps.append(r.last_write)
            if r.excl:
                deps.extend(o for o in r.reads if o.eng != eng)
        for w in writes:
            if w.last_write is not None:
                deps.append(w.last_write)
            deps.extend(w.reads)
        for r in reads:
            r.reads.append(op)
        for w in writes:
            w.last_write = op
            w.reads = []
        self.nops += 1
        op.idx = self.nops
        seen = set()
        latest = {}
        for d in deps:
            if id(d) in seen or d is op:
                continue
            seen.add(id(d))
            if d.eng == "pe" and eng == "pe" and not d.dma and not dma:
                continue
            if d.dma:
                op.deps.append(d)
                d.marked = True
            else:
                if d.eng not in latest or latest[d.eng].idx < d.idx:
                    latest[d.eng] = d
        for d in latest.values():
            op.deps.append(d)
            d.marked = True
        if dma:
            op.marked = True
        self.ops.append(op)
        return op

    def emit(self, block_name):
        nc = self.nc
        last = {}
        for op in self.ops:
            if not op.dma:
                last[op.eng] = op
        for op in last.values():
            op.marked = True
        for op in self.ops:
            if op.marked and op.ev is None:
                if op.dma:
                    key = "d_" + op.semkey
                    self._sem(key)
                    self.semcount[key] += 16
                else:
                    key = "e_" + op.eng
                    self._sem(key)
                    self.semcount[key] += 1
                op.ev = (key, self.semcount[key])
        per = {e: [] for e in self.ENGS}
        for op in self.ops:
            per[op.eng].append(op)
        prog = self

        def run(engname, e):
            known = prog.known[engname]
            for op in per[engname]:
                for d in op.deps:
                    key, val = d.ev
                    if known.get(key, 0) >= val:
                        continue
                    e.wait_ge(prog.sems[key], val)
                    known[key] = val
                ins = op.fn(e)
                if op.marked:
                    key, val = op.ev
                    ins.then_inc(prog.sems[key], 16 if op.dma else 1)
            for key, val in prog.semcount.items():
                if val > 0 and known.get(key, 0) < val:
                    e.wait_ge(prog.sems[key], val)
                    known[key] = val

        with nc.Block(block_name) as block:
            @block.tensor
            def _(e):
                run("pe", e)

            @block.scalar
            def _(e):
                run("act", e)

            @block.vector
            def _(e):
                run("dve", e)

            @block.gpsimd
            def _(e):
                run("pool", e)

            @block.sync
            def _(e):
                run("sp", e)
        self.ops = []
        for r in self.all_res:
            r.last_write = None
            r.reads = []


class Ring:
    def __init__(self, items):
        self.items = items
        self.i = -1

    def next(self):
        self.i = (self.i + 1) % len(self.items)
        return self.items[self.i]

    def cur(self):
        return self.items[self.i]


import os
_MAXPH = int(os.environ.get("KPH", "9"))
_KSUB = int(os.environ.get("KSUB", "0"))


def build(S_OWN, S_SEQ):
    NT_OWN = S_OWN // 128
    NT_SEQ = S_SEQ // 128
    NB_OWN = S_OWN // 512
    NB_SEQ = S_SEQ // 512
    NCH = NT_SEQ
    NPAIR = NCH // 2
    NQB = S_OWN // 256

    nc = bass.Bass("TRN2", target_bir_lowering=False)

    def din(name, shape, dt=F32):
        return nc.dram_tensor(name, list(shape), dt, kind="ExternalInput").ap()

    def dscr(name, shape, dt):
        return nc.dram_tensor(name, list(shape), dt, kind="Internal").ap()

    x_d = din("x", [S_SEQ, D])
    pos_d = din("pos", [1, S_SEQ], I32)
    cT_d = din("cT", [128, KC])
    wada_d = din("w_ada", [D, 6 * D])
    bada_d = din("b_ada", [1, 6 * D])
    badac_d = din("b_ada_col", [128, 48])
    g1c_d = din("g1_col", [128, KC])
    g2c_d = din("g2_col", [128, KC])
    win_d = din("w_in", [D, NCOL])
    lng_d = din("ln_g", [1, D])
    lnb_d = din("ln_b", [1, D])
    wsT_d = din("wsT", [128, 8, 128])
    bsc_d = din("bs_col", [128, 8])
    lamv_d = din("lamv", [1, 256])
    subg_d = din("subln_g", [1, 128])
    wout_d = din("w_out", [D, D])
    wff1_d = din("w_ff1", [D, DFF])
    wff2_d = din("w_ff2", [DFF, D])
    gfin_d = din("g_final", [1, D])
    ident_d = din("ident", [128, 128])
    rmat_d = din("rmat", [128, 128])
    cst_d = din("cst", [128, 4])
    out_d = nc.dram_tensor("out", [S_OWN, D], F32, kind="ExternalOutput").ap()

    TAB_d = dscr("tab", [2, 128, S_SEQ], F32)
    KT_d = dscr("ktd", [H, 128, S_SEQ], BF16)
    QT_d = dscr("qtd", [H, 128, S_OWN], BF16)
    V_d = dscr("vd", [H, 128, NCH, VW], BF16)
    GA_d = dscr("gad", [S_OWN, D], BF16)
    GB_d = dscr("gbd", [S_OWN, D], BF16)
    BB_d = dscr("bbd", [S_OWN, D], BF16)
    X1_d = dscr("x1d", [S_OWN, D], F32)
    H2T_d = dscr("h2td", [128, KC, S_OWN], BF16)
    GT_d = dscr("gtd", [2, D], F32)

    with ExitStack() as semstack, ExitStack() as glob:
        P = Prog(nc, semstack)

        def T(es, name, shape, dt):
            return es.enter_context(nc.sbuf_tensor("sb_" + name, list(shape), dt)), P.res(name)

        def PS(es, name, shape, dt=F32):
            return es.enter_context(nc.psum_tensor("pp_" + name, list(shape), dt)), P.res(name, excl=True)

        def dma(eng, out, in_, reads=(), writes=(), key=None):
            P.add(eng, lambda e: e.dma_start(out=out, in_=in_), reads=reads, writes=writes,
                  dma=True, semkey=key)

        class WParts:
            NP = 3

            def __init__(self, name):
                self.name = name
                self.parts = [P.res(f"{name}_p{i}") for i in range(self.NP)]
                self.n = 0

            def load(self, out, in_):
                i = self.n % self.NP
                self.n += 1
                dma("pool", out, in_, writes=[self.parts[i]], key=f"{self.name}{i}")

        ident, r_ident = T(glob, "ident", [128, 128], BF16)
        modcol, r_modcol = T(glob, "modcol", [128, 48], F32)
        A1, r_A1 = T(glob, "A1", [128, KC], F32)
        A2, r_A2 = T(glob, "A2", [128, KC], F32)
        lamc, r_lamc = T(glob, "lamc", [128, 1], F32)
        cst, r_cst = T(glob, "cst", [128, 4], F32)

        with ExitStack() as es:
            cTt, r_cTt = T(es, "cTt", [128, KC], F32)
            gt1row, r_gt1 = T(es, "gt1row", [128, D], F32)
            gt2row, r_gt2 = T(es, "gt2row", [128, D], F32)
            cact2, r_cact2 = T(es, "cact2", [128, KC, 2], F32)
            CB, r_CB = T(es, "CB", [128, KC, 128], F32)
            wad = [T(es, f"wad{i}", [128, KC, 1024], F32) for i in range(2)]
            badac, r_badac = T(es, "badac", [128, 48], F32)
            g1c, r_g1c = T(es, "g1c", [128, KC], F32)
            g2c, r_g2c = T(es, "g2c", [128, KC], F32)
            lamv, r_lamv = T(es, "lamv", [128, 256], F32)
            lprod, r_lprod = T(es, "lprod", [128, 128], F32)
            ls12, r_ls12 = T(es, "ls12", [128, 2], F32)
            posi, r_posi = T(es, "posi", [128, S_SEQ], I32)
            CW = min(2048, S_SEQ)
            posf, r_posf = T(es, "posf", [128, CW], F32)
            ang, r_ang = T(es, "ang", [128, CW], F32)
            kf, r_kf = T(es, "kf", [128, CW], F32)
            ki, r_ki = T(es, "ki", [128, CW], I32)
            tsin = [T(es, f"tsin{i}", [128, CW], F32) for i in range(2)]
            tcos = [T(es, f"tcos{i}", [128, CW], F32) for i in range(2)]
            sgnhp, r_sgnhp = T(es, "sgnhp", [128, 2], F32)
            ps_col, r_pscol = PS(es, "ps_col", [128, 512])
            ps_row = [PS(es, f"ps_row{i}", [128, 512]) for i in range(2)]

            dma("sp", cTt[:], cT_d, writes=[r_cTt], key="cTt")
            dma("sp", cst[:], cst_d, writes=[r_cst], key="cst")
            dma("sp", badac[:], badac_d, writes=[r_badac], key="badac")
            dma("sp", g1c[:], g1c_d, writes=[r_g1c], key="g1c")
            dma("sp", g2c[:], g2c_d, writes=[r_g2c], key="g2c")
            dma("sp", lamv[:], lamv_d.partition_broadcast(128), writes=[r_lamv], key="lamv")
            dma("sp", posi[:], pos_d.partition_broadcast(128), writes=[r_posi], key="posi")
            dma("pool", ident[:], ident_d, writes=[r_ident], key="ident")
            dma("sp", gt1row[:], bada_d[:, 2 * D:3 * D].partition_broadcast(128), writes=[r_gt1], key="gt1")
            dma("sp", gt2row[:], bada_d[:, 5 * D:6 * D].partition_broadcast(128), writes=[r_gt2], key="gt2")

            for c2 in range(2):
                P.add("act", lambda e, c2=c2: e.activation(out=cact2[:, :, c2], in_=cTt[:], func=AF.Silu),
                      reads=[r_cTt], writes=[r_cact2])
            P.add("dve", lambda e: e.tensor_copy(out=CB[:], in_=cact2[:, :, 0:1].to_broadcast([128, KC, 128])),
                  reads=[r_cact2], writes=[r_CB])

            wada_v = wada_d.rearrange("(kc p) n -> p kc n", p=128)
            for g in range(6):
                wt, r_wt = wad[g % 2]
                dma("sp", wt[:], wada_v[:, :, g * D:(g + 1) * D], writes=[r_wt], key=f"wad{g % 2}")
                if g in (0, 1, 3, 4):
                    for jj in range(8):
                        j = g * 8 + jj
                        for kc in range(KC):
                            P.add("pe", lambda e, wt=wt, jj=jj, j=j, kc=kc: e.matmul(
                                ps_col[:, 2 * j:2 * j + 2], lhsT=wt[:, kc, jj * 128:(jj + 1) * 128],
                                rhs=cact2[:, kc, :], start=(kc == 0), stop=(kc == KC - 1)),
                                reads=[r_wt, r_cact2], writes=[r_pscol])
                else:
                    grow, r_grow = (gt1row, r_gt1) if g == 2 else (gt2row, r_gt2)
                    for half in range(2):
                        pr, r_pr = ps_row[half]
                        for kc in range(KC):
                            P.add("pe", lambda e, wt=wt, pr=pr, half=half, kc=kc: e.matmul(
                                pr[:, :], lhsT=CB[:, kc, :], rhs=wt[:, kc, half * 512:(half + 1) * 512],
                                start=(kc == 0), stop=(kc == KC - 1)),
                                reads=[r_wt, r_CB], writes=[r_pr])
                        P.add("dve", lambda e, grow=grow, pr=pr, half=half: e.tensor_tensor(
                            out=grow[:, half * 512:(half + 1) * 512], in0=pr[:, :],
                            in1=grow[:, half * 512:(half + 1) * 512], op=ALU.add),
                            reads=[r_pr, r_grow], writes=[r_grow])
            dma("sp", GT_d[0:1, :], gt1row[0:1, :], reads=[r_gt1], key="gt1")
            dma("sp", GT_d[1:2, :], gt2row[0:1, :], reads=[r_gt2], key="gt2")
            for (j0, j1) in ((0, 16), (24, 40)):
                P.add("dve", lambda e, j0=j0, j1=j1: e.tensor_tensor(
                    out=modcol[:, j0:j1], in0=ps_col[:, 2 * j0:2 * j1].rearrange("p (j t) -> p j t", t=2)[:, :, 0],
                    in1=badac[:, j0:j1], op=ALU.add), reads=[r_pscol, r_badac], writes=[r_modcol])
            P.add("dve", lambda e: e.scalar_tensor_tensor(out=A1[:], in0=modcol[:, 8:16], scalar=1.0, in1=g1c[:],
                                                           op0=ALU.add, op1=ALU.mult),
                  reads=[r_modcol, r_g1c], writes=[r_A1])
            P.add("dve", lambda e: e.scalar_tensor_tensor(out=A2[:], in0=modcol[:, 32:40], scalar=1.0, in1=g2c[:],
                                                           op0=ALU.add, op1=ALU.mult),
                  reads=[r_modcol, r_g2c], writes=[r_A2])
            P.add("dve", lambda e: e.tensor_tensor(out=lprod[:], in0=lamv[:, 0:128], in1=lamv[:, 128:256], op=ALU.mult),
                  reads=[r_lamv], writes=[r_lprod])
            P.add("dve", lambda e: e.tensor_reduce(out=ls12[:], in_=lprod[:].rearrange("p (a b) -> p a b", a=2),
                                                   axis=AX.X, op=ALU.add), reads=[r_lprod], writes=[r_ls12])
            P.add("act", lambda e: e.activation(out=ls12[:], in_=ls12[:], func=AF.Exp), reads=[r_ls12], writes=[r_ls12])
            P.add("dve", lambda e: e.tensor_tensor(out=lamc[:], in0=ls12[:, 0:1], in1=ls12[:, 1:2], op=ALU.subtract),
                  reads=[r_ls12], writes=[r_lamc])
            P.add("dve", lambda e: e.tensor_scalar(out=lamc[:], in0=lamc[:], scalar1=LAMBDA_INIT, scalar2=None, op0=ALU.add),
                  reads=[r_lamc], writes=[r_lamc])
            for ci in range(S_SEQ // CW):
                cs = slice(ci * CW, (ci + 1) * CW)
                ts_, r_ts = tsin[ci % 2]
                tc_, r_tc = tcos[ci % 2]
                P.add("dve", lambda e, cs=cs: e.tensor_copy(out=posf[:], in_=posi[:, cs]), reads=[r_posi], writes=[r_posf])
                P.add("dve", lambda e: e.tensor_scalar(out=ang[:], in0=posf[:], scalar1=cst[:, 0:1], scalar2=None, op0=ALU.mult),
                      reads=[r_posf, r_cst], writes=[r_ang])
                P.add("dve", lambda e: e.tensor_scalar(out=kf[:], in0=ang[:], scalar1=1.0 / (2 * PI), scalar2=None, op0=ALU.mult),
                      reads=[r_ang], writes=[r_kf])
                P.add("dve", lambda e: e.tensor_copy(out=ki[:], in_=kf[:]), reads=[r_kf], writes=[r_ki])
                P.add("dve", lambda e: e.tensor_copy(out=kf[:], in_=ki[:]), reads=[r_ki], writes=[r_kf])
                P.add("dve", lambda e: e.scalar_tensor_tensor(out=kf[:], in0=kf[:], scalar=-2 * PI, in1=ang[:], op0=ALU.mult, op1=ALU.add),
                      reads=[r_kf, r_ang], writes=[r_kf])
                P.add("act", lambda e, ts_=ts_: e.activation(out=ts_[:], in_=kf[:], func=AF.Sin, scale=cst[:, 1:2]),
                      reads=[r_kf, r_cst], writes=[r_ts])
                P.add("dve", lambda e: e.tensor_scalar(out=kf[:], in0=ang[:], scalar1=1.0 / (2 * PI), scalar2=0.25, op0=ALU.mult, op1=ALU.add),
                      reads=[r_ang], writes=[r_kf])
                P.add("dve", lambda e: e.tensor_copy(out=ki[:], in_=kf[:]), reads=[r_kf], writes=[r_ki])
                P.add("dve", lambda e: e.tensor_copy(out=kf[:], in_=ki[:]), reads=[r_ki], writes=[r_kf])
                P.add("dve", lambda e: e.scalar_tensor_tensor(out=kf[:], in0=kf[:], scalar=-2 * PI, in1=ang[:], op0=ALU.mult, op1=ALU.add),
                      reads=[r_kf, r_ang], writes=[r_kf])
                P.add("act", lambda e, tc_=tc_: e.activation(out=tc_[:], in_=kf[:], func=AF.Sin, bias=cst[:, 2:3]),
                      reads=[r_kf, r_cst], writes=[r_tc])
                dma("sp", TAB_d[0, :, cs], tc_[:], reads=[r_tc], key=f"tcos{ci % 2}")
                dma("sp", TAB_d[1, :, cs], ts_[:], reads=[r_ts], key=f"tsin{ci % 2}")
            P.emit("phase0")

        with ExitStack() as es:
            P.disabled = (1 > _MAXPH)
            win, _ = T(es, "win", [128, KC, NCOL], BF16)
            wp_win = WParts("win")
            wsT, r_wsT = T(es, "wsT", [128, 8, 128], BF16)
            rmat, r_rmat = T(es, "rmat", [128, 128], BF16)
            lngr, r_lngr = T(es, "lngr", [128, D], F32)
            lnbr, r_lnbr = T(es, "lnbr", [128, D], F32)
            bsc, r_bsc = T(es, "bsc", [128, 8], F32)
            xt = [T(es, f"xt{i}", [128, D], F32) for i in range(2)]
            junk, r_junk = T(es, "junk", [128, D], BF16)
            st = [T(es, f"st{i}", [128, 8], F32) for i in range(2)]
            xn = [T(es, f"xn{i}", [128, D], BF16) for i in range(2)]
            hT = [T(es, f"hT{i}", [128, KC, 512], BF16) for i in range(2)]
            cosb = [T(es, f"cosb{i}", [128, 512], F32) for i in range(2)]
            sinb = [T(es, f"sinb{i}", [128, 512], F32) for i in range(2)]
            kraw = Ring([T(es, f"kraw{i}", [128, 512], BF16) for i in range(2)])
            t1r = Ring([T(es, f"t1r{i}", [128, 512], F32) for i in range(2)])
            t2r = Ring([T(es, f"t2r{i}", [128, 512], F32) for i in range(2)])
            kfin = Ring([T(es, f"kfin{i}", [128, 512], BF16) for i in range(2)])
            vblk = [T(es, f"vblk{i}", [128, H, VW], BF16) for i in range(2)]
            gu = [T(es, f"gu{i}", [128, D], BF16) for i in range(2)]
            gv = [T(es, f"gv{i}", [128, D], F32) for i in range(2)]
            sga = [T(es, f"sga{i}", [128, D], BF16) for i in range(2)]
            lst = [T(es, f"lst{i}", [128, 8], F32) for i in range(2)]
            vln, r_vln = T(es, "vln", [128, D], BF16)
            tmpf, r_tmpf = T(es, "tmpf", [128, D], F32)
            gbt = [T(es, f"gbt{i}", [128, D], BF16) for i in range(2)]
            gat = [T(es, f"gat{i}", [128, D], BF16) for i in range(2)]
            ps_tr, r_pstr = PS(es, "ps_tr", [128, KC, 128], BF16)
            ps_tm = Ring([PS(es, f"ps_tm{i}", [128, 512]) for i in range(2)])
            ps_sv, r_pssv = PS(es, "ps_sv", [128, 8, 128])
            ps_k = Ring([PS(es, f"ps_k{i}", [128, 512]) for i in range(2)])
            ps_rot, r_psrot = PS(es, "ps_rot", [128, 512])

            for (c0, c1) in ((4096, 5120), (3072, 4096), (2048, 3072), (0, 2048), (5120, 7168)):
                for kc in range(KC):
                    wp_win.load(win[:, kc, c0:c1], win_d[kc * 128:(kc + 1) * 128, c0:c1])
            dma("pool", wsT[:], wsT_d, writes=[r_wsT], key="wsT")
            dma("pool", rmat[:], rmat_d, writes=[r_rmat], key="rmat")
            dma("sp", lngr[:], lng_d.partition_broadcast(128), writes=[r_lngr], key="lngr")
            dma("sp", lnbr[:], lnb_d.partition_broadcast(128), writes=[r_lnbr], key="lnbr")
            dma("sp", bsc[:], bsc_d, writes=[r_bsc], key="bsc")
            for (vb, r_vb) in vblk:
                P.add("pool", lambda e, vb=vb: e.memset(vb[:, :, 128:VW], 1.0), writes=[r_vb])

            def is_own(t):
                return (t // 4) < NB_OWN and not (_KSUB & 1)

            def stageA(t):
                blk, i = divmod(t, 4)
                hTt, r_hT = hT[blk % 2]
                if i == 0:
                    cb, r_cb = cosb[blk % 2]
                    sb_, r_sb = sinb[blk % 2]
                    bs = slice(blk * 512, (blk + 1) * 512)
                    dma("sp", cb[:], TAB_d[0, :, bs], writes=[r_cb], key=f"cosb{blk % 2}")
                    dma("sp", sb_[:], TAB_d[1, :, bs], writes=[r_sb], key=f"sinb{blk % 2}")
                xtt, r_xt = xt[t % 2]
                stt, r_st = st[t % 2]
                xnt, r_xn = xn[t % 2]
                dma("sp", xtt[:], x_d[t * 128:(t + 1) * 128, :], writes=[r_xt], key=f"xt{t % 2}")
                P.add("act", lambda e: e.activation(out=junk[:], in_=xtt[:], func=AF.Square, accum_out=stt[:, 0:1]),
                      reads=[r_xt], writes=[r_junk, r_st])
                P.add("dve", lambda e: e.tensor_scalar(out=stt[:, 1:2], in0=stt[:, 0:1], scalar1=1.0 / D, scalar2=EPS,
                                                       op0=ALU.mult, op1=ALU.add), reads=[r_st], writes=[r_st])
                P.add("act", lambda e: e.activation(out=stt[:, 2:3], in_=stt[:, 1:2], func=AF.Sqrt), reads=[r_st], writes=[r_st])
                P.add("dve", lambda e: e.reciprocal(out=stt[:, 3:4], in_=stt[:, 2:3]), reads=[r_st], writes=[r_st])
                P.add("pool", lambda e: e.tensor_scalar(out=xnt[:], in0=xtt[:], scalar1=stt[:, 3:4], scalar2=None, op0=ALU.mult),
                      reads=[r_xt, r_st], writes=[r_xn])
                for kc in range(KC):
                    P.add("pe", lambda e, kc=kc: e.transpose(out=ps_tr[:, kc, :], in_=xnt[:, kc * 128:(kc + 1) * 128], identity=ident[:]),
                          reads=[r_xn, r_ident], writes=[r_pstr])
                for kc in range(KC):
                    P.add("dve", lambda e, kc=kc: e.tensor_scalar(
                        out=hTt[:, kc, i * 128:(i + 1) * 128], in0=ps_tr[:, kc, :], scalar1=A1[:, kc:kc + 1], scalar2=modcol[:, kc:kc + 1],
                        op0=ALU.mult, op1=ALU.add), reads=[r_pstr, r_A1, r_modcol], writes=[r_hT])

            def tm_group(t, cg, consumer):
                blk, i = divmod(t, 4)
                hTt, r_hT = hT[blk % 2]
                pt, r_pt = ps_tm.next()
                for kc in range(KC):
                    P.add("pe", lambda e, kc=kc: e.matmul(
                        pt[:, :], lhsT=hTt[:, kc, i * 128:(i + 1) * 128], rhs=win[:, kc, cg * 512:(cg + 1) * 512],
                        start=(kc == 0), stop=(kc == KC - 1)), reads=[r_hT] + wp_win.parts, writes=[r_pt])
                consumer(pt, r_pt)

            def stageB(t):
                vb, r_vb = vblk[t % 2]
                if not (_KSUB & 4):
                    for hf in range(2):
                        def cons_v(pt, r_pt, hf=hf):
                            P.add("dve", lambda e: e.tensor_copy(out=vb[:, hf * 4:(hf + 1) * 4, 0:128],
                                                                 in_=pt[:, :].rearrange("p (h e) -> p h e", h=4)),
                                  reads=[r_pt], writes=[r_vb])
                        tm_group(t, 8 + hf, cons_v)
                    dma("sp", V_d[:, :, t, :].rearrange("h p e -> p h e"), vb[:], reads=[r_vb], key=f"vblk{t % 2}")
                if not is_own(t):
                    return
                gu_, r_gu = gu[t % 2]
                gv_, r_gv = gv[t % 2]
                sga_, r_sga = sga[t % 2]
                lst_, r_lst = lst[t % 2]
                gbt_t, r_gbt = gbt[t % 2]
                for hf in range(2):
                    def cons_u(pt, r_pt, hf=hf):
                        P.add("act", lambda e: e.activation(out=gu_[:, hf * 512:(hf + 1) * 512], in_=pt[:, :], func=AF.Gelu_apprx_tanh),
                              reads=[r_pt], writes=[r_gu])
                    tm_group(t, 0 + hf, cons_u)
                for hf in range(2):
                    def cons_va(pt, r_pt, hf=hf):
                        P.add("act", lambda e: e.activation(out=gv_[:, hf * 512:(hf + 1) * 512], in_=pt[:, :], func=AF.Gelu_apprx_tanh,
                                                            accum_out=lst_[:, hf:hf + 1]), reads=[r_pt], writes=[r_gv, r_lst])
                    tm_group(t, 2 + hf, cons_va)
                for hf in range(2):
                    def cons_ga(pt, r_pt, hf=hf):
                        P.add("act", lambda e: e.activation(out=sga_[:, hf * 512:(hf + 1) * 512], in_=pt[:, :], func=AF.Sigmoid),
                              reads=[r_pt], writes=[r_sga])
                    tm_group(t, 10 + hf, cons_ga)
                for hf in range(2):
                    def cons_gb(pt, r_pt, hf=hf):
                        P.add("act", lambda e: e.activation(out=gbt_t[:, hf * 512:(hf + 1) * 512], in_=pt[:, :], func=AF.Sigmoid),
                              reads=[r_pt], writes=[r_gbt])
                    tm_group(t, 12 + hf, cons_gb)
                dma("sp", GB_d[t * 128:(t + 1) * 128, :], gbt_t[:], reads=[r_gbt], key=f"gbt{t % 2}")

            def stageC(t):
                if not is_own(t):
                    return
                gu_, r_gu = gu[t % 2]
                gv_, r_gv = gv[t % 2]
                sga_, r_sga = sga[t % 2]
                lst_, r_lst = lst[t % 2]
                gat_t, r_gat = gat[t % 2]
                P.add("act", lambda e: e.activation(out=junk[:], in_=gv_[:], func=AF.Square, accum_out=lst_[:, 2:3]),
                      reads=[r_gv], writes=[r_junk, r_lst])
                P.add("dve", lambda e: e.tensor_tensor(out=lst_[:, 3:4], in0=lst_[:, 0:1], in1=lst_[:, 1:2], op=ALU.add), reads=[r_lst], writes=[r_lst])
                P.add("dve", lambda e: e.tensor_scalar(out=lst_[:, 3:4], in0=lst_[:, 3:4], scalar1=-1.0 / D, scalar2=None, op0=ALU.mult),
                      reads=[r_lst], writes=[r_lst])
                P.add("dve", lambda e: e.tensor_tensor(out=lst_[:, 4:5], in0=lst_[:, 3:4], in1=lst_[:, 3:4], op=ALU.mult), reads=[r_lst], writes=[r_lst])
                P.add("dve", lambda e: e.scalar_tensor_tensor(out=lst_[:, 5:6], in0=lst_[:, 2:3], scalar=1.0 / D, in1=lst_[:, 4:5],
                                                               op0=ALU.mult, op1=ALU.subtract), reads=[r_lst], writes=[r_lst])
                P.add("dve", lambda e: e.tensor_scalar(out=lst_[:, 5:6], in0=lst_[:, 5:6], scalar1=EPS, scalar2=None, op0=ALU.add),
                      reads=[r_lst], writes=[r_lst])
                P.add("act", lambda e: e.activation(out=lst_[:, 6:7], in_=lst_[:, 5:6], func=AF.Sqrt), reads=[r_lst], writes=[r_lst])
                P.add("dve", lambda e: e.reciprocal(out=lst_[:, 7:8], in_=lst_[:, 6:7]), reads=[r_lst], writes=[r_lst])
                P.add("dve", lambda e: e.tensor_scalar(out=gv_[:], in0=gv_[:], scalar1=lst_[:, 3:4], scalar2=lst_[:, 7:8], op0=ALU.add, op1=ALU.mult),
                      reads=[r_gv, r_lst], writes=[r_gv])
                P.add("pool", lambda e: e.tensor_tensor(out=gv_[:], in0=gv_[:], in1=lngr[:], op=ALU.mult), reads=[r_gv, r_lngr], writes=[r_gv])
                P.add("pool", lambda e: e.tensor_tensor(out=vln[:], in0=gv_[:], in1=lnbr[:], op=ALU.add), reads=[r_gv, r_lnbr], writes=[r_vln])
                for g in range(8):
                    P.add("pe", lambda e, g=g: e.matmul(ps_sv[:, g, :], lhsT=wsT[:, g, :], rhs=vln[:, g * 128:(g + 1) * 128], start=True, stop=True),
                          reads=[r_wsT, r_vln], writes=[r_pssv])
                for g in range(8):
                    P.add("dve", lambda e, g=g: e.scalar_tensor_tensor(out=tmpf[:, g * 128:(g + 1) * 128], in0=ps_sv[:, g, :], scalar=bsc[:, g:g + 1],
                                                                        in1=gu_[:, g * 128:(g + 1) * 128], op0=ALU.add, op1=ALU.mult),
                          reads=[r_pssv, r_bsc, r_gu], writes=[r_tmpf])
                P.add("pool", lambda e: e.tensor_tensor(out=gat_t[:], in0=tmpf[:], in1=sga_[:], op=ALU.mult),
                      reads=[r_tmpf, r_sga], writes=[r_gat])
                dma("sp", GA_d[t * 128:(t + 1) * 128, :], gat_t[:], reads=[r_gat], key=f"gat{t % 2}")

            def stageR(blk):
                if _KSUB & 2:
                    return
                hTt, r_hT = hT[blk % 2]
                cb, r_cb = cosb[blk % 2]
                sb_, r_sb = sinb[blk % 2]
                bs = slice(blk * 512, (blk + 1) * 512)
                jobs = [("k", 3072, h) for h in range(H)]
                if blk < NB_OWN:
                    jobs += [("q", 2048, h) for h in range(H)]

                def mm(job):
                    nm, cbase, h = job
                    pk, r_pk = ps_k.next()
                    for kc in range(KC):
                        P.add("pe", lambda e, kc=kc: e.matmul(
                            pk[:, :], lhsT=win[:, kc, cbase + h * 128:cbase + (h + 1) * 128], rhs=hTt[:, kc, :],
                            start=(kc == 0), stop=(kc == KC - 1)), reads=[r_hT] + wp_win.parts, writes=[r_pk])
                    return pk, r_pk

                pend = mm(jobs[0])
                for ji, job in enumerate(jobs):
                    nm, cbase, h = job
                    pk, r_pk = pend
                    if ji + 1 < len(jobs):
                        pend = mm(jobs[ji + 1])
                    kr, r_kr = kraw.next()
                    t1, r_t1 = t1r.next()
                    t2, r_t2 = t2r.next()
                    kf_t, r_kf_t = kfin.next()
                    kfi = kfin.i
                    P.add("act", lambda e, kr=kr, pk=pk: e.activation(out=kr[:], in_=pk[:, :], func=AF.Copy), reads=[r_pk], writes=[r_kr])
                    P.add("pe", lambda e, kr=kr: e.matmul(ps_rot[:, :], lhsT=rmat[:], rhs=kr[:], start=True, stop=True),
                          reads=[r_rmat, r_kr], writes=[r_psrot])
                    P.add("dve", lambda e, t1=t1, pk=pk: e.tensor_tensor(out=t1[:], in0=pk[:, :], in1=cb[:], op=ALU.mult),
                          reads=[r_pk, r_cb], writes=[r_t1])
                    P.add("dve", lambda e, t2=t2: e.tensor_tensor(out=t2[:], in0=ps_rot[:, :], in1=sb_[:], op=ALU.mult),
                          reads=[r_psrot, r_sb], writes=[r_t2])
                    P.add("pool", lambda e, kf_t=kf_t, t1=t1, t2=t2: e.tensor_tensor(out=kf_t[:], in0=t1[:], in1=t2[:], op=ALU.add),
                          reads=[r_t1, r_t2], writes=[r_kf_t])
                    dst = KT_d if nm == "k" else QT_d
                    dma("sp", dst[h, :, bs], kf_t[:], reads=[r_kf_t], key=f"kfin{kfi}")

            for s in range(NT_SEQ + 2):
                if s < NT_SEQ:
                    stageA(s)
                if 0 <= s - 1 < NT_SEQ:
                    stageB(s - 1)
                    if (s - 1) % 4 == 3:
                        stageR((s - 1) // 4)
                if 0 <= s - 2 < NT_SEQ:
                    if not (_KSUB & 8):
                        stageC(s - 2)
            P.emit("phase1")

        with ExitStack() as es:
            P.disabled = (2 > _MAXPH)
            KT = [T(es, f"KT{i}", [128, S_SEQ], BF16) for i in range(2)]
            VA = [T(es, f"VA{i}", [128, NCH, VW], BF16) for i in range(2)]
            Q1 = [T(es, f"Q1p{i}", [128, S_OWN], BF16) for i in range(2)]
            Q2 = [T(es, f"Q2p{i}", [128, S_OWN], BF16) for i in range(2)]
            ET = Ring([T(es, f"ET{i}", [128, 2, 2, 256], BF16) for i in range(3)])
            obuf = [T(es, f"obuf{i}", [128, NT_OWN, 128], BF16) for i in range(2)]
            nst = Ring([T(es, f"nst{i}", [128, 4], F32) for i in range(2)])
            ntmp = Ring([T(es, f"ntmp{i}", [128, 128], F32) for i in range(2)])
            SP_ = Ring([PS(es, f"S{i}", [128, 2, 2, 256]) for i in range(2)])
            acc = [[PS(es, f"acc{m}{qt}", [128, 512]) for qt in range(2)] for m in range(2)]

            for s in range(2):
                P.add("pool", lambda e, s=s: e.memset(Q1[s][0][64:128, :], 0.0), writes=[Q1[s][1]])
                P.add("pool", lambda e, s=s: e.memset(Q2[s][0][0:64, :], 0.0), writes=[Q2[s][1]])

            def load_head(h):
                s = h % 2
                dma("sp", KT[s][0][:], KT_d[h], writes=[KT[s][1]], key=f"KT{s}")
                dma("sp", VA[s][0][:], V_d[h], writes=[VA[s][1]], key=f"VA{s}")
                dma("sp", Q1[s][0][0:64, :], QT_d[h, 0:64, :], writes=[Q1[s][1]], key=f"Q1{s}")
                dma("sp", Q2[s][0][64:128, :], QT_d[h, 64:128, :], writes=[Q2[s][1]], key=f"Q2{s}")

            steps = [(h, qb, j) for h in range(H) for qb in range(NQB) for j in range(NPAIR)]

            def issue_qk(step):
                h, qb, j = step
                s = h % 2
                S_, r_S = SP_.next()
                qs = slice(qb * 256, (qb + 1) * 256)
                for kk in range(2):
                    c = 2 * j + kk
                    P.add("pe", lambda e, S_=S_, kk=kk, c=c, s=s, qs=qs: e.matmul(
                        S_[:, kk, 0, :], lhsT=KT[s][0][:, c * 128:(c + 1) * 128], rhs=Q1[s][0][:, qs], start=True, stop=True),
                        reads=[KT[s][1], Q1[s][1]], writes=[r_S])
                    P.add("pe", lambda e, S_=S_, kk=kk, c=c, s=s, qs=qs: e.matmul(
                        S_[:, kk, 1, :], lhsT=KT[s][0][:, c * 128:(c + 1) * 128], rhs=Q2[s][0][:, qs], start=True, stop=True),
                        reads=[KT[s][1], Q2[s][1]], writes=[r_S])
                return S_, r_S

            load_head(0)
            pending = issue_qk(steps[0])
            for si, (h, qb, j) in enumerate(steps):
                s = h % 2
                if qb == 0 and j == 0 and h + 1 < H:
                    load_head(h + 1)
                S_, r_S = pending
                if si + 1 < len(steps):
                    pending = issue_qk(steps[si + 1])
                E_, r_E = ET.next()
                P.add("act", lambda e, E_=E_, S_=S_: e.activation(out=E_[:].rearrange("p a b c -> p (a b c)"),
                                                                  in_=S_[:].rearrange("p a b c -> p (a b c)"), func=AF.Exp, scale=0.125),
                      reads=[r_S], writes=[r_E])
                for kk in range(2):
                    c = 2 * j + kk
                    for m in range(2):
                        for qt in range(2):
                            a_, r_a = acc[m][qt]
                            P.add("pe", lambda e, a_=a_, E_=E_, kk=kk, m=m, qt=qt, c=c, s=s: e.matmul(
                                a_[:, 0:129], lhsT=E_[:, kk, m, qt * 128:(qt + 1) * 128], rhs=VA[s][0][:, c, 0:129],
                                start=(c == 0), stop=(c == NCH - 1)), reads=[r_E, VA[s][1]], writes=[r_a])
                if j == NPAIR - 1:
                    ob, r_ob = obuf[h % 2]
                    for qt in range(2):
                        a0, r_a0 = acc[0][qt]
                        a1, r_a1 = acc[1][qt]
                        ns, r_ns = nst.next()
                        nt, r_nt = ntmp.next()
                        P.add("dve", lambda e, ns=ns, a0=a0: e.reciprocal(out=ns[:, 0:1], in_=a0[:, 128:129]), reads=[r_a0], writes=[r_ns])
                        P.add("dve", lambda e, ns=ns, a1=a1: e.reciprocal(out=ns[:, 1:2], in_=a1[:, 128:129]), reads=[r_a1], writes=[r_ns])
                        P.add("dve", lambda e, ns=ns: e.tensor_tensor(out=ns[:, 2:3], in0=ns[:, 1:2], in1=lamc[:], op=ALU.mult),
                              reads=[r_ns, r_lamc], writes=[r_ns])
                        P.add("dve", lambda e, ns=ns, nt=nt, a1=a1: e.tensor_scalar(out=nt[:], in0=a1[:, 0:128], scalar1=ns[:, 2:3], scalar2=None, op0=ALU.mult),
                              reads=[r_a1, r_ns], writes=[r_nt])
                        P.add("dve", lambda e, ns=ns, nt=nt, a0=a0, ob=ob, qb=qb, qt=qt: e.scalar_tensor_tensor(
                            out=ob[:, qb * 2 + qt, :], in0=a0[:, 0:128], scalar=ns[:, 0:1], in1=nt[:], op0=ALU.mult, op1=ALU.subtract),
                            reads=[r_a0, r_ns, r_nt], writes=[r_ob])
                    if qb == NQB - 1:
                        TG = min(8, NT_OWN)
                        for t0 in range(0, NT_OWN, TG):
                            dma("sp", BB_d[t0 * 128:(t0 + TG) * 128, h * 128:(h + 1) * 128].rearrange("(t p) e -> p t e", p=128),
                                ob[:, t0:t0 + TG, :], reads=[r_ob], key=f"obuf{h % 2}")
            P.emit("phase2")

        with ExitStack() as es:
            P.disabled = (3 > _MAXPH)
            wout, _ = T(es, "wout", [128, KC, D], BF16)
            wp_wout = WParts("wout")
            g08, r_g08 = T(es, "g08", [128, 128], F32)
            gt1row, r_gt1 = T(es, "gt1row3", [128, D], F32)
            dma("sp", gt1row[:], GT_d[0:1, :].partition_broadcast(128), writes=[r_gt1], key="gt1row3")
            gaT = [T(es, f"gaT{i}", [128, D], BF16) for i in range(2)]
            gbT = [T(es, f"gbT{i}", [128, D], BF16) for i in range(2)]
            bbT = [T(es, f"bbT{i}", [128, D], BF16) for i in range(2)]
            xa = [T(es, f"xa{i}", [128, D], F32) for i in range(3)]
            sq = [T(es, f"sq{i}", [128, D], F32) for i in range(2)]
            s8 = [T(es, f"s8{i}", [128, 32], F32) for i in range(3)]
            mg = [T(es, f"mg{i}", [128, D], BF16) for i in range(2)]
            mT = [T(es, f"mT{i}", [128, KC, 128], BF16) for i in range(2)]
            x1 = [T(es, f"x1{i}", [128, D], F32) for i in range(3)]
            junk, r_junk = T(es, "junk3", [128, D], BF16)
            xn2 = [T(es, f"xn2{i}", [128, D], BF16) for i in range(2)]
            h2 = [T(es, f"h2{i}", [128, KC, 128], BF16) for i in range(2)]
            ps_tr = [PS(es, f"ps_tr3{i}", [128, KC, 128], BF16) for i in range(2)]
            ps_tr2 = [PS(es, f"ps_tr3b{i}", [128, KC, 128], BF16) for i in range(2)]
            ps_o = [[PS(es, f"ps_o{i}{hf}", [128, 512]) for hf in range(2)] for i in range(2)]

            for kc in range(KC):
                wp_wout.load(wout[:, kc, :], wout_d[kc * 128:(kc + 1) * 128, :])
            dma("sp", g08[:], subg_d.partition_broadcast(128), writes=[r_g08], key="g08")
            P.add("dve", lambda e: e.tensor_scalar(out=g08[:], in0=g08[:], scalar1=1.0 - LAMBDA_INIT, scalar2=None, op0=ALU.mult),
                  reads=[r_g08], writes=[r_g08])

            def s3A(t):
                rows = slice(t * 128, (t + 1) * 128)
                ga_, r_ga = gaT[t % 2]
                gb_, r_gb = gbT[t % 2]
                bb_, r_bb = bbT[t % 2]
                xa_, r_xa = xa[t % 3]
                s8t, r_s8 = s8[t % 3]
                sq_, r_sq = sq[t % 2]
                mg_, r_mg = mg[t % 2]
                dma("sp", bb_[:], BB_d[rows, :], writes=[r_bb], key=f"bbT{t % 2}")
                dma("sp", gb_[:], GB_d[rows, :], writes=[r_gb], key=f"gbT{t % 2}")
                dma("sp", ga_[:], GA_d[rows, :], writes=[r_ga], key=f"gaT{t % 2}")
                dma("sp", xa_[:], x_d[rows, :], writes=[r_xa], key=f"xa{t % 3}")
                P.add("dve", lambda e: e.tensor_tensor(out=sq_[:], in0=bb_[:], in1=bb_[:], op=ALU.mult), reads=[r_bb], writes=[r_sq])
                P.add("dve", lambda e: e.tensor_reduce(out=s8t[:, 0:8], in_=sq_[:].rearrange("p (h e) -> p h e", h=8), axis=AX.X, op=ALU.add),
                      reads=[r_sq], writes=[r_s8])
                P.add("dve", lambda e: e.tensor_scalar(out=s8t[:, 8:16], in0=s8t[:, 0:8], scalar1=1.0 / 128, scalar2=EPS, op0=ALU.mult, op1=ALU.add),
                      reads=[r_s8], writes=[r_s8])
                P.add("act", lambda e: e.activation(out=s8t[:, 16:24], in_=s8t[:, 8:16], func=AF.Sqrt), reads=[r_s8], writes=[r_s8])
                P.add("dve", lambda e: e.reciprocal(out=s8t[:, 24:32], in_=s8t[:, 16:24]), reads=[r_s8], writes=[r_s8])
                for hh in range(8):
                    P.add("dve", lambda e, hh=hh: e.scalar_tensor_tensor(
                        out=sq_[:, hh * 128:(hh + 1) * 128], in0=bb_[:, hh * 128:(hh + 1) * 128], scalar=s8t[:, 24 + hh:25 + hh], in1=g08[:],
                        op0=ALU.mult, op1=ALU.mult), reads=[r_bb, r_s8, r_g08], writes=[r_sq])
                P.add("pool", lambda e: e.tensor_tensor(out=sq_[:], in0=sq_[:], in1=gb_[:], op=ALU.mult), reads=[r_sq, r_gb], writes=[r_sq])
                P.add("pool", lambda e: e.tensor_tensor(out=mg_[:], in0=sq_[:], in1=ga_[:], op=ALU.add), reads=[r_sq, r_ga], writes=[r_mg])

            def s3B(t):
                rows = slice(t * 128, (t + 1) * 128)
                mg_, r_mg = mg[t % 2]
                mT_, r_mT = mT[t % 2]
                ptr, r_ptr = ps_tr[t % 2]
                xa_, r_xa = xa[t % 3]
                s8t, r_s8 = s8[t % 3]
                x1t, r_x1 = x1[t % 3]
                xn2_, r_xn2 = xn2[t % 2]
                for kc in range(KC):
                    P.add("pe", lambda e, kc=kc: e.transpose(out=ptr[:, kc, :], in_=mg_[:, kc * 128:(kc + 1) * 128], identity=ident[:]),
                          reads=[r_mg, r_ident], writes=[r_ptr])
                P.add("act", lambda e: e.activation(out=mT_[:].rearrange("p a b -> p (a b)"), in_=ptr[:].rearrange("p a b -> p (a b)"), func=AF.Copy),
                      reads=[r_ptr], writes=[r_mT])
                for hf in range(2):
                    po, r_po = ps_o[t % 2][hf]
                    for kc in range(KC):
                        P.add("pe", lambda e, po=po, kc=kc, hf=hf: e.matmul(po[:, :], lhsT=mT_[:, kc, :], rhs=wout[:, kc, hf * 512:(hf + 1) * 512],
                                                                           start=(kc == 0), stop=(kc == KC - 1)), reads=[r_mT] + wp_wout.parts, writes=[r_po])
                    P.add("dve", lambda e, po=po, hf=hf: e.tensor_tensor(out=x1t[:, hf * 512:(hf + 1) * 512], in0=po[:, :],
                                                                      in1=gt1row[:, hf * 512:(hf + 1) * 512], op=ALU.mult),
                          reads=[r_po, r_gt1], writes=[r_x1])
                P.add("pool", lambda e: e.tensor_tensor(out=x1t[:], in0=x1t[:], in1=xa_[:], op=ALU.add), reads=[r_x1, r_xa], writes=[r_x1])
                dma("sp", X1_d[rows, :], x1t[:], reads=[r_x1], key=f"x1{t % 3}")
                P.add("act", lambda e: e.activation(out=junk[:], in_=x1t[:], func=AF.Square, accum_out=s8t[:, 0:1]),
                      reads=[r_x1, r_s8], writes=[r_junk, r_s8])
                P.add("dve", lambda e: e.tensor_scalar(out=s8t[:, 1:2], in0=s8t[:, 0:1], scalar1=1.0 / D, scalar2=EPS, op0=ALU.mult, op1=ALU.add),
                      reads=[r_s8], writes=[r_s8])
                P.add("act", lambda e: e.activation(out=s8t[:, 2:3], in_=s8t[:, 1:2], func=AF.Sqrt), reads=[r_s8], writes=[r_s8])
                P.add("dve", lambda e: e.reciprocal(out=s8t[:, 3:4], in_=s8t[:, 2:3]), reads=[r_s8], writes=[r_s8])
                P.add("pool", lambda e: e.tensor_scalar(out=xn2_[:], in0=x1t[:], scalar1=s8t[:, 3:4], scalar2=None, op0=ALU.mult),
                      reads=[r_x1, r_s8], writes=[r_xn2])

            def s3C(t):
                rows = slice(t * 128, (t + 1) * 128)
                xn2_, r_xn2 = xn2[t % 2]
                ptr2, r_ptr2 = ps_tr2[t % 2]
                h2t, r_h2 = h2[t % 2]
                for kc in range(KC):
                    P.add("pe", lambda e, kc=kc: e.transpose(out=ptr2[:, kc, :], in_=xn2_[:, kc * 128:(kc + 1) * 128], identity=ident[:]),
                          reads=[r_xn2, r_ident], writes=[r_ptr2])
                for kc in range(KC):
                    P.add("dve", lambda e, kc=kc: e.tensor_scalar(out=h2t[:, kc, :], in0=ptr2[:, kc, :], scalar1=A2[:, kc:kc + 1],
                                                                 scalar2=modcol[:, 24 + kc:25 + kc], op0=ALU.mult, op1=ALU.add),
                          reads=[r_ptr2, r_A2, r_modcol], writes=[r_h2])
                dma("sp", H2T_d[:, :, rows], h2t[:], reads=[r_h2], key=f"h2{t % 2}")

            for s in range(NT_OWN + 2):
                if s < NT_OWN:
                    s3A(s)
                if 0 <= s - 1 < NT_OWN:
                    s3B(s - 1)
                if 0 <= s - 2 < NT_OWN:
                    s3C(s - 2)
            P.emit("phase3a")

        with ExitStack() as es:
            P.disabled = (4 > _MAXPH)
            wf1, _ = T(es, "wf1", [128, KC, DFF], BF16)
            wf2, _ = T(es, "wf2", [128, 32, D], BF16)
            wp_wf1 = WParts("wf1")
            wp_wf2 = WParts("wf2")
            gfr, r_gfr = T(es, "gfr", [128, D], F32)
            gt2row, r_gt2 = T(es, "gt2row3", [128, D], F32)
            dma("sp", gt2row[:], GT_d[1:2, :].partition_broadcast(128), writes=[r_gt2], key="gt2row3")
            LA = 3
            h2b = [T(es, f"h2b{i}", [128, KC, 256], BF16) for i in range(2)]
            sqr = Ring([T(es, f"sqr{i}", [128, 256], F32) for i in range(LA + 1)])
            aT = Ring([T(es, f"aT{i}", [128, 256], BF16) for i in range(LA + 2)])
            x1 = Ring([T(es, f"x1b{i}", [128, D], F32) for i in range(2)])
            x2 = Ring([T(es, f"x2{i}", [128, D], F32) for i in range(2)])
            junk, r_junk = T(es, "junk4", [128, D], BF16)
            s4 = Ring([T(es, f"s4{i}", [128, 4], F32) for i in range(2)])
            ps_f = Ring([PS(es, f"ps_f{i}", [128, 512]) for i in range(LA + 1)])
            acc = [[PS(es, f"fa{tl}{cg}", [128, 512]) for cg in range(2)] for tl in range(2)]

            for kc in range(KC):
                for hf in range(2):
                    wp_wf1.load(wf1[:, kc, hf * 2048:(hf + 1) * 2048], wff1_d[kc * 128:(kc + 1) * 128, hf * 2048:(hf + 1) * 2048])
            for j0 in range(0, 32, 4):
                wp_wf2.load(wf2[:, j0:j0 + 4, :], wff2_d[j0 * 128:(j0 + 4) * 128, :].rearrange("(j p) n -> p j n", p=128))
            dma("sp", gfr[:], gfin_d.partition_broadcast(128), writes=[r_gfr], key="gfr")

            NB3 = S_OWN // 256
            fsteps = [(b3, j) for b3 in range(NB3) for j in range(32)]

            def load_h2(b3):
                hb, r_hb = h2b[b3 % 2]
                dma("sp", hb[:], H2T_d[:, :, b3 * 256:(b3 + 1) * 256], writes=[r_hb], key=f"h2b{b3 % 2}")

            def issue_f1(step):
                b3, j = step
                hb, r_hb = h2b[b3 % 2]
                pf, r_pf = ps_f.next()
                for kc in range(KC):
                    P.add("pe", lambda e, kc=kc: e.matmul(pf[:, 0:256], lhsT=wf1[:, kc, j * 128:(j + 1) * 128], rhs=hb[:, kc, :],
                                                         start=(kc == 0), stop=(kc == KC - 1)), reads=[r_hb] + wp_wf1.parts, writes=[r_pf])
                return pf, r_pf

            load_h2(0)
            if NB3 > 1:
                load_h2(1)
            pend = [issue_f1(fsteps[k]) for k in range(min(LA, len(fsteps)))]
            for si, (b3, j) in enumerate(fsteps):
                pf, r_pf = pend.pop(0)
                if si + LA < len(fsteps):
                    pend.append(issue_f1(fsteps[si + LA]))
                sq_, r_sq = sqr.next()
                a_, r_a = aT.next()
                P.add("act", lambda e, sq_=sq_, pf=pf: e.activation(out=sq_[:], in_=pf[:, 0:256], func=AF.Square), reads=[r_pf], writes=[r_sq])
                P.add("dve", lambda e, sq_=sq_, pf=pf, a_=a_: e.scalar_tensor_tensor(out=a_[:], in0=pf[:, 0:256], scalar=0.0, in1=sq_[:],
                                                                                 op0=ALU.is_gt, op1=ALU.mult), reads=[r_pf, r_sq], writes=[r_a])
                for tl in range(2):
                    for cg in range(2):
                        fa, r_fa = acc[tl][cg]
                        P.add("pe", lambda e, fa=fa, a_=a_, tl=tl, cg=cg, j=j: e.matmul(fa[:, :], lhsT=a_[:, tl * 128:(tl + 1) * 128],
                                                                                      rhs=wf2[:, j, cg * 512:(cg + 1) * 512], start=(j == 0), stop=(j == 31)),
                              reads=[r_a] + wp_wf2.parts, writes=[r_fa])
                if j != 31:
                    continue
                if b3 + 2 < NB3:
                    load_h2(b3 + 2)
                for tl in range(2):
                    t = b3 * 2 + tl
                    rows = slice(t * 128, (t + 1) * 128)
                    x1t, r_x1 = x1.next()
                    x2t, r_x2 = x2.next()
                    s4t, r_s4 = s4.next()
                    dma("sp", x1t[:], X1_d[rows, :], writes=[r_x1], key=f"x1b{x1.i}")
                    for cg in range(2):
                        fa, r_fa = acc[tl][cg]
                        P.add("dve", lambda e, fa=fa, cg=cg, x2t=x2t: e.tensor_tensor(out=x2t[:, cg * 512:(cg + 1) * 512], in0=fa[:, :],
                                                                                   in1=gt2row[:, cg * 512:(cg + 1) * 512], op=ALU.mult),
                              reads=[r_fa, r_gt2], writes=[r_x2])
                    P.add("pool", lambda e, x2t=x2t, x1t=x1t: e.tensor_tensor(out=x2t[:], in0=x2t[:], in1=x1t[:], op=ALU.add), reads=[r_x2, r_x1], writes=[r_x2])
                    P.add("act", lambda e, x2t=x2t, s4t=s4t: e.activation(out=junk[:], in_=x2t[:], func=AF.Square, accum_out=s4t[:, 0:1]),
                          reads=[r_x2], writes=[r_junk, r_s4])
                    P.add("dve", lambda e, s4t=s4t: e.tensor_scalar(out=s4t[:, 1:2], in0=s4t[:, 0:1], scalar1=1.0 / D, scalar2=EPS, op0=ALU.mult, op1=ALU.add),
                          reads=[r_s4], writes=[r_s4])
                    P.add("act", lambda e, s4t=s4t: e.activation(out=s4t[:, 2:3], in_=s4t[:, 1:2], func=AF.Sqrt), reads=[r_s4], writes=[r_s4])
                    P.add("dve", lambda e, s4t=s4t: e.reciprocal(out=s4t[:, 3:4], in_=s4t[:, 2:3]), reads=[r_s4], writes=[r_s4])
                    P.add("dve", lambda e, x2t=x2t, s4t=s4t: e.scalar_tensor_tensor(out=x2t[:], in0=x2t[:], scalar=s4t[:, 3:4], in1=gfr[:], op0=ALU.mult, op1=ALU.mult),
                          reads=[r_x2, r_s4, r_gfr], writes=[r_x2])
                    dma("sp", out_d[rows, :], x2t[:], reads=[r_x2], key=f"x2{x2.i}")
            P.emit("phase3b")
    return nc


_NC_CACHE = {}


def _consts():
    ident = np.eye(128, dtype=np.float32)
    rmat = np.zeros((128, 128), dtype=np.float32)
    for p in range(128):
        partner = p + 32 if (p % 64) < 32 else p - 32
        rmat[partner, p] = 1.0
    inv_freq = (np.float32(10000.0) ** (-np.arange(0, 64, 2, dtype=np.float32) / np.float32(64))).astype(np.float32)
    cst = np.zeros((128, 4), dtype=np.float32)
    for p in range(128):
        cst[p, 0] = inv_freq[p % 32]
        cst[p, 1] = -1.0 if (p % 64) < 32 else 1.0
        cst[p, 2] = np.float32(np.pi / 2)
    return ident, rmat, cst


def _run(inputs, n_cores_per_seq=2):
    x = np.asarray(inputs["x"], dtype=np.float32)
    B, S, _ = x.shape
    S_OWN = S // n_cores_per_seq
    n_cores = B * n_cores_per_seq
    key = (S_OWN, S)
    if key not in _NC_CACHE:
        _NC_CACHE[key] = build(S_OWN, S)
    nc = _NC_CACHE[key]
    ident, rmat, cst = _consts()
    f = lambda k: np.ascontiguousarray(np.asarray(inputs[k], dtype=np.float32))
    pos = np.asarray(inputs["positions"]).astype(np.int32, copy=False)
    c = f("c")
    col = lambda v: np.ascontiguousarray(v.reshape(-1, 128).T)
    shared = {
        "w_ada": f("w_ada")[0], "b_ada": f("b_ada")[0][None, :], "b_ada_col": col(f("b_ada")[0]),
        "g1_col": col(f("g_norm1")[0]), "g2_col": col(f("g_norm2")[0]),
        "w_in": f("w_in")[0], "ln_g": f("gmlp_ln_g")[0][None, :], "ln_b": f("gmlp_ln_b")[0][None, :],
        "wsT": np.ascontiguousarray(f("w_spatial")[0].transpose(2, 0, 1)),
        "bs_col": np.ascontiguousarray(f("b_spatial")[0].T),
        "lamv": np.concatenate([f("lambda_q1")[0], f("lambda_q2")[0], f("lambda_k1")[0], f("lambda_k2")[0]])[None, :],
        "subln_g": f("subln_g")[0][None, :], "w_out": f("w_out")[0], "w_ff1": f("w_ff1")[0], "w_ff2": f("w_ff2")[0],
        "g_final": f("g_final")[None, :], "ident": ident, "rmat": rmat, "cst": cst,
    }
    in_maps = []
    for core in range(n_cores):
        b, half = divmod(core, n_cores_per_seq)
        own = slice(half * S_OWN, (half + 1) * S_OWN)
        order = np.concatenate([np.arange(own.start, own.stop),
                                np.arange(0, own.start), np.arange(own.stop, S)])
        m = dict(shared)
        m["x"] = np.ascontiguousarray(x[b][order])
        m["pos"] = np.ascontiguousarray(pos[b][order][None, :])
        m["cT"] = col(c[b])
        in_maps.append(m)
    res = run_bass_kernel_spmd(nc, in_maps, core_ids=list(range(n_cores)))
    out = np.empty((B, S, D), dtype=np.float32)
    for core in range(n_cores):
        b, half = divmod(core, n_cores_per_seq)
        out[b, half * S_OWN:(half + 1) * S_OWN] = res.results[core]["out"]
    return out


def kernel(**inputs):
    return _run(inputs, n_cores_per_seq=2)
```

```python
import numpy as np
from contextlib import ExitStack

import concourse.bass as bass
import concourse.mybir as mybir
from concourse.bass_utils import run_bass_kernel_spmd

F32 = mybir.dt.float32
BF16 = mybir.dt.bfloat16
I32 = mybir.dt.int32
AF = mybir.ActivationFunctionType
ALU = mybir.AluOpType
AX = mybir.AxisListType
PI = float(np.pi)

D = 1024
KC = 8
H = 8
DFF = 4096
NCOL = 7168
EPS = 1e-6
VW = 130
LAMBDA_INIT = 0.2


class Res:
    __slots__ = ("name", "last_write", "reads", "excl")

    def __init__(self, name, excl=False):
        self.name = name
        self.last_write = None
        self.reads = []
        self.excl = excl


class Op:
    __slots__ = ("eng", "fn", "wdeps", "odeps", "deps", "dma", "semkey", "marked", "ev", "idx", "cost", "xfer",
                 "pos", "end", "ready", "nin", "kids")

    def __init__(self, eng, fn, dma, semkey):
        self.eng = eng
        self.fn = fn
        self.wdeps = []
        self.odeps = []
        self.deps = []
        self.dma = dma
        self.semkey = semkey
        self.marked = False
        self.ev = None


class _Fake:
    def __init__(self):
        self.rec = None

    def __getattr__(self, name):
        def f(*a, **k):
            self.rec = (name, a, k)
            return self
        return f


_DTSZ = {}


def _fsz(ap):
    n = 1
    for v in ap.shape[1:]:
        n *= v
    return n


def _estimate(eng, fn, dma):
    fk = _Fake()
    fn(fk)
    name, a, k = fk.rec
    if dma:
        out = k.get("out")
        esz = 2 if out.dtype == BF16 else 4
        nbytes = _fsz(out) * out.shape[0] * esz
        issue = 1500.0 if eng == "pool" else 80.0
        return issue, nbytes
    if name == "matmul":
        n = _fsz(k["rhs"])
        return max(n, 64) / 2.0 + 10.0, 0
    if name == "transpose":
        return 75.0, 0
    ap = k.get("in_", None)
    if ap is None:
        ap = k.get("in0", None)
    if ap is None:
        ap = k.get("out", None)
    if ap is None:
        ap = a[0]
    n = _fsz(ap)
    if eng == "act":
        return 230.0 + 0.833 * n, 0
    if eng == "dve":
        return 65.0 + n / 0.96, 0
    return 130.0 + 2.0 * n, 0


class Prog:
    ENGS = ("pe", "act", "dve", "pool", "sp")
    HOP = 1200.0
    SAME = 150.0
    DMA_LAT = 2200.0
    DMA_BW = 0.16

    def __init__(self, nc, semstack):
        self.nc = nc
        self._semstack = semstack
        self.all_res = []
        self.sems = {}
        self.semcount = {}
        self.known = {e: {} for e in self.ENGS}
        self.ops = []
        self.nres = 0
        self.nops = 0
        self.disabled = False
        self.schedule = True

    def res(self, name=None, excl=False):
        self.nres += 1
        r = Res(name or f"r{self.nres}", excl)
        self.all_res.append(r)
        return r

    def _sem(self, key):
        if key not in self.sems:
            self.sems[key] = self._semstack.enter_context(self.nc.semaphore(f"s_{key}"))
            self.semcount[key] = 0
        return self.sems[key]

    def add(self, eng, fn, reads=(), writes=(), dma=False, semkey=None):
        if self.disabled:
            return None
        op = Op(eng, fn, dma, semkey)
        op.cost, op.xfer = _estimate(eng, fn, dma)
        deps = []
        for r in reads:
            if r.last_write is not None:
                deps.append(r.last_write)
            if r.excl:
                deps.extend(o for o in r.reads if o.eng != eng)
        for w in writes:
            if w.last_write is not None:
                deps.append(w.last_write)
            deps.extend(w.reads)
        for r in reads:
            r.reads.append(op)
        for w in writes:
            w.last_write = op
            w.reads = []
        self.nops += 1
        op.idx = self.nops
        seen = set()
        for d in deps:
            if id(d) in seen or d is op:
                continue
            seen.add(id(d))
            if d.eng == "pe" and eng == "pe" and not d.dma and not dma:
                op.odeps.append(d)
            else:
                op.wdeps.append(d)
        self.ops.append(op)
        return op

    def _schedule(self):
        import heapq
        ops = self.ops
        for op in ops:
            op.kids = []
            op.nin = 0
            op.ready = 0.0
        for op in ops:
            for d in op.wdeps + op.odeps:
                d.kids.append(op)
                op.nin += 1
        heaps = {e: [] for e in self.ENGS}
        for op in ops:
            if op.nin == 0:
                heapq.heappush(heaps[op.eng], (0.0, op.idx, op))
        free = {e: 0.0 for e in self.ENGS}
        dma_free = 0.0
        queues = {e: [] for e in self.ENGS}
        left = len(ops)
        while left:
            best = None
            for e in self.ENGS:
                h = heaps[e]
                if not h:
                    continue
                rt, idx, op = h[0]
                stt = max(rt, free[e])
                if best is None or (stt, idx) < best[0]:
                    best = ((stt, idx), e)
            (stt, _), e = best
            _, _, op = heapq.heappop(heaps[e])
            left -= 1
            op.pos = len(queues[e])
            queues[e].append(op)
            free[e] = stt + op.cost
            if op.dma:
                dma_free = max(dma_free, stt) + op.xfer / self.DMA_BW
                op.end = max(dma_free, stt + op.cost) + self.DMA_LAT
            else:
                op.end = stt + op.cost
            for k in op.kids:
                if op in k.odeps and not (op in k.wdeps):
                    lat = 0.0
                elif op.dma:
                    lat = 300.0
                elif k.eng == op.eng:
                    lat = self.SAME
                else:
                    lat = self.HOP
                k.ready = max(k.ready, op.end + lat)
                k.nin -= 1
                if k.nin == 0:
                    heapq.heappush(heaps[k.eng], (k.ready, k.idx, k))
        return queues

    def emit(self, block_name):
        nc = self.nc
        if self.schedule:
            per = self._schedule()
        else:
            per = {e: [] for e in self.ENGS}
            for op in self.ops:
                op.pos = len(per[op.eng])
                per[op.eng].append(op)
        for op in self.ops:
            latest = {}
            op.deps = []
            for d in op.wdeps:
                if d.dma:
                    op.deps.append(d)
                    d.marked = True
                elif d.eng not in latest or latest[d.eng].pos < d.pos:
                    latest[d.eng] = d
            for d in latest.values():
                op.deps.append(d)
                d.marked = True
            if op.dma:
                op.marked = True
        for e in self.ENGS:
            lastc = None
            for op in per[e]:
                if not op.dma:
                    lastc = op
            if lastc is not None:
                lastc.marked = True
        for e in self.ENGS:
            for op in per[e]:
                if op.marked and not op.dma:
                    key = "e_" + op.eng
                    self._sem(key)
                    self.semcount[key] += 1
                    op.ev = (key, self.semcount[key])
        for op in self.ops:
            if op.dma:
                key = "d_" + op.semkey
                self._sem(key)
                self.semcount[key] += 16
                op.ev = (key, self.semcount[key])
        prog = self

        def run(engname, e):
            known = prog.known[engname]
            for op in per[engname]:
                for d in op.deps:
                    key, val = d.ev
                    if known.get(key, 0) >= val:
                        continue
                    e.wait_ge(prog.sems[key], val)
                    known[key] = val
                ins = op.fn(e)
                if op.marked:
                    key, val = op.ev
                    ins.then_inc(prog.sems[key], 16 if op.dma else 1)
            for key, val in prog.semcount.items():
                if val > 0 and known.get(key, 0) < val:
                    e.wait_ge(prog.sems[key], val)
                    known[key] = val

        with nc.Block(block_name) as block:
            @block.tensor
            def _(e):
                run("pe", e)

            @block.scalar
            def _(e):
                run("act", e)

            @block.vector
            def _(e):
                run("dve", e)

            @block.gpsimd
            def _(e):
                run("pool", e)

            @block.sync
            def _(e):
                run("sp", e)
        self.ops = []
        for r in self.all_res:
            r.last_write = None
            r.reads = []


class Ring:
    def __init__(self, items):
        self.items = items
        self.i = -1

    def next(self):
        self.i = (self.i + 1) % len(self.items)
        return self.items[self.i]

    def cur(self):
        return self.items[self.i]


import os
_MAXPH = int(os.environ.get("KPH", "9"))
_KSUB = int(os.environ.get("KSUB", "0"))


def build(S_OWN, S_SEQ):
    NT_OWN = S_OWN // 128
    NT_SEQ = S_SEQ // 128
    NB_OWN = S_OWN // 512
    NB_SEQ = S_SEQ // 512
    NCH = NT_SEQ
    NPAIR = NCH // 2
    NQB = S_OWN // 256

    nc = bass.Bass("TRN2", target_bir_lowering=False)

    def din(name, shape, dt=F32):
        return nc.dram_tensor(name, list(shape), dt, kind="ExternalInput").ap()

    def dscr(name, shape, dt):
        return nc.dram_tensor(name, list(shape), dt, kind="Internal").ap()

    x_d = din("x", [S_SEQ, D])
    pos_d = din("pos", [1, S_SEQ], I32)
    cT_d = din("cT", [128, KC])
    wada_d = din("w_ada", [D, 6 * D])
    bada_d = din("b_ada", [1, 6 * D])
    badac_d = din("b_ada_col", [128, 48])
    g1c_d = din("g1_col", [128, KC])
    g2c_d = din("g2_col", [128, KC])
    win_d = din("w_in", [D, NCOL])
    lng_d = din("ln_g", [1, D])
    lnb_d = din("ln_b", [1, D])
    wsT_d = din("wsT", [128, 8, 128])
    bsc_d = din("bs_col", [128, 8])
    lamv_d = din("lamv", [1, 256])
    subg_d = din("subln_g", [1, 128])
    wout_d = din("w_out", [D, D])
    wff1_d = din("w_ff1", [D, DFF])
    wff2_d = din("w_ff2", [DFF, D])
    gfin_d = din("g_final", [1, D])
    ident_d = din("ident", [128, 128])
    rmat_d = din("rmat", [128, 128])
    cst_d = din("cst", [128, 4])
    out_d = nc.dram_tensor("out", [S_OWN, D], F32, kind="ExternalOutput").ap()

    TAB_d = dscr("tab", [2, 128, S_SEQ], F32)
    KT_d = dscr("ktd", [H, 128, S_SEQ], BF16)
    QT_d = dscr("qtd", [H, 128, S_OWN], BF16)
    V_d = dscr("vd", [H, 128, NCH, VW], BF16)
    GA_d = dscr("gad", [S_OWN, D], BF16)
    GB_d = dscr("gbd", [S_OWN, D], BF16)
    BB_d = dscr("bbd", [S_OWN, D], BF16)
    X1_d = dscr("x1d", [S_OWN, D], F32)
    H2T_d = dscr("h2td", [128, KC, S_OWN], BF16)
    GT_d = dscr("gtd", [2, D], F32)

    with ExitStack() as semstack, ExitStack() as glob:
        P = Prog(nc, semstack)

        def T(es, name, shape, dt):
            return es.enter_context(nc.sbuf_tensor("sb_" + name, list(shape), dt)), P.res(name)

        def PS(es, name, shape, dt=F32):
            return es.enter_context(nc.psum_tensor("pp_" + name, list(shape), dt)), P.res(name, excl=True)

        def dma(eng, out, in_, reads=(), writes=(), key=None):
            P.add(eng, lambda e: e.dma_start(out=out, in_=in_), reads=reads, writes=writes,
                  dma=True, semkey=key)

        class WParts:
            NP = 3

            def __init__(self, name):
                self.name = name
                self.parts = [P.res(f"{name}_p{i}") for i in range(self.NP)]
                self.n = 0

            def load(self, out, in_):
                i = self.n % self.NP
                self.n += 1
                dma("pool", out, in_, writes=[self.parts[i]], key=f"{self.name}{i}")

        ident, r_ident = T(glob, "ident", [128, 128], BF16)
        modcol, r_modcol = T(glob, "modcol", [128, 48], F32)
        A1, r_A1 = T(glob, "A1", [128, KC], F32)
        A2, r_A2 = T(glob, "A2", [128, KC], F32)
        lamc, r_lamc = T(glob, "lamc", [128, 1], F32)
        cst, r_cst = T(glob, "cst", [128, 4], F32)

        with ExitStack() as es:
            cTt, r_cTt = T(es, "cTt", [128, KC], F32)
            gt1row, r_gt1 = T(es, "gt1row", [128, D], F32)
            gt2row, r_gt2 = T(es, "gt2row", [128, D], F32)
            cact2, r_cact2 = T(es, "cact2", [128, KC, 2], F32)
            CB, r_CB = T(es, "CB", [128, KC, 128], F32)
            wad = [T(es, f"wad{i}", [128, KC, 1024], F32) for i in range(2)]
            badac, r_badac = T(es, "badac", [128, 48], F32)
            g1c, r_g1c = T(es, "g1c", [128, KC], F32)
            g2c, r_g2c = T(es, "g2c", [128, KC], F32)
            lamv, r_lamv = T(es, "lamv", [128, 256], F32)
            lprod, r_lprod = T(es, "lprod", [128, 128], F32)
            ls12, r_ls12 = T(es, "ls12", [128, 2], F32)
            posi, r_posi = T(es, "posi", [128, S_SEQ], I32)
            CW = min(2048, S_SEQ)
            posf, r_posf = T(es, "posf", [128, CW], F32)
            ang, r_ang = T(es, "ang", [128, CW], F32)
            kf, r_kf = T(es, "kf", [128, CW], F32)
            ki, r_ki = T(es, "ki", [128, CW], I32)
            tsin = [T(es, f"tsin{i}", [128, CW], F32) for i in range(2)]
            tcos = [T(es, f"tcos{i}", [128, CW], F32) for i in range(2)]
            sgnhp, r_sgnhp = T(es, "sgnhp", [128, 2], F32)
            ps_col, r_pscol = PS(es, "ps_col", [128, 512])
            ps_row = [PS(es, f"ps_row{i}", [128, 512]) for i in range(2)]

            dma("sp", cTt[:], cT_d, writes=[r_cTt], key="cTt")
            dma("sp", cst[:], cst_d, writes=[r_cst], key="cst")
            dma("sp", badac[:], badac_d, writes=[r_badac], key="badac")
            dma("sp", g1c[:], g1c_d, writes=[r_g1c], key="g1c")
            dma("sp", g2c[:], g2c_d, writes=[r_g2c], key="g2c")
            dma("sp", lamv[:], lamv_d.partition_broadcast(128), writes=[r_lamv], key="lamv")
            dma("sp", posi[:], pos_d.partition_broadcast(128), writes=[r_posi], key="posi")
            dma("pool", ident[:], ident_d, writes=[r_ident], key="ident")
            dma("sp", gt1row[:], bada_d[:, 2 * D:3 * D].partition_broadcast(128), writes=[r_gt1], key="gt1")
            dma("sp", gt2row[:], bada_d[:, 5 * D:6 * D].partition_broadcast(128), writes=[r_gt2], key="gt2")

            for c2 in range(2):
                P.add("act", lambda e, c2=c2: e.activation(out=cact2[:, :, c2], in_=cTt[:], func=AF.Silu),
                      reads=[r_cTt], writes=[r_cact2])
            P.add("dve", lambda e: e.tensor_copy(out=CB[:], in_=cact2[:, :, 0:1].to_broadcast([128, KC, 128])),
                  reads=[r_cact2], writes=[r_CB])

            wada_v = wada_d.rearrange("(kc p) n -> p kc n", p=128)
            for g in range(6):
                wt, r_wt = wad[g % 2]
                dma("sp", wt[:], wada_v[:, :, g * D:(g + 1) * D], writes=[r_wt], key=f"wad{g % 2}")
                if g in (0, 1, 3, 4):
                    for jj in range(8):
                        j = g * 8 + jj
                        for kc in range(KC):
                            P.add("pe", lambda e, wt=wt, jj=jj, j=j, kc=kc: e.matmul(
                                ps_col[:, 2 * j:2 * j + 2], lhsT=wt[:, kc, jj * 128:(jj + 1) * 128],
                                rhs=cact2[:, kc, :], start=(kc == 0), stop=(kc == KC - 1)),
                                reads=[r_wt, r_cact2], writes=[r_pscol])
                else:
                    grow, r_grow = (gt1row, r_gt1) if g == 2 else (gt2row, r_gt2)
                    for half in range(2):
                        pr, r_pr = ps_row[half]
                        for kc in range(KC):
                            P.add("pe", lambda e, wt=wt, pr=pr, half=half, kc=kc: e.matmul(
                                pr[:, :], lhsT=CB[:, kc, :], rhs=wt[:, kc, half * 512:(half + 1) * 512],
                                start=(kc == 0), stop=(kc == KC - 1)),
                                reads=[r_wt, r_CB], writes=[r_pr])
                        P.add("dve", lambda e, grow=grow, pr=pr, half=half: e.tensor_tensor(
                            out=grow[:, half * 512:(half + 1) * 512], in0=pr[:, :],
                            in1=grow[:, half * 512:(half + 1) * 512], op=ALU.add),
                            reads=[r_pr, r_grow], writes=[r_grow])
            dma("sp", GT_d[0:1, :], gt1row[0:1, :], reads=[r_gt1], key="gt1")
            dma("sp", GT_d[1:2, :], gt2row[0:1, :], reads=[r_gt2], key="gt2")
            for (j0, j1) in ((0, 16), (24, 40)):
                P.add("dve", lambda e, j0=j0, j1=j1: e.tensor_tensor(
                    out=modcol[:, j0:j1], in0=ps_col[:, 2 * j0:2 * j1].rearrange("p (j t) -> p j t", t=2)[:, :, 0],
                    in1=badac[:, j0:j1], op=ALU.add), reads=[r_pscol, r_badac], writes=[r_modcol])
            P.add("dve", lambda e: e.scalar_tensor_tensor(out=A1[:], in0=modcol[:, 8:16], scalar=1.0, in1=g1c[:],
                                                           op0=ALU.add, op1=ALU.mult),
                  reads=[r_modcol, r_g1c], writes=[r_A1])
            P.add("dve", lambda e: e.scalar_tensor_tensor(out=A2[:], in0=modcol[:, 32:40], scalar=1.0, in1=g2c[:],
                                                           op0=ALU.add, op1=ALU.mult),
                  reads=[r_modcol, r_g2c], writes=[r_A2])
            P.add("dve", lambda e: e.tensor_tensor(out=lprod[:], in0=lamv[:, 0:128], in1=lamv[:, 128:256], op=ALU.mult),
                  reads=[r_lamv], writes=[r_lprod])
            P.add("dve", lambda e: e.tensor_reduce(out=ls12[:], in_=lprod[:].rearrange("p (a b) -> p a b", a=2),
                                                   axis=AX.X, op=ALU.add), reads=[r_lprod], writes=[r_ls12])
            P.add("act", lambda e: e.activation(out=ls12[:], in_=ls12[:], func=AF.Exp), reads=[r_ls12], writes=[r_ls12])
            P.add("dve", lambda e: e.tensor_tensor(out=lamc[:], in0=ls12[:, 0:1], in1=ls12[:, 1:2], op=ALU.subtract),
                  reads=[r_ls12], writes=[r_lamc])
            P.add("dve", lambda e: e.tensor_scalar(out=lamc[:], in0=lamc[:], scalar1=LAMBDA_INIT, scalar2=None, op0=ALU.add),
                  reads=[r_lamc], writes=[r_lamc])
            for ci in range(S_SEQ // CW):
                cs = slice(ci * CW, (ci + 1) * CW)
                ts_, r_ts = tsin[ci % 2]
                tc_, r_tc = tcos[ci % 2]
                P.add("dve", lambda e, cs=cs: e.tensor_copy(out=posf[:], in_=posi[:, cs]), reads=[r_posi], writes=[r_posf])
                P.add("dve", lambda e: e.tensor_scalar(out=ang[:], in0=posf[:], scalar1=cst[:, 0:1], scalar2=None, op0=ALU.mult),
                      reads=[r_posf, r_cst], writes=[r_ang])
                P.add("dve", lambda e: e.tensor_scalar(out=kf[:], in0=ang[:], scalar1=1.0 / (2 * PI), scalar2=None, op0=ALU.mult),
                      reads=[r_ang], writes=[r_kf])
                P.add("dve", lambda e: e.tensor_copy(out=ki[:], in_=kf[:]), reads=[r_kf], writes=[r_ki])
                P.add("dve", lambda e: e.tensor_copy(out=kf[:], in_=ki[:]), reads=[r_ki], writes=[r_kf])
                P.add("dve", lambda e: e.scalar_tensor_tensor(out=kf[:], in0=kf[:], scalar=-2 * PI, in1=ang[:], op0=ALU.mult, op1=ALU.add),
                      reads=[r_kf, r_ang], writes=[r_kf])
                P.add("act", lambda e, ts_=ts_: e.activation(out=ts_[:], in_=kf[:], func=AF.Sin, scale=cst[:, 1:2]),
                      reads=[r_kf, r_cst], writes=[r_ts])
                P.add("dve", lambda e: e.tensor_scalar(out=kf[:], in0=ang[:], scalar1=1.0 / (2 * PI), scalar2=0.25, op0=ALU.mult, op1=ALU.add),
                      reads=[r_ang], writes=[r_kf])
                P.add("dve", lambda e: e.tensor_copy(out=ki[:], in_=kf[:]), reads=[r_kf], writes=[r_ki])
                P.add("dve", lambda e: e.tensor_copy(out=kf[:], in_=ki[:]), reads=[r_ki], writes=[r_kf])
                P.add("dve", lambda e: e.scalar_tensor_tensor(out=kf[:], in0=kf[:], scalar=-2 * PI, in1=ang[:], op0=ALU.mult, op1=ALU.add),
                      reads=[r_kf, r_ang], writes=[r_kf])
                P.add("act", lambda e, tc_=tc_: e.activation(out=tc_[:], in_=kf[:], func=AF.Sin, bias=cst[:, 2:3]),
                      reads=[r_kf, r_cst], writes=[r_tc])
                dma("sp", TAB_d[0, :, cs], tc_[:], reads=[r_tc], key=f"tcos{ci % 2}")
                dma("sp", TAB_d[1, :, cs], ts_[:], reads=[r_ts], key=f"tsin{ci % 2}")
            P.emit("phase0")

        with ExitStack() as es:
            P.disabled = (1 > _MAXPH)
            win, _ = T(es, "win", [128, KC, NCOL], BF16)
            wp_win = WParts("win")
            wsT, r_wsT = T(es, "wsT", [128, 8, 128], BF16)
            rmat, r_rmat = T(es, "rmat", [128, 128], BF16)
            lngr, r_lngr = T(es, "lngr", [128, D], F32)
            lnbr, r_lnbr = T(es, "lnbr", [128, D], F32)
            bsc, r_bsc = T(es, "bsc", [128, 8], F32)
            xt = [T(es, f"xt{i}", [128, D], F32) for i in range(2)]
            junk, r_junk = T(es, "junk", [128, D], BF16)
            st = [T(es, f"st{i}", [128, 8], F32) for i in range(2)]
            xn = [T(es, f"xn{i}", [128, D], BF16) for i in range(2)]
            hT = [T(es, f"hT{i}", [128, KC, 512], BF16) for i in range(2)]
            cosb = [T(es, f"cosb{i}", [128, 512], F32) for i in range(2)]
            sinb = [T(es, f"sinb{i}", [128, 512], F32) for i in range(2)]
            kraw = Ring([T(es, f"kraw{i}", [128, 512], BF16) for i in range(2)])
            t1r = Ring([T(es, f"t1r{i}", [128, 512], F32) for i in range(2)])
            t2r = Ring([T(es, f"t2r{i}", [128, 512], F32) for i in range(2)])
            kfin = Ring([T(es, f"kfin{i}", [128, 512], BF16) for i in range(2)])
            vblk = [T(es, f"vblk{i}", [128, H, VW], BF16) for i in range(2)]
            gu = [T(es, f"gu{i}", [128, D], BF16) for i in range(2)]
            gv = [T(es, f"gv{i}", [128, D], F32) for i in range(2)]
            sga = [T(es, f"sga{i}", [128, D], BF16) for i in range(2)]
            lst = [T(es, f"lst{i}", [128, 8], F32) for i in range(2)]
            vln, r_vln = T(es, "vln", [128, D], BF16)
            tmpf, r_tmpf = T(es, "tmpf", [128, D], F32)
            gbt = [T(es, f"gbt{i}", [128, D], BF16) for i in range(2)]
            gat = [T(es, f"gat{i}", [128, D], BF16) for i in range(2)]
            ps_tr, r_pstr = PS(es, "ps_tr", [128, KC, 128], BF16)
            ps_tm = Ring([PS(es, f"ps_tm{i}", [128, 512]) for i in range(2)])
            ps_sv, r_pssv = PS(es, "ps_sv", [128, 8, 128])
            ps_k = Ring([PS(es, f"ps_k{i}", [128, 512]) for i in range(2)])
            ps_rot, r_psrot = PS(es, "ps_rot", [128, 512])

            for (c0, c1) in ((4096, 5120), (3072, 4096), (2048, 3072), (0, 2048), (5120, 7168)):
                for kc in range(KC):
                    wp_win.load(win[:, kc, c0:c1], win_d[kc * 128:(kc + 1) * 128, c0:c1])
            dma("pool", wsT[:], wsT_d, writes=[r_wsT], key="wsT")
            dma("pool", rmat[:], rmat_d, writes=[r_rmat], key="rmat")
            dma("sp", lngr[:], lng_d.partition_broadcast(128), writes=[r_lngr], key="lngr")
            dma("sp", lnbr[:], lnb_d.partition_broadcast(128), writes=[r_lnbr], key="lnbr")
            dma("sp", bsc[:], bsc_d, writes=[r_bsc], key="bsc")
            for (vb, r_vb) in vblk:
                P.add("pool", lambda e, vb=vb: e.memset(vb[:, :, 128:VW], 1.0), writes=[r_vb])

            def is_own(t):
                return (t // 4) < NB_OWN and not (_KSUB & 1)

            def stageA(t):
                blk, i = divmod(t, 4)
                hTt, r_hT = hT[blk % 2]
                if i == 0:
                    cb, r_cb = cosb[blk % 2]
                    sb_, r_sb = sinb[blk % 2]
                    bs = slice(blk * 512, (blk + 1) * 512)
                    dma("sp", cb[:], TAB_d[0, :, bs], writes=[r_cb], key=f"cosb{blk % 2}")
                    dma("sp", sb_[:], TAB_d[1, :, bs], writes=[r_sb], key=f"sinb{blk % 2}")
                xtt, r_xt = xt[t % 2]
                stt, r_st = st[t % 2]
                xnt, r_xn = xn[t % 2]
                dma("sp", xtt[:], x_d[t * 128:(t + 1) * 128, :], writes=[r_xt], key=f"xt{t % 2}")
                P.add("act", lambda e: e.activation(out=junk[:], in_=xtt[:], func=AF.Square, accum_out=stt[:, 0:1]),
                      reads=[r_xt], writes=[r_junk, r_st])
                P.add("dve", lambda e: e.tensor_scalar(out=stt[:, 1:2], in0=stt[:, 0:1], scalar1=1.0 / D, scalar2=EPS,
                                                       op0=ALU.mult, op1=ALU.add), reads=[r_st], writes=[r_st])
                P.add("act", lambda e: e.activation(out=stt[:, 2:3], in_=stt[:, 1:2], func=AF.Sqrt), reads=[r_st], writes=[r_st])
                P.add("dve", lambda e: e.reciprocal(out=stt[:, 3:4], in_=stt[:, 2:3]), reads=[r_st], writes=[r_st])
                P.add("pool", lambda e: e.tensor_scalar(out=xnt[:], in0=xtt[:], scalar1=stt[:, 3:4], scalar2=None, op0=ALU.mult),
                      reads=[r_xt, r_st], writes=[r_xn])
                for kc in range(KC):
                    P.add("pe", lambda e, kc=kc: e.transpose(out=ps_tr[:, kc, :], in_=xnt[:, kc * 128:(kc + 1) * 128], identity=ident[:]),
                          reads=[r_xn, r_ident], writes=[r_pstr])
                for kc in range(KC):
                    P.add("dve", lambda e, kc=kc: e.tensor_scalar(
                        out=hTt[:, kc, i * 128:(i + 1) * 128], in0=ps_tr[:, kc, :], scalar1=A1[:, kc:kc + 1], scalar2=modcol[:, kc:kc + 1],
                        op0=ALU.mult, op1=ALU.add), reads=[r_pstr, r_A1, r_modcol], writes=[r_hT])

            def tm_group(t, cg, consumer):
                blk, i = divmod(t, 4)
                hTt, r_hT = hT[blk % 2]
                pt, r_pt = ps_tm.next()
                for kc in range(KC):
                    P.add("pe", lambda e, kc=kc: e.matmul(
                        pt[:, :], lhsT=hTt[:, kc, i * 128:(i + 1) * 128], rhs=win[:, kc, cg * 512:(cg + 1) * 512],
                        start=(kc == 0), stop=(kc == KC - 1)), reads=[r_hT] + wp_win.parts, writes=[r_pt])
                consumer(pt, r_pt)

            def stageB(t):
                vb, r_vb = vblk[t % 2]
                if not (_KSUB & 4):
                    for hf in range(2):
                        def cons_v(pt, r_pt, hf=hf):
                            P.add("dve", lambda e: e.tensor_copy(out=vb[:, hf * 4:(hf + 1) * 4, 0:128],
                                                                 in_=pt[:, :].rearrange("p (h e) -> p h e", h=4)),
                                  reads=[r_pt], writes=[r_vb])
                        tm_group(t, 8 + hf, cons_v)
                    dma("sp", V_d[:, :, t, :].rearrange("h p e -> p h e"), vb[:], reads=[r_vb], key=f"vblk{t % 2}")
                if not is_own(t):
                    return
                gu_, r_gu = gu[t % 2]
                gv_, r_gv = gv[t % 2]
                sga_, r_sga = sga[t % 2]
                lst_, r_lst = lst[t % 2]
                gbt_t, r_gbt = gbt[t % 2]
                for hf in range(2):
                    def cons_u(pt, r_pt, hf=hf):
                        P.add("act", lambda e: e.activation(out=gu_[:, hf * 512:(hf + 1) * 512], in_=pt[:, :], func=AF.Gelu_apprx_tanh),
                              reads=[r_pt], writes=[r_gu])
                    tm_group(t, 0 + hf, cons_u)
                for hf in range(2):
                    def cons_va(pt, r_pt, hf=hf):
                        P.add("act", lambda e: e.activation(out=gv_[:, hf * 512:(hf + 1) * 512], in_=pt[:, :], func=AF.Gelu_apprx_tanh,
                                                            accum_out=lst_[:, hf:hf + 1]), reads=[r_pt], writes=[r_gv, r_lst])
                    tm_group(t, 2 + hf, cons_va)
                for hf in range(2):
                    def cons_ga(pt, r_pt, hf=hf):
                        P.add("act", lambda e: e.activation(out=sga_[:, hf * 512:(hf + 1) * 512], in_=pt[:, :], func=AF.Sigmoid),
                              reads=[r_pt], writes=[r_sga])
                    tm_group(t, 10 + hf, cons_ga)
                for hf in range(2):
                    def cons_gb(pt, r_pt, hf=hf):
                        P.add("act", lambda e: e.activation(out=gbt_t[:, hf * 512:(hf + 1) * 512], in_=pt[:, :], func=AF.Sigmoid),
                              reads=[r_pt], writes=[r_gbt])
                    tm_group(t, 12 + hf, cons_gb)
                dma("sp", GB_d[t * 128:(t + 1) * 128, :], gbt_t[:], reads=[r_gbt], key=f"gbt{t % 2}")

            def stageC(t):
                if not is_own(t):
                    return
                gu_, r_gu = gu[t % 2]
                gv_, r_gv = gv[t % 2]
                sga_, r_sga = sga[t % 2]
                lst_, r_lst = lst[t % 2]
                gat_t, r_gat = gat[t % 2]
                P.add("act", lambda e: e.activation(out=junk[:], in_=gv_[:], func=AF.Square, accum_out=lst_[:, 2:3]),
                      reads=[r_gv], writes=[r_junk, r_lst])
                P.add("dve", lambda e: e.tensor_tensor(out=lst_[:, 3:4], in0=lst_[:, 0:1], in1=lst_[:, 1:2], op=ALU.add), reads=[r_lst], writes=[r_lst])
                P.add("dve", lambda e: e.tensor_scalar(out=lst_[:, 3:4], in0=lst_[:, 3:4], scalar1=-1.0 / D, scalar2=None, op0=ALU.mult),
                      reads=[r_lst], writes=[r_lst])
                P.add("dve", lambda e: e.tensor_tensor(out=lst_[:, 4:5], in0=lst_[:, 3:4], in1=lst_[:, 3:4], op=ALU.mult), reads=[r_lst], writes=[r_lst])
                P.add("dve", lambda e: e.scalar_tensor_tensor(out=lst_[:, 5:6], in0=lst_[:, 2:3], scalar=1.0 / D, in1=lst_[:, 4:5],
                                                               op0=ALU.mult, op1=ALU.subtract), reads=[r_lst], writes=[r_lst])
                P.add("dve", lambda e: e.tensor_scalar(out=lst_[:, 5:6], in0=lst_[:, 5:6], scalar1=EPS, scalar2=None, op0=ALU.add),
                      reads=[r_lst], writes=[r_lst])
                P.add("act", lambda e: e.activation(out=lst_[:, 6:7], in_=lst_[:, 5:6], func=AF.Sqrt), reads=[r_lst], writes=[r_lst])
                P.add("dve", lambda e: e.reciprocal(out=lst_[:, 7:8], in_=lst_[:, 6:7]), reads=[r_lst], writes=[r_lst])
                P.add("dve", lambda e: e.tensor_scalar(out=gv_[:], in0=gv_[:], scalar1=lst_[:, 3:4], scalar2=lst_[:, 7:8], op0=ALU.add, op1=ALU.mult),
                      reads=[r_gv, r_lst], writes=[r_gv])
                P.add("pool", lambda e: e.tensor_tensor(out=gv_[:], in0=gv_[:], in1=lngr[:], op=ALU.mult), reads=[r_gv, r_lngr], writes=[r_gv])
                P.add("pool", lambda e: e.tensor_tensor(out=vln[:], in0=gv_[:], in1=lnbr[:], op=ALU.add), reads=[r_gv, r_lnbr], writes=[r_vln])
                for g in range(8):
                    P.add("pe", lambda e, g=g: e.matmul(ps_sv[:, g, :], lhsT=wsT[:, g, :], rhs=vln[:, g * 128:(g + 1) * 128], start=True, stop=True),
                          reads=[r_wsT, r_vln], writes=[r_pssv])
                for g in range(8):
                    P.add("dve", lambda e, g=g: e.scalar_tensor_tensor(out=tmpf[:, g * 128:(g + 1) * 128], in0=ps_sv[:, g, :], scalar=bsc[:, g:g + 1],
                                                                        in1=gu_[:, g * 128:(g + 1) * 128], op0=ALU.add, op1=ALU.mult),
                          reads=[r_pssv, r_bsc, r_gu], writes=[r_tmpf])
                P.add("pool", lambda e: e.tensor_tensor(out=gat_t[:], in0=tmpf[:], in1=sga_[:], op=ALU.mult),
                      reads=[r_tmpf, r_sga], writes=[r_gat])
                dma("sp", GA_d[t * 128:(t + 1) * 128, :], gat_t[:], reads=[r_gat], key=f"gat{t % 2}")

            def stageR(blk):
                if _KSUB & 2:
                    return
                hTt, r_hT = hT[blk % 2]
                cb, r_cb = cosb[blk % 2]
                sb_, r_sb = sinb[blk % 2]
                bs = slice(blk * 512, (blk + 1) * 512)
                jobs = [("k", 3072, h) for h in range(H)]
                if blk < NB_OWN:
                    jobs += [("q", 2048, h) for h in range(H)]

                def mm(job):
                    nm, cbase, h = job
                    pk, r_pk = ps_k.next()
                    for kc in range(KC):
                        P.add("pe", lambda e, kc=kc: e.matmul(
                            pk[:, :], lhsT=win[:, kc, cbase + h * 128:cbase + (h + 1) * 128], rhs=hTt[:, kc, :],
                            start=(kc == 0), stop=(kc == KC - 1)), reads=[r_hT] + wp_win.parts, writes=[r_pk])
                    return pk, r_pk

                pend = mm(jobs[0])
                for ji, job in enumerate(jobs):
                    nm, cbase, h = job
                    pk, r_pk = pend
                    if ji + 1 < len(jobs):
                        pend = mm(jobs[ji + 1])
                    kr, r_kr = kraw.next()
                    t1, r_t1 = t1r.next()
                    t2, r_t2 = t2r.next()
                    kf_t, r_kf_t = kfin.next()
                    kfi = kfin.i
                    P.add("act", lambda e, kr=kr, pk=pk: e.activation(out=kr[:], in_=pk[:, :], func=AF.Copy), reads=[r_pk], writes=[r_kr])
                    P.add("pe", lambda e, kr=kr: e.matmul(ps_rot[:, :], lhsT=rmat[:], rhs=kr[:], start=True, stop=True),
                          reads=[r_rmat, r_kr], writes=[r_psrot])
                    P.add("dve", lambda e, t1=t1, pk=pk: e.tensor_tensor(out=t1[:], in0=pk[:, :], in1=cb[:], op=ALU.mult),
                          reads=[r_pk, r_cb], writes=[r_t1])
                    P.add("dve", lambda e, t2=t2: e.tensor_tensor(out=t2[:], in0=ps_rot[:, :], in1=sb_[:], op=ALU.mult),
                          reads=[r_psrot, r_sb], writes=[r_t2])
                    P.add("pool", lambda e, kf_t=kf_t, t1=t1, t2=t2: e.tensor_tensor(out=kf_t[:], in0=t1[:], in1=t2[:], op=ALU.add),
                          reads=[r_t1, r_t2], writes=[r_kf_t])
                    dst = KT_d if nm == "k" else QT_d
                    dma("sp", dst[h, :, bs], kf_t[:], reads=[r_kf_t], key=f"kfin{kfi}")

            for s in range(NT_SEQ + 2):
                if s < NT_SEQ:
                    stageA(s)
                if 0 <= s - 1 < NT_SEQ:
                    stageB(s - 1)
                    if (s - 1) % 4 == 3:
                        stageR((s - 1) // 4)
                if 0 <= s - 2 < NT_SEQ:
                    if not (_KSUB & 8):
                        stageC(s - 2)
            P.emit("phase1")

        late = glob.enter_context(ExitStack())
        wout, _ = T(late, "wout", [128, KC, D], BF16)
        wp_wout = WParts("wout")
        wf1, _ = T(late, "wf1", [128, KC, DFF], BF16)
        wp_wf1 = WParts("wf1")

        with ExitStack() as es:
            P.disabled = (2 > _MAXPH)
            KT = [T(es, f"KT{i}", [128, S_SEQ], BF16) for i in range(2)]
            VA = [T(es, f"VA{i}", [128, NCH, VW], BF16) for i in range(2)]
            Q1 = [T(es, f"Q1p{i}", [128, S_OWN], BF16) for i in range(2)]
            Q2 = [T(es, f"Q2p{i}", [128, S_OWN], BF16) for i in range(2)]
            ET = Ring([T(es, f"ET{i}", [128, 2, 2, 256], BF16) for i in range(3)])
            obuf = [T(es, f"obuf{i}", [128, NT_OWN, 128], BF16) for i in range(2)]
            nst = Ring([T(es, f"nst{i}", [128, 4], F32) for i in range(2)])
            ntmp = Ring([T(es, f"ntmp{i}", [128, 128], F32) for i in range(2)])
            SP_ = Ring([PS(es, f"S{i}", [128, 2, 2, 256]) for i in range(2)])
            acc = [[PS(es, f"acc{m}{qt}", [128, 512]) for qt in range(2)] for m in range(2)]

            for s in range(2):
                P.add("pool", lambda e, s=s: e.memset(Q1[s][0][64:128, :], 0.0), writes=[Q1[s][1]])
                P.add("pool", lambda e, s=s: e.memset(Q2[s][0][0:64, :], 0.0), writes=[Q2[s][1]])

            def load_head(h):
                s = h % 2
                dma("sp", KT[s][0][:], KT_d[h], writes=[KT[s][1]], key=f"KT{s}")
                dma("sp", VA[s][0][:], V_d[h], writes=[VA[s][1]], key=f"VA{s}")
                dma("sp", Q1[s][0][0:64, :], QT_d[h, 0:64, :], writes=[Q1[s][1]], key=f"Q1{s}")
                dma("sp", Q2[s][0][64:128, :], QT_d[h, 64:128, :], writes=[Q2[s][1]], key=f"Q2{s}")

            steps = [(h, qb, j) for h in range(H) for qb in range(NQB) for j in range(NPAIR)]

            def issue_qk(step):
                h, qb, j = step
                s = h % 2
                S_, r_S = SP_.next()
                qs = slice(qb * 256, (qb + 1) * 256)
                for kk in range(2):
                    c = 2 * j + kk
                    P.add("pe", lambda e, S_=S_, kk=kk, c=c, s=s, qs=qs: e.matmul(
                        S_[:, kk, 0, :], lhsT=KT[s][0][:, c * 128:(c + 1) * 128], rhs=Q1[s][0][:, qs], start=True, stop=True),
                        reads=[KT[s][1], Q1[s][1]], writes=[r_S])
                    P.add("pe", lambda e, S_=S_, kk=kk, c=c, s=s, qs=qs: e.matmul(
                        S_[:, kk, 1, :], lhsT=KT[s][0][:, c * 128:(c + 1) * 128], rhs=Q2[s][0][:, qs], start=True, stop=True),
                        reads=[KT[s][1], Q2[s][1]], writes=[r_S])
                return S_, r_S

            load_head(0)
            for kc in range(KC):
                wp_wout.load(wout[:, kc, :], wout_d[kc * 128:(kc + 1) * 128, :])
            for kc in range(KC):
                for hf in range(2):
                    wp_wf1.load(wf1[:, kc, hf * 2048:(hf + 1) * 2048], wff1_d[kc * 128:(kc + 1) * 128, hf * 2048:(hf + 1) * 2048])
            pending = issue_qk(steps[0])
            for si, (h, qb, j) in enumerate(steps):
                s = h % 2
                if qb == 0 and j == 0 and h + 1 < H:
                    load_head(h + 1)
                S_, r_S = pending
                if si + 1 < len(steps):
                    pending = issue_qk(steps[si + 1])
                E_, r_E = ET.next()
                P.add("act", lambda e, E_=E_, S_=S_: e.activation(out=E_[:].rearrange("p a b c -> p (a b c)"),
                                                                  in_=S_[:].rearrange("p a b c -> p (a b c)"), func=AF.Exp, scale=0.125),
                      reads=[r_S], writes=[r_E])
                for kk in range(2):
                    c = 2 * j + kk
                    for m in range(2):
                        for qt in range(2):
                            a_, r_a = acc[m][qt]
                            P.add("pe", lambda e, a_=a_, E_=E_, kk=kk, m=m, qt=qt, c=c, s=s: e.matmul(
                                a_[:, 0:129], lhsT=E_[:, kk, m, qt * 128:(qt + 1) * 128], rhs=VA[s][0][:, c, 0:129],
                                start=(c == 0), stop=(c == NCH - 1)), reads=[r_E, VA[s][1]], writes=[r_a])
                if j == NPAIR - 1:
                    ob, r_ob = obuf[h % 2]
                    for qt in range(2):
                        a0, r_a0 = acc[0][qt]
                        a1, r_a1 = acc[1][qt]
                        ns, r_ns = nst.next()
                        nt, r_nt = ntmp.next()
                        P.add("dve", lambda e, ns=ns, a0=a0: e.reciprocal(out=ns[:, 0:1], in_=a0[:, 128:129]), reads=[r_a0], writes=[r_ns])
                        P.add("dve", lambda e, ns=ns, a1=a1: e.reciprocal(out=ns[:, 1:2], in_=a1[:, 128:129]), reads=[r_a1], writes=[r_ns])
                        P.add("dve", lambda e, ns=ns: e.tensor_tensor(out=ns[:, 2:3], in0=ns[:, 1:2], in1=lamc[:], op=ALU.mult),
                              reads=[r_ns, r_lamc], writes=[r_ns])
                        P.add("dve", lambda e, ns=ns, nt=nt, a1=a1: e.tensor_scalar(out=nt[:], in0=a1[:, 0:128], scalar1=ns[:, 2:3], scalar2=None, op0=ALU.mult),
                              reads=[r_a1, r_ns], writes=[r_nt])
                        P.add("dve", lambda e, ns=ns, nt=nt, a0=a0, ob=ob, qb=qb, qt=qt: e.scalar_tensor_tensor(
                            out=ob[:, qb * 2 + qt, :], in0=a0[:, 0:128], scalar=ns[:, 0:1], in1=nt[:], op0=ALU.mult, op1=ALU.subtract),
                            reads=[r_a0, r_ns, r_nt], writes=[r_ob])
                    if qb == NQB - 1:
                        TG = min(8, NT_OWN)
                        for t0 in range(0, NT_OWN, TG):
                            dma("sp", BB_d[t0 * 128:(t0 + TG) * 128, h * 128:(h + 1) * 128].rearrange("(t p) e -> p t e", p=128),
                                ob[:, t0:t0 + TG, :], reads=[r_ob], key=f"obuf{h % 2}")
            P.emit("phase2")

        with ExitStack() as es:
            P.disabled = (3 > _MAXPH)
            wf2, _ = T(late, "wf2", [128, 32, D], BF16)
            wp_wf2 = WParts("wf2")
            g08, r_g08 = T(es, "g08", [128, 128], F32)
            gt1row, r_gt1 = T(es, "gt1row3", [128, D], F32)
            dma("sp", gt1row[:], GT_d[0:1, :].partition_broadcast(128), writes=[r_gt1], key="gt1row3")
            gaT = [T(es, f"gaT{i}", [128, D], BF16) for i in range(2)]
            gbT = [T(es, f"gbT{i}", [128, D], BF16) for i in range(2)]
            bbT = [T(es, f"bbT{i}", [128, D], BF16) for i in range(2)]
            xa = [T(es, f"xa{i}", [128, D], F32) for i in range(2)]
            sq = [T(es, f"sq{i}", [128, D], F32) for i in range(2)]
            s8 = [T(es, f"s8{i}", [128, 32], F32) for i in range(3)]
            mg = [T(es, f"mg{i}", [128, D], BF16) for i in range(2)]
            mT = [T(es, f"mT{i}", [128, KC, 128], BF16) for i in range(2)]
            x1 = [T(es, f"x1{i}", [128, D], F32) for i in range(2)]
            junk, r_junk = T(es, "junk3", [128, D], BF16)
            xn2 = [T(es, f"xn2{i}", [128, D], BF16) for i in range(2)]
            h2 = [T(es, f"h2{i}", [128, KC, 128], BF16) for i in range(2)]
            ps_tr = [PS(es, f"ps_tr3{i}", [128, KC, 128], BF16) for i in range(2)]
            ps_tr2 = [PS(es, f"ps_tr3b{i}", [128, KC, 128], BF16) for i in range(2)]
            ps_o = [[PS(es, f"ps_o{i}{hf}", [128, 512]) for hf in range(2)] for i in range(2)]

            for j0 in range(0, 32, 4):
                wp_wf2.load(wf2[:, j0:j0 + 4, :], wff2_d[j0 * 128:(j0 + 4) * 128, :].rearrange("(j p) n -> p j n", p=128))
            dma("sp", g08[:], subg_d.partition_broadcast(128), writes=[r_g08], key="g08")
            P.add("dve", lambda e: e.tensor_scalar(out=g08[:], in0=g08[:], scalar1=1.0 - LAMBDA_INIT, scalar2=None, op0=ALU.mult),
                  reads=[r_g08], writes=[r_g08])

            def s3A(t):
                rows = slice(t * 128, (t + 1) * 128)
                ga_, r_ga = gaT[t % 2]
                gb_, r_gb = gbT[t % 2]
                bb_, r_bb = bbT[t % 2]
                xa_, r_xa = xa[t % 2]
                s8t, r_s8 = s8[t % 3]
                sq_, r_sq = sq[t % 2]
                mg_, r_mg = mg[t % 2]
                dma("sp", bb_[:], BB_d[rows, :], writes=[r_bb], key=f"bbT{t % 2}")
                dma("sp", gb_[:], GB_d[rows, :], writes=[r_gb], key=f"gbT{t % 2}")
                dma("sp", ga_[:], GA_d[rows, :], writes=[r_ga], key=f"gaT{t % 2}")
                dma("sp", xa_[:], x_d[rows, :], writes=[r_xa], key=f"xa{t % 2}")
                P.add("dve", lambda e: e.tensor_tensor(out=sq_[:], in0=bb_[:], in1=bb_[:], op=ALU.mult), reads=[r_bb], writes=[r_sq])
                P.add("dve", lambda e: e.tensor_reduce(out=s8t[:, 0:8], in_=sq_[:].rearrange("p (h e) -> p h e", h=8), axis=AX.X, op=ALU.add),
                      reads=[r_sq], writes=[r_s8])
                P.add("dve", lambda e: e.tensor_scalar(out=s8t[:, 8:16], in0=s8t[:, 0:8], scalar1=1.0 / 128, scalar2=EPS, op0=ALU.mult, op1=ALU.add),
                      reads=[r_s8], writes=[r_s8])
                P.add("act", lambda e: e.activation(out=s8t[:, 16:24], in_=s8t[:, 8:16], func=AF.Sqrt), reads=[r_s8], writes=[r_s8])
                P.add("dve", lambda e: e.reciprocal(out=s8t[:, 24:32], in_=s8t[:, 16:24]), reads=[r_s8], writes=[r_s8])
                for hh in range(8):
                    P.add("dve", lambda e, hh=hh: e.scalar_tensor_tensor(
                        out=sq_[:, hh * 128:(hh + 1) * 128], in0=bb_[:, hh * 128:(hh + 1) * 128], scalar=s8t[:, 24 + hh:25 + hh], in1=g08[:],
                        op0=ALU.mult, op1=ALU.mult), reads=[r_bb, r_s8, r_g08], writes=[r_sq])
                P.add("pool", lambda e: e.tensor_tensor(out=sq_[:], in0=sq_[:], in1=gb_[:], op=ALU.mult), reads=[r_sq, r_gb], writes=[r_sq])
                P.add("dve", lambda e: e.tensor_tensor(out=mg_[:], in0=sq_[:], in1=ga_[:], op=ALU.add), reads=[r_sq, r_ga], writes=[r_mg])

            def s3B(t):
                rows = slice(t * 128, (t + 1) * 128)
                mg_, r_mg = mg[t % 2]
                mT_, r_mT = mT[t % 2]
                ptr, r_ptr = ps_tr[t % 2]
                xa_, r_xa = xa[t % 2]
                s8t, r_s8 = s8[t % 3]
                x1t, r_x1 = x1[t % 2]
                xn2_, r_xn2 = xn2[t % 2]
                for kc in range(KC):
                    P.add("pe", lambda e, kc=kc: e.transpose(out=ptr[:, kc, :], in_=mg_[:, kc * 128:(kc + 1) * 128], identity=ident[:]),
                          reads=[r_mg, r_ident], writes=[r_ptr])
                P.add("act", lambda e: e.activation(out=mT_[:].rearrange("p a b -> p (a b)"), in_=ptr[:].rearrange("p a b -> p (a b)"), func=AF.Copy),
                      reads=[r_ptr], writes=[r_mT])
                for hf in range(2):
                    po, r_po = ps_o[t % 2][hf]
                    for kc in range(KC):
                        P.add("pe", lambda e, po=po, kc=kc, hf=hf: e.matmul(po[:, :], lhsT=mT_[:, kc, :], rhs=wout[:, kc, hf * 512:(hf + 1) * 512],
                                                                           start=(kc == 0), stop=(kc == KC - 1)), reads=[r_mT] + wp_wout.parts, writes=[r_po])
                    P.add("dve", lambda e, po=po, hf=hf: e.tensor_tensor(out=x1t[:, hf * 512:(hf + 1) * 512], in0=po[:, :],
                                                                      in1=gt1row[:, hf * 512:(hf + 1) * 512], op=ALU.mult),
                          reads=[r_po, r_gt1], writes=[r_x1])
                P.add("pool", lambda e: e.tensor_tensor(out=x1t[:], in0=x1t[:], in1=xa_[:], op=ALU.add), reads=[r_x1, r_xa], writes=[r_x1])
                dma("sp", X1_d[rows, :], x1t[:], reads=[r_x1], key=f"x1{t % 2}")
                P.add("act", lambda e: e.activation(out=junk[:], in_=x1t[:], func=AF.Square, accum_out=s8t[:, 0:1]),
                      reads=[r_x1, r_s8], writes=[r_junk, r_s8])
                P.add("dve", lambda e: e.tensor_scalar(out=s8t[:, 1:2], in0=s8t[:, 0:1], scalar1=1.0 / D, scalar2=EPS, op0=ALU.mult, op1=ALU.add),
                      reads=[r_s8], writes=[r_s8])
                P.add("act", lambda e: e.activation(out=s8t[:, 2:3], in_=s8t[:, 1:2], func=AF.Sqrt), reads=[r_s8], writes=[r_s8])
                P.add("dve", lambda e: e.reciprocal(out=s8t[:, 3:4], in_=s8t[:, 2:3]), reads=[r_s8], writes=[r_s8])
                P.add("act", lambda e: e.activation(out=xn2_[:], in_=x1t[:], func=AF.Copy, scale=s8t[:, 3:4]),
                      reads=[r_x1, r_s8], writes=[r_xn2])

            def s3C(t):
                rows = slice(t * 128, (t + 1) * 128)
                xn2_, r_xn2 = xn2[t % 2]
                ptr2, r_ptr2 = ps_tr2[t % 2]
                h2t, r_h2 = h2[t % 2]
                for kc in range(KC):
                    P.add("pe", lambda e, kc=kc: e.transpose(out=ptr2[:, kc, :], in_=xn2_[:, kc * 128:(kc + 1) * 128], identity=ident[:]),
                          reads=[r_xn2, r_ident], writes=[r_ptr2])
                for kc in range(KC):
                    P.add("dve", lambda e, kc=kc: e.tensor_scalar(out=h2t[:, kc, :], in0=ptr2[:, kc, :], scalar1=A2[:, kc:kc + 1],
                                                                 scalar2=modcol[:, 24 + kc:25 + kc], op0=ALU.mult, op1=ALU.add),
                          reads=[r_ptr2, r_A2, r_modcol], writes=[r_h2])
                dma("sp", H2T_d[:, :, rows], h2t[:], reads=[r_h2], key=f"h2{t % 2}")

            for s in range(NT_OWN + 2):
                if s < NT_OWN:
                    s3A(s)
                if 0 <= s - 1 < NT_OWN:
                    s3B(s - 1)
                if 0 <= s - 2 < NT_OWN:
                    s3C(s - 2)
            P.emit("phase3a")

        with ExitStack() as es:
            P.disabled = (4 > _MAXPH)
            gfr, r_gfr = T(es, "gfr", [128, D], F32)
            gt2row, r_gt2 = T(es, "gt2row3", [128, D], F32)
            dma("sp", gt2row[:], GT_d[1:2, :].partition_broadcast(128), writes=[r_gt2], key="gt2row3")
            LA = 3
            h2b = [T(es, f"h2b{i}", [128, KC, 256], BF16) for i in range(2)]
            sqr = Ring([T(es, f"sqr{i}", [128, 256], F32) for i in range(LA + 1)])
            aT = Ring([T(es, f"aT{i}", [128, 256], BF16) for i in range(LA + 2)])
            x1 = Ring([T(es, f"x1b{i}", [128, D], F32) for i in range(2)])
            x2 = Ring([T(es, f"x2{i}", [128, D], F32) for i in range(2)])
            junk, r_junk = T(es, "junk4", [128, D], BF16)
            s4 = Ring([T(es, f"s4{i}", [128, 4], F32) for i in range(2)])
            ps_f = Ring([PS(es, f"ps_f{i}", [128, 512]) for i in range(LA + 1)])
            acc = [[PS(es, f"fa{tl}{cg}", [128, 512]) for cg in range(2)] for tl in range(2)]

            dma("sp", gfr[:], gfin_d.partition_broadcast(128), writes=[r_gfr], key="gfr")

            NB3 = S_OWN // 256
            fsteps = [(b3, j) for b3 in range(NB3) for j in range(32)]

            def load_h2(b3):
                hb, r_hb = h2b[b3 % 2]
                dma("sp", hb[:], H2T_d[:, :, b3 * 256:(b3 + 1) * 256], writes=[r_hb], key=f"h2b{b3 % 2}")

            def issue_f1(step):
                b3, j = step
                hb, r_hb = h2b[b3 % 2]
                pf, r_pf = ps_f.next()
                for kc in range(KC):
                    P.add("pe", lambda e, kc=kc: e.matmul(pf[:, 0:256], lhsT=wf1[:, kc, j * 128:(j + 1) * 128], rhs=hb[:, kc, :],
                                                         start=(kc == 0), stop=(kc == KC - 1)), reads=[r_hb] + wp_wf1.parts, writes=[r_pf])
                return pf, r_pf

            load_h2(0)
            if NB3 > 1:
                load_h2(1)
            pend = [issue_f1(fsteps[k]) for k in range(min(LA, len(fsteps)))]
            for si, (b3, j) in enumerate(fsteps):
                pf, r_pf = pend.pop(0)
                if si + LA < len(fsteps):
                    pend.append(issue_f1(fsteps[si + LA]))
                sq_, r_sq = sqr.next()
                a_, r_a = aT.next()
                P.add("act", lambda e, sq_=sq_, pf=pf: e.activation(out=sq_[:], in_=pf[:, 0:256], func=AF.Square), reads=[r_pf], writes=[r_sq])
                P.add("dve", lambda e, sq_=sq_, pf=pf, a_=a_: e.scalar_tensor_tensor(out=a_[:], in0=pf[:, 0:256], scalar=0.0, in1=sq_[:],
                                                                                 op0=ALU.is_gt, op1=ALU.mult), reads=[r_pf, r_sq], writes=[r_a])
                for tl in range(2):
                    for cg in range(2):
                        fa, r_fa = acc[tl][cg]
                        P.add("pe", lambda e, fa=fa, a_=a_, tl=tl, cg=cg, j=j: e.matmul(fa[:, :], lhsT=a_[:, tl * 128:(tl + 1) * 128],
                                                                                      rhs=wf2[:, j, cg * 512:(cg + 1) * 512], start=(j == 0), stop=(j == 31)),
                              reads=[r_a] + wp_wf2.parts, writes=[r_fa])
                if j != 31:
                    continue
                if b3 + 2 < NB3:
                    load_h2(b3 + 2)
                for tl in range(2):
                    t = b3 * 2 + tl
                    rows = slice(t * 128, (t + 1) * 128)
                    x1t, r_x1 = x1.next()
                    x2t, r_x2 = x2.next()
                    s4t, r_s4 = s4.next()
                    dma("sp", x1t[:], X1_d[rows, :], writes=[r_x1], key=f"x1b{x1.i}")
                    for cg in range(2):
                        fa, r_fa = acc[tl][cg]
                        P.add("dve", lambda e, fa=fa, cg=cg, x2t=x2t: e.tensor_tensor(out=x2t[:, cg * 512:(cg + 1) * 512], in0=fa[:, :],
                                                                                   in1=gt2row[:, cg * 512:(cg + 1) * 512], op=ALU.mult),
                              reads=[r_fa, r_gt2], writes=[r_x2])
                    P.add("pool", lambda e, x2t=x2t, x1t=x1t: e.tensor_tensor(out=x2t[:], in0=x2t[:], in1=x1t[:], op=ALU.add), reads=[r_x2, r_x1], writes=[r_x2])
                    P.add("act", lambda e, x2t=x2t, s4t=s4t: e.activation(out=junk[:], in_=x2t[:], func=AF.Square, accum_out=s4t[:, 0:1]),
                          reads=[r_x2], writes=[r_junk, r_s4])
                    P.add("dve", lambda e, s4t=s4t: e.tensor_scalar(out=s4t[:, 1:2], in0=s4t[:, 0:1], scalar1=1.0 / D, scalar2=EPS, op0=ALU.mult, op1=ALU.add),
                          reads=[r_s4], writes=[r_s4])
                    P.add("act", lambda e, s4t=s4t: e.activation(out=s4t[:, 2:3], in_=s4t[:, 1:2], func=AF.Sqrt), reads=[r_s4], writes=[r_s4])
                    P.add("dve", lambda e, s4t=s4t: e.reciprocal(out=s4t[:, 3:4], in_=s4t[:, 2:3]), reads=[r_s4], writes=[r_s4])
                    P.add("dve", lambda e, x2t=x2t, s4t=s4t: e.scalar_tensor_tensor(out=x2t[:], in0=x2t[:], scalar=s4t[:, 3:4], in1=gfr[:], op0=ALU.mult, op1=ALU.mult),
                          reads=[r_x2, r_s4, r_gfr], writes=[r_x2])
                    dma("sp", out_d[rows, :], x2t[:], reads=[r_x2], key=f"x2{x2.i}")
            P.emit("phase3b")
    return nc


_NC_CACHE = {}


def _consts():
    ident = np.eye(128, dtype=np.float32)
    rmat = np.zeros((128, 128), dtype=np.float32)
    for p in range(128):
        partner = p + 32 if (p % 64) < 32 else p - 32
        rmat[partner, p] = 1.0
    inv_freq = (np.float32(10000.0) ** (-np.arange(0, 64, 2, dtype=np.float32) / np.float32(64))).astype(np.float32)
    cst = np.zeros((128, 4), dtype=np.float32)
    for p in range(128):
        cst[p, 0] = inv_freq[p % 32]
        cst[p, 1] = -1.0 if (p % 64) < 32 else 1.0
        cst[p, 2] = np.float32(np.pi / 2)
    return ident, rmat, cst


def _run(inputs, n_cores_per_seq=2):
    x = np.asarray(inputs["x"], dtype=np.float32)
    B, S, _ = x.shape
    S_OWN = S // n_cores_per_seq
    n_cores = B * n_cores_per_seq
    key = (S_OWN, S)
    if key not in _NC_CACHE:
        _NC_CACHE[key] = build(S_OWN, S)
    nc = _NC_CACHE[key]
    ident, rmat, cst = _consts()
    f = lambda k: np.ascontiguousarray(np.asarray(inputs[k], dtype=np.float32))
    pos = np.asarray(inputs["positions"]).astype(np.int32, copy=False)
    c = f("c")
    col = lambda v: np.ascontiguousarray(v.reshape(-1, 128).T)
    shared = {
        "w_ada": f("w_ada")[0], "b_ada": f("b_ada")[0][None, :], "b_ada_col": col(f("b_ada")[0]),
        "g1_col": col(f("g_norm1")[0]), "g2_col": col(f("g_norm2")[0]),
        "w_in": f("w_in")[0], "ln_g": f("gmlp_ln_g")[0][None, :], "ln_b": f("gmlp_ln_b")[0][None, :],
        "wsT": np.ascontiguousarray(f("w_spatial")[0].transpose(2, 0, 1)),
        "bs_col": np.ascontiguousarray(f("b_spatial")[0].T),
        "lamv": np.concatenate([f("lambda_q1")[0], f("lambda_q2")[0], f("lambda_k1")[0], f("lambda_k2")[0]])[None, :],
        "subln_g": f("subln_g")[0][None, :], "w_out": f("w_out")[0], "w_ff1": f("w_ff1")[0], "w_ff2": f("w_ff2")[0],
        "g_final": f("g_final")[None, :], "ident": ident, "rmat": rmat, "cst": cst,
    }
    in_maps = []
    for core in range(n_cores):
        b, half = divmod(core, n_cores_per_seq)
        own = slice(half * S_OWN, (half + 1) * S_OWN)
        order = np.concatenate([np.arange(own.start, own.stop),
                                np.arange(0, own.start), np.arange(own.stop, S)])
        m = dict(shared)
        m["x"] = np.ascontiguousarray(x[b][order])
        m["pos"] = np.ascontiguousarray(pos[b][order][None, :])
        m["cT"] = col(c[b])
        in_maps.append(m)
    res = run_bass_kernel_spmd(nc, in_maps, core_ids=list(range(n_cores)))
    out = np.empty((B, S, D), dtype=np.float32)
    for core in range(n_cores):
        b, half = divmod(core, n_cores_per_seq)
        out[b, half * S_OWN:(half + 1) * S_OWN] = res.results[core]["out"]
    return out


def kernel(**inputs):
    return _run(inputs, n_cores_per_seq=2)
```
